# Optimizing a Trainium2 kernel written in Bass

```python
import math
import jax, jax.numpy as jnp
from jax import lax
import numpy as np

D_MODEL = 1024
BATCH = 4
SEQ = 8192
DEPTH = 4

CTX_LEN = 256
GRID_W = 64
N_MIXERS = 3
MIXER_POOL = 0
MIXER_ATTN = 1
MIXER_RET = 2
POOL_WINDOWS = (2, 4, 8, 16)
POOL_GROUP = D_MODEL // 4
ATTN_HEADS = 8
ATTN_KV_HEADS = 2
ATTN_HEAD_DIM = D_MODEL // ATTN_HEADS
ATTN_GROUP = ATTN_HEADS // ATTN_KV_HEADS
Q_BLOCK = 128
ROPE_THETA = 10000.0
RET_HEADS = 4
RET_DK = D_MODEL // RET_HEADS
RET_DV = 2 * D_MODEL // RET_HEADS
RET_CHUNK = 128
D_FF = 2816
CONV_WIDTH = 3
EPS = 1e-6

kernel_name = 'hybrid_pool_gqa_retention_convffn_dit'


def _rmsnorm(x, gain):
    xf = x.astype(jnp.float32)
    y = xf * lax.rsqrt(jnp.mean(xf * xf, axis=-1, keepdims=True) + EPS)
    return (y * gain.astype(jnp.float32)).astype(x.dtype)


def _modulate(h, shift, scale):
    return h * (1.0 + scale) + shift


def _pool_mixer(h, w, b, scale):
    s = h.shape[1]
    hf = h.astype(jnp.float32)
    cs = jnp.concatenate([jnp.zeros_like(hf[:, :1]), jnp.cumsum(hf, axis=1)], axis=1)
    t = jnp.arange(s)
    outs = []
    for g, win in enumerate(POOL_WINDOWS):
        lo = jnp.clip(t - win // 2, 0, s)
        hi = jnp.clip(t + win // 2, 0, s)
        sl = slice(g * POOL_GROUP, (g + 1) * POOL_GROUP)
        csg = cs[..., sl]
        mean = (csg[:, hi] - csg[:, lo]) / (hi - lo).astype(jnp.float32)[None, :, None]
        outs.append((mean - hf[..., sl]).astype(h.dtype) @ w[g])
    return (jnp.concatenate(outs, axis=-1) + b) * scale


def _axial_angles(n_tokens):
    rows = n_tokens // GRID_W
    row = jnp.broadcast_to(jnp.arange(rows)[:, None], (rows, GRID_W)).reshape(-1).astype(jnp.float32)
    col = jnp.broadcast_to(jnp.arange(GRID_W)[None, :], (rows, GRID_W)).reshape(-1).astype(jnp.float32)
    axis_dim = ATTN_HEAD_DIM // 2
    inv = ROPE_THETA ** (-jnp.arange(0, axis_dim, 2, dtype=jnp.float32) / axis_dim)
    return row[:, None] * inv, col[:, None] * inv


def _rotate(xa, ang):
    x1, x2 = jnp.split(xa, 2, axis=-1)
    cos = jnp.cos(ang)[None, :, None, :]
    sin = jnp.sin(ang)[None, :, None, :]
    return jnp.concatenate([x1 * cos - x2 * sin, x1 * sin + x2 * cos], axis=-1)


def _apply_axial_rope(x, ang_row, ang_col):
    xr, xc = jnp.split(x.astype(jnp.float32), 2, axis=-1)
    return jnp.concatenate([_rotate(xr, ang_row), _rotate(xc, ang_col)], axis=-1).astype(x.dtype)


def _gqa(q, k, v):
    s = jnp.einsum('bqkgd,bskd->bkgqs', q, k).astype(jnp.float32) * (ATTN_HEAD_DIM ** -0.5)
    p = jax.nn.softmax(s, axis=-1).astype(v.dtype)
    return jnp.einsum('bkgqs,bskd->bqkgd', p, v)


def _attention_mixer(h_lat, h_ctx, w_qkv, q_gain, k_gain, w_o, need_ctx_out):
    def project(h):
        b, s, _ = h.shape
        nq = ATTN_HEADS * ATTN_HEAD_DIM
        nk = ATTN_KV_HEADS * ATTN_HEAD_DIM
        q, k, v = jnp.split(h @ w_qkv, [nq, nq + nk], axis=-1)
        q = _rmsnorm(q.reshape(b, s, ATTN_HEADS, ATTN_HEAD_DIM), q_gain)
        k = _rmsnorm(k.reshape(b, s, ATTN_KV_HEADS, ATTN_HEAD_DIM), k_gain)
        return q, k, v.reshape(b, s, ATTN_KV_HEADS, ATTN_HEAD_DIM)

    b, s = h_lat.shape[:2]
    qc, kc, vc = project(h_ctx)
    ql, kl, vl = project(h_lat)
    ang_r, ang_c = _axial_angles(s)
    ql = _apply_axial_rope(ql, ang_r, ang_c)
    kl = _apply_axial_rope(kl, ang_r, ang_c)
    keys = jnp.concatenate([kl, kc], axis=1)
    vals = jnp.concatenate([vl, vc], axis=1)
    q_blocks = ql.reshape(b, s // Q_BLOCK, Q_BLOCK, ATTN_KV_HEADS, ATTN_GROUP, ATTN_HEAD_DIM)
    q_blocks = q_blocks.transpose(1, 0, 2, 3, 4, 5)
    o = lax.map(lambda qb: _gqa(qb, keys, vals), q_blocks)
    o = o.transpose(1, 0, 2, 3, 4, 5).reshape(b, s, ATTN_HEADS * ATTN_HEAD_DIM)
    y_lat = o @ w_o
    y_ctx = None
    if need_ctx_out:
        l = h_ctx.shape[1]
        oc = _gqa(qc.reshape(b, l, ATTN_KV_HEADS, ATTN_GROUP, ATTN_HEAD_DIM), kc, vc)
        y_ctx = oc.reshape(b, l, ATTN_HEADS * ATTN_HEAD_DIM) @ w_o
    return y_lat, y_ctx


def _retention_chunks(q, k, v, log_gamma, state):
    b, h, s, _ = q.shape
    dv = v.shape[-1]
    n = s // RET_CHUNK

    def chunks(a):
        return jnp.moveaxis(a.reshape(b, h, n, RET_CHUNK, a.shape[-1]), 2, 0)

    idx = jnp.arange(RET_CHUNK, dtype=jnp.float32)
    diff = idx[:, None] - idx[None, :]
    intra = jnp.where(diff >= 0, jnp.exp(jnp.maximum(diff, 0.0) * log_gamma[:, None, None]), 0.0)
    q_dec = jnp.exp((idx + 1.0) * log_gamma[:, None])[None, :, :, None]
    k_dec = jnp.exp((RET_CHUNK - 1.0 - idx) * log_gamma[:, None])[None, :, :, None]
    chunk_dec = jnp.exp(RET_CHUNK * log_gamma)[None, :, None, None]

    def step(r, qkv):
        qc, kc, vc = qkv
        att = jnp.einsum('bhid,bhjd->bhij', qc, kc) * intra
        o = jnp.einsum('bhij,bhje->bhie', att, vc) + jnp.einsum('bhid,bhde->bhie', qc, r) * q_dec
        r = r * chunk_dec + jnp.einsum('bhjd,bhje->bhde', kc * k_dec, vc)
        return r, o

    r, o = lax.scan(step, state, (chunks(q), chunks(k), chunks(v)))
    return jnp.moveaxis(o, 0, 2).reshape(b, h, s, dv), r


def _final_state(k, v, log_gamma):
    l = k.shape[2]
    w = jnp.exp((l - 1.0 - jnp.arange(l, dtype=jnp.float32)) * log_gamma[:, None])
    return jnp.einsum('bhjd,bhje->bhde', k * w[None, :, :, None], v)


def _retention_mixer(h_lat, h_ctx, w_in, decay_logit, gn_w, w_out, need_ctx_out):
    log_gamma = jax.nn.log_sigmoid(decay_logit.astype(jnp.float32))
    nq = RET_HEADS * RET_DK
    nv = RET_HEADS * RET_DV

    def project(h):
        b, s, _ = h.shape
        q, k, v, g = jnp.split(h @ w_in, [nq, 2 * nq, 2 * nq + nv], axis=-1)

        def heads(a, d):
            return a.reshape(b, s, RET_HEADS, d).transpose(0, 2, 1, 3).astype(jnp.float32)
        return heads(q, RET_DK), heads(k, RET_DK) * (RET_DK ** -0.5), heads(v, RET_DV), g

    def flip(a):
        return jnp.flip(a, axis=2)

    def readout(y, g):
        b, h, s, dv = y.shape
        mu = jnp.mean(y, axis=-1, keepdims=True)
        var = jnp.mean(jnp.square(y - mu), axis=-1, keepdims=True)
        yn = (y - mu) * lax.rsqrt(var + EPS) * gn_w.astype(jnp.float32).reshape(1, h, 1, dv)
        yn = yn.transpose(0, 2, 1, 3).reshape(b, s, h * dv).astype(g.dtype)
        return (jax.nn.silu(g) * yn) @ w_out

    y_ctx = None
    if need_ctx_out:
        qc, kc, vc, gc = project(h_ctx)
        zero = jnp.zeros(qc.shape[:2] + (RET_DK, RET_DV), jnp.float32)
        oc_f, r_f = _retention_chunks(qc, kc, vc, log_gamma[0], zero)
        oc_b, r_b = _retention_chunks(flip(qc), flip(kc), flip(vc), log_gamma[1], zero)
        y_ctx = readout(oc_f + flip(oc_b), gc)
    else:
        b, l, _ = h_ctx.shape
        kv_c = (h_ctx @ w_in[:, nq:2 * nq + nv])
        kc = kv_c[..., :nq].reshape(b, l, RET_HEADS, RET_DK).transpose(0, 2, 1, 3).astype(jnp.float32) * (RET_DK ** -0.5)
        vc = kv_c[..., nq:].reshape(b, l, RET_HEADS, RET_DV).transpose(0, 2, 1, 3).astype(jnp.float32)
        r_f = _final_state(kc, vc, log_gamma[0])
        r_b = _final_state(flip(kc), flip(vc), log_gamma[1])
    ql, kl, vl, gl = project(h_lat)
    ol_f, _ = _retention_chunks(ql, kl, vl, log_gamma[0], r_f)
    ol_b, _ = _retention_chunks(flip(ql), flip(kl), flip(vl), log_gamma[1], r_b)
    y_lat = readout(ol_f + flip(ol_b), gl)
    return y_lat, y_ctx


def _conv_ffn(h, w_up, conv_w, conv_b, w_down):
    u = h @ w_up
    up = jnp.pad(u, ((0, 0), (1, 1), (0, 0)))
    u = up[:, :-2] * conv_w[0] + up[:, 1:-1] * conv_w[1] + up[:, 2:] * conv_w[2] + conv_b
    a, v = jnp.split(u, 2, axis=-1)
    return (jax.nn.silu(a) * v) @ w_down


def _layer_counts():
    kinds = [i % N_MIXERS for i in range(DEPTH)]
    return kinds.count(MIXER_POOL), kinds.count(MIXER_ATTN), kinds.count(MIXER_RET)


def setup_inputs(seed: int = 0) -> dict:
    key = jax.random.key(seed)
    ks = jax.random.split(key, 22)
    n_pool, n_attn, n_ret = _layer_counts()
    d = D_MODEL

    def nrm(k, shape):
        return jax.random.normal(k, shape, jnp.float32)

    def w(k, shape, fan_in, gain=1.0):
        return nrm(k, shape) * (gain * fan_in ** -0.5)

    def ones_noise(k, shape):
        return 1.0 + 0.05 * nrm(k, shape)

    decay_base = jnp.asarray(np.log(2.0 ** (5 + np.arange(RET_HEADS)) - 1.0).astype(np.float32))
    return {
        'x': nrm(ks[0], (BATCH, SEQ, d)),
        'c': nrm(ks[1], (BATCH, d)),
        'ctx': nrm(ks[2], (BATCH, CTX_LEN, d)),
        'c_ctx': nrm(ks[3], (d,)),
        'ada_w': w(ks[4], (DEPTH, d, 6 * d), d, 0.5),
        'ada_b': 0.01 * nrm(ks[5], (DEPTH, 6 * d)),
        'norm_w': ones_noise(ks[6], (DEPTH, 2, d)),
        'pool_w': w(ks[7], (n_pool, 4, POOL_GROUP, POOL_GROUP), POOL_GROUP),
        'pool_b': 0.01 * nrm(ks[8], (n_pool, d)),
        'pool_scale': ones_noise(ks[9], (n_pool, d)),
        'attn_w_qkv': w(ks[10], (n_attn, d, (ATTN_HEADS + 2 * ATTN_KV_HEADS) * ATTN_HEAD_DIM), d),
        'attn_q_gain': ones_noise(ks[11], (n_attn, ATTN_HEAD_DIM)),
        'attn_k_gain': ones_noise(ks[12], (n_attn, ATTN_HEAD_DIM)),
        'attn_w_o': w(ks[13], (n_attn, ATTN_HEADS * ATTN_HEAD_DIM, d), ATTN_HEADS * ATTN_HEAD_DIM),
        'ret_w_in': w(ks[14], (n_ret, d, 2 * RET_HEADS * RET_DK + 2 * RET_HEADS * RET_DV), d),
        'ret_decay_logit': decay_base[None, None, :] + 0.1 * nrm(ks[15], (n_ret, 2, RET_HEADS)),
        'ret_gn_w': ones_noise(ks[16], (n_ret, RET_HEADS * RET_DV)),
        'ret_w_out': w(ks[17], (n_ret, RET_HEADS * RET_DV, d), RET_HEADS * RET_DV),
        'ffn_w_up': w(ks[18], (DEPTH, d, 2 * D_FF), d),
        'ffn_conv_w': w(ks[19], (DEPTH, CONV_WIDTH, 2 * D_FF), CONV_WIDTH),
        'ffn_conv_b': 0.01 * nrm(ks[20], (DEPTH, 2 * D_FF)),
        'ffn_w_down': w(ks[21], (DEPTH, D_FF, d), D_FF),
    }


def reference(x, c, ctx, c_ctx, ada_w, ada_b, norm_w, pool_w, pool_b, pool_scale,
              attn_w_qkv, attn_q_gain, attn_k_gain, attn_w_o,
              ret_w_in, ret_decay_logit, ret_gn_w, ret_w_out,
              ffn_w_up, ffn_conv_w, ffn_conv_b, ffn_w_down):
    ctx_s = ctx
    silu_c = jax.nn.silu(c)
    silu_cc = jax.nn.silu(c_ctx)
    for i in range(DEPTH):
        kind = i % N_MIXERS
        j = i // N_MIXERS
        need_ctx_out = any(k % N_MIXERS != MIXER_POOL for k in range(i + 1, DEPTH))
        need_ctx_in = need_ctx_out or kind != MIXER_POOL

        sh1, sc1, g1, sh2, sc2, g2 = [m[:, None, :] for m in jnp.split(silu_c @ ada_w[i] + ada_b[i], 6, axis=-1)]
        h = _modulate(_rmsnorm(x, norm_w[i, 0]), sh1, sc1)
        hc = None
        if need_ctx_in:
            csh1, csc1, cg1, csh2, csc2, cg2 = jnp.split(silu_cc @ ada_w[i] + ada_b[i], 6, axis=-1)
            hc = _modulate(_rmsnorm(ctx_s, norm_w[i, 0]), csh1, csc1)

        if kind == MIXER_POOL:
            y = _pool_mixer(h, pool_w[j], pool_b[j], pool_scale[j])
            y_c = _pool_mixer(hc, pool_w[j], pool_b[j], pool_scale[j]) if need_ctx_out else None
        elif kind == MIXER_ATTN:
            y, y_c = _attention_mixer(h, hc, attn_w_qkv[j], attn_q_gain[j], attn_k_gain[j], attn_w_o[j], need_ctx_out)
        else:
            y, y_c = _retention_mixer(h, hc, ret_w_in[j], ret_decay_logit[j], ret_gn_w[j], ret_w_out[j], need_ctx_out)

        x = x + g1 * y
        x = x + g2 * _conv_ffn(_modulate(_rmsnorm(x, norm_w[i, 1]), sh2, sc2),
                               ffn_w_up[i], ffn_conv_w[i], ffn_conv_b[i], ffn_w_down[i])
        if need_ctx_out:
            ctx_s = ctx_s + cg1 * y_c
            ctx_s = ctx_s + cg2 * _conv_ffn(_modulate(_rmsnorm(ctx_s, norm_w[i, 1]), csh2, csc2),
                                            ffn_w_up[i], ffn_conv_w[i], ffn_conv_b[i], ffn_w_down[i])
    return x
```

```python
import contextlib
import numpy as np
import concourse.bass as bass
import concourse.mybir as mybir
from concourse.bass_utils import run_bass_kernel_spmd

F32 = mybir.dt.float32
BF16 = mybir.dt.bfloat16
AF = mybir.ActivationFunctionType
ALU = mybir.AluOpType

ENGS = ("pe", "act", "dve", "pool", "sp")
D_FF = 2816
NCTX = 256


class Res:
    __slots__ = ("name", "last_w", "readers", "dsem", "dcount")

    def __init__(self, name):
        self.name = name
        self.last_w = None
        self.readers = []
        self.dsem = None
        self.dcount = 0


class Op:
    __slots__ = ("eng", "fn", "deps", "signal", "count", "dma", "sem")

    def __init__(self, eng, fn, dma):
        self.eng = eng
        self.fn = fn
        self.deps = []
        self.signal = False
        self.count = 0
        self.dma = dma
        self.sem = None


class Prog:
    def __init__(self, nc):
        self.nc = nc
        self.es = contextlib.ExitStack()
        self.streams = {e: [] for e in ENGS}
        self.eng_sem = {}
        self.dma_res = []
        self.nres = 0
        self.sem_pool = []
        self.last = {e: None for e in ENGS}

    def sem(self, name):
        return self.es.enter_context(self.nc.semaphore(name))

    def res(self, name=None):
        self.nres += 1
        return Res(name or f"r{self.nres}")

    def _deps(self, op, reads, writes):
        deps = []
        for r in reads:
            if r.last_w is not None:
                deps.append(r.last_w)
        for r in writes:
            if r.last_w is not None:
                deps.append(r.last_w)
            deps.extend(r.readers)
        seen = set()
        for d in deps:
            if id(d) in seen or d is op:
                continue
            seen.add(id(d))
            if (not d.dma) and d.eng == op.eng and d.eng == "pe":
                continue
            d.signal = True
            op.deps.append(d)
        for r in reads:
            r.readers.append(op)
        for r in writes:
            r.last_w = op
            r.readers = []

    def op(self, eng, fn, reads=(), writes=()):
        o = Op(eng, fn, False)
        self._deps(o, reads, writes)
        self.streams[eng].append(o)
        self.last[eng] = o
        return o

    def dma(self, eng, out, in_, reads=(), writes=(), owner=None):
        o = Op(eng, None, True)
        if owner is None:
            owner = (list(writes) + list(reads))[0]
        if owner.dsem is None:
            if self.sem_pool:
                owner.dsem, owner.dcount = self.sem_pool.pop()
            else:
                owner.dsem = self.sem("d_" + owner.name)
                owner.dcount = 0
            self.dma_res.append(owner)
        owner.dcount += 16
        o.sem = owner.dsem
        o.count = owner.dcount
        o.fn = lambda e, out=out, in_=in_: e.dma_start(out=out, in_=in_)
        self._deps(o, reads, writes)
        self.streams[eng].append(o)
        return o

    def barrier(self, dummies):
        lasts = [self.last[e] for e in ENGS if self.last[e] is not None]
        dmas = []
        for r in self.dma_res:
            o = Op("sp", None, True)
            o.sem, o.count = r.dsem, r.dcount
            dmas.append(o)
        for e in ENGS:
            fn = dummies.get(e)
            o = Op(e, fn, False)
            for d in lasts:
                if d.eng == e:
                    continue
                d.signal = True
                o.deps.append(d)
            o.deps.extend(dmas)
            self.streams[e].append(o)
            if fn is not None:
                self.last[e] = o
        for r in self.dma_res:
            self.sem_pool.append((r.dsem, r.dcount))
            r.dsem = None
            r.dcount = 0
        self.dma_res = []

    def emit(self):
        nc = self.nc
        for e in ENGS:
            self.eng_sem[e] = self.sem("e_" + e)
        for e in ENGS:
            c = 0
            for o in self.streams[e]:
                if o.dma or o.fn is None:
                    continue
                if o.signal:
                    c += 1
                    o.count = c
                    o.sem = self.eng_sem[e]
        with nc.Block() as block:
            def run(e, eng):
                waited = {}
                for o in self.streams[e]:
                    for d in o.deps:
                        s, c = d.sem, d.count
                        if waited.get(s.num, 0) < c:
                            eng.wait_ge(s, c)
                            waited[s.num] = c
                    if o.fn is None:
                        continue
                    ins = o.fn(eng)
                    if o.dma:
                        ins.then_inc(o.sem, 16)
                    elif o.signal:
                        ins.then_inc(o.sem, 1)
                if e == "sp":
                    for r in self.dma_res:
                        if waited.get(r.dsem.num, 0) < r.dcount:
                            eng.wait_ge(r.dsem, r.dcount)
                    for e2 in ENGS:
                        if e2 == e:
                            continue
                        n = sum(1 for o in self.streams[e2] if (not o.dma) and o.fn is not None and o.signal)
                        if n and waited.get(self.eng_sem[e2].num, 0) < n:
                            eng.wait_ge(self.eng_sem[e2], n)

            @block.tensor
            def _(eng):
                run("pe", eng)

            @block.scalar
            def _(eng):
                run("act", eng)

            @block.vector
            def _(eng):
                run("dve", eng)

            @block.gpsimd
            def _(eng):
                run("pool", eng)

            @block.sync
            def _(eng):
                run("sp", eng)


def colsplit(n, m=512):
    return [(a, min(a + m, n)) for a in range(0, n, m)]


class KB:
    def __init__(self, S, NL=4, dbg=None):
        self.dbg = dbg
        self.S = S
        self.NL = NL
        self.nc = bass.Bass("TRN2", target_bir_lowering=False)
        self.P = Prog(self.nc)
        self.dres = {}

    def mm(self, out, lhsT, rhs, start, stop, reads, writes):
        return self.P.op("pe", lambda e: e.matmul(out, lhsT=lhsT, rhs=rhs, start=start, stop=stop), reads, writes)

    def transpose(self, out, in_, ident, reads, writes):
        return self.P.op("pe", lambda e: e.transpose(out, in_, ident), reads, writes)

    def act(self, out, in_, func, reads, writes, bias=None, scale=1.0):
        if bias is None:
            return self.P.op("act", lambda e: e.activation(out=out, in_=in_, func=func, scale=scale), reads, writes)
        return self.P.op("act", lambda e: e.activation(out=out, in_=in_, func=func, bias=bias, scale=scale), reads, writes)

    def tt(self, eng, out, in0, in1, op, reads, writes):
        return self.P.op(eng, lambda e: e.tensor_tensor(out=out, in0=in0, in1=in1, op=op), reads, writes)

    def ts(self, eng, out, in0, s1, s2, op0, op1, reads, writes):
        return self.P.op(eng, lambda e: e.tensor_scalar(out=out, in0=in0, scalar1=s1, scalar2=s2, op0=op0, op1=op1), reads, writes)

    def tsm(self, eng, out, in0, s1, reads, writes):
        return self.P.op(eng, lambda e: e.tensor_scalar_mul(out, in0, s1), reads, writes)

    def stt(self, eng, out, in0, scalar, in1, op0, op1, reads, writes):
        return self.P.op(eng, lambda e: e.scalar_tensor_tensor(out=out, in0=in0, scalar=scalar, in1=in1, op0=op0, op1=op1), reads, writes)

    def copy(self, eng, out, in_, reads, writes):
        if eng == "act":
            return self.P.op("act", lambda e: e.copy(out=out, in_=in_), reads, writes)
        return self.P.op(eng, lambda e: e.tensor_copy(out=out, in_=in_), reads, writes)

    def memset(self, eng, ap, val, writes):
        return self.P.op(eng, lambda e: e.memset(ap, val), (), writes)

    def recip(self, out, in_, reads, writes):
        return self.P.op("dve", lambda e: e.reciprocal(out=out, in_=in_), reads, writes)

    def ld(self, q, out, in_, reads, writes, owner=None):
        return self.P.dma(q, out, in_, reads, writes, owner)

    def inp(self, name, shape, dt=F32):
        return self.nc.dram_tensor(name, list(shape), dt, kind="ExternalInput").ap()

    def dram(self, name, shape, dt=F32):
        return self.nc.dram_tensor(name, list(shape), dt, kind="Internal").ap()

    def sb(self, es, name, shape, dt, nres=1):
        self.nsb = getattr(self, "nsb", 0) + 1
        name = f"s{self.nsb}_{name}"
        t = es.enter_context(self.nc.sbuf_tensor(name, list(shape), dt))
        rs = [self.P.res(f"{name}{i}") for i in range(nres)]
        return t, rs

    def dr(self, key, lo, hi, g=512):
        out = []
        for b in range(lo // g, (hi - 1) // g + 1):
            k = (key, b)
            if k not in self.dres:
                self.dres[k] = self.P.res(f"{key}_{b}")
            out.append(self.dres[k])
        return out

    def dump(self, name, ap, shape, dt, reads):
        if not self.dbg:
            return
        self.ndump = getattr(self, "ndump", 0) + 1
        t = self.nc.dram_tensor(f"dbg_{name}", list(shape), dt, kind="ExternalOutput").ap()
        self.P.dma("sp", t, ap, reads, [], owner=reads[0])

    def phase_barrier(self):
        bs = self.bar_s
        self.P.barrier({
            "act": lambda e: e.copy(out=bs[:, 0:1], in_=bs[:, 4:5]),
            "dve": lambda e: e.memset(bs[:, 1:2], 0.0),
            "pool": lambda e: e.memset(bs[:, 2:3], 0.0),
            "pe": None, "sp": None,
        })

    def build(self):
        nc, P, S = self.nc, self.P, self.S
        es0 = P.es
        I = self.inp
        self.xT = I("xT", [128, 8, S])
        self.ctxT = I("ctxT", [128, 8, NCTX])
        self.cc = I("cc", [128, 8, 2])
        self.ada_w = I("ada_w", [4, 1024, 6144])
        self.ada_b = I("ada_b", [128, 4, 48])
        self.norm_w = I("norm_w", [128, 4, 2, 8])
        self.pool_w = I("pool_w", [2, 4, 256, 256])
        self.pool_bs = I("pool_bs", [128, 2, 2, 8])
        self.pool_inv = I("pool_inv", [4, S])
        self.pool_invc = I("pool_invc", [4, NCTX])
        self.wup = I("wup", [4, 22, 128, 8, 256])
        self.convw = I("convw", [128, 4, 3, 44])
        self.convb = I("convb", [128, 4, 44])
        self.wdn = I("wdn", [4, D_FF, 1024])
        self.ident_in = I("ident", [128, 128])
        self.wqkv = I("wqkv", [1024, 1536])
        self.wqksw = I("wqksw", [1024, 1280])
        self.wo = I("wo", [1024, 1024])
        self.qkgain = I("qkgain", [128, 4])
        self.ropeC = I("ropeC", [128, S])
        self.ropeS = I("ropeS", [128, S])
        self.QT_d = self.dram("QT_d", [128, 8, S], BF16)
        NCH = S // 128
        self.ret_win = I("ret_win", [1024, 6144])
        self.ret_dl = I("ret_dl", [1, 8])
        self.ret_gnw = I("ret_gnw", [1, 2048])
        self.ret_wout = I("ret_wout", [2048, 1024])
        self.rpos = I("rpos", [128, 4])
        self.rneg = I("rneg", [2, 128])
        self.rmask = I("rmask", [128, 2, 128])
        self.RQ_d = self.dram("RQ_d", [128, 8, S], BF16)
        self.RKT_d = self.dram("RKT_d", [128, 2, 8, S], BF16)
        self.RKtm_d = self.dram("RKtm_d", [128, 2, NCH, 1024], BF16)
        self.RV_d = self.dram("RV_d", [128, NCH, 2048], BF16)
        self.RG_d = self.dram("RG_d", [128, NCH, 2048], BF16)
        self.ROF_d = self.dram("ROF_d", [128, NCH, 2048], F32)
        self.RKtmc_d = self.dram("RKtmc_d", [128, 2, 2, 1024], BF16)
        self.RVc_d = self.dram("RVc_d", [128, 2, 2048], BF16)
        self.outT = nc.dram_tensor("outT", [128, 8, S], F32, kind="ExternalOutput").ap()
        self.xA = self.dram("xA", [128, 8, S])
        self.xB = self.dram("xB", [128, 8, S])
        self.cA = self.dram("cA", [128, 8, NCTX])
        self.cB = self.dram("cB", [128, 8, NCTX])

        self.ps = es0.enter_context(nc.psum_tensor("ps", [128, 8, 512], F32))
        self.psr = [P.res(f"psb{i}") for i in range(8)]
        self.bar_s, _ = self.sb(es0, "bar_s", [128, 8], F32)
        self.ones_bf, self.r_ones = self.sb(es0, "ones_bf", [128, 128], BF16)
        self.eps_t, self.r_eps = self.sb(es0, "eps_t", [128, 1], F32)
        self.ident, self.r_ident = self.sb(es0, "ident", [128, 128], F32)
        self.modT, self.r_mod = self.sb(es0, "modT", [128, 4, 2, 48], F32)
        self.Gt, self.r_G = self.sb(es0, "Gt", [128, 4, 2, 2, 8], F32)
        self.nw_s, self.r_nw = self.sb(es0, "nw_s", [128, 4, 2, 8], F32)
        self.memset("dve", self.ones_bf[:, :], 1.0, self.r_ones)
        self.memset("dve", self.eps_t[:, :], 1e-6, self.r_eps)
        self.one_t, self.r_one = self.sb(es0, "one_t", [128, 1], F32)
        self.memset("dve", self.one_t[:, :], 1.0, self.r_one)
        self.memset("dve", self.bar_s[:, :], 0.0, [])
        self.ld("sp", self.ident[:, :], self.ident_in, [], self.r_ident)
        self.ld("sp", self.nw_s[:, :, :, :], self.norm_w, [], self.r_nw)

        self.modulation()
        self.phase_barrier()
        xin, cin = self.xT, self.ctxT
        kin, kcin = "xT", "ctxT"
        for i in range(self.NL):
            kind = i % 3
            need_ctx_out = any(k % 3 != 0 for k in range(i + 1, 4))
            lastl = (i == self.NL - 1)
            xout = self.outT if lastl else self.xB
            kout = "outT" if lastl else "xB"
            xmid, kmid_ = self.xA, "xA"
            if lastl and self.dbg == "mix":
                xmid, kmid_ = self.outT, "outT"
            if kind == 0:
                self.pool_mixer(i, i // 3, 0, xin, kin, xmid, kmid_, S, self.pool_inv)
                if need_ctx_out:
                    self.phase_barrier()
                    self.pool_mixer(i, i // 3, 1, cin, kcin, self.cA, "cA", NCTX, self.pool_invc)
            elif kind == 1:
                self.attention(i, xin, kin, cin, kcin, xmid, kmid_, self.cA, "cA", need_ctx_out)
            else:
                self.retention(i, xin, kin, cin, kcin, xmid, kmid_)
            self.phase_barrier()
            if lastl and self.dbg == "mix":
                break
            self.ffn(i, 0, self.xA, "xA", xout, kout, S)
            if need_ctx_out:
                self.phase_barrier()
                self.ffn(i, 1, self.cA, "cA", self.cB, "cB", NCTX)
            self.phase_barrier()
            xin, kin = self.xB, "xB"
            cin, kcin = self.cB, "cB"
        P.emit()
        P.es.close()
        return nc

    def modulation(self):
        nc, P = self.nc, self.P
        with contextlib.ExitStack() as es:
            ccs, r_cc = self.sb(es, "ccs", [128, 8, 2], F32)
            scc, r_scc = self.sb(es, "scc", [128, 8, 2], F32)
            adab, r_adab = self.sb(es, "adab", [128, 4, 48], F32)
            wbuf = [self.sb(es, f"adaw{k}", [128, 6144], F32) for k in range(2)]
            self.ld("sp", ccs[:, :, :], self.cc, [], r_cc)
            self.ld("sp", adab[:, :, :], self.ada_b, [], r_adab)
            self.act(scc[:, :, :], ccs[:, :, :], AF.Silu, r_cc, r_scc)
            n = 0
            macc, r_macc = self.sb(es, "macc", [128, 96], F32)
            for i in range(4):
                for k in range(8):
                    wt, rw = wbuf[n % 2]
                    pb = n % 2
                    n += 1
                    self.ld("sp", wt[:, :], self.ada_w[i, k * 128:(k + 1) * 128, :], [], rw)
                    for m in range(48):
                        self.mm(self.ps[:, pb, 2 * m:2 * m + 2], wt[:, m * 128:(m + 1) * 128], scc[:, k, :],
                                True, True, rw + r_scc, [self.psr[pb]])
                    if k == 0:
                        self.copy("dve", macc[:, :], self.ps[:, pb, 0:96], [self.psr[pb]], r_macc)
                    else:
                        self.tt("dve", macc[:, :], macc[:, :], self.ps[:, pb, 0:96], ALU.add, [self.psr[pb]] + r_macc, r_macc)
                rps = r_macc
                for col in range(2):
                    pv = macc[:, :].rearrange("p (m c) -> p m c", c=2)[:, :, col]
                    self.tt("dve", self.modT[:, i, col, :], pv, adab[:, i, :], ALU.add, rps + r_adab, self.r_mod)
                    for wn in range(2):
                        sc = self.modT[:, i, col, 8 + 24 * wn:16 + 24 * wn]
                        self.stt("dve", self.Gt[:, i, col, wn, :], sc, 1.0, self.nw_s[:, i, wn, :], ALU.add, ALU.mult,
                                 self.r_mod + self.r_nw, self.r_G)

    def mod(self, i, col, which):
        return self.modT[:, i, col, which * 8:(which + 1) * 8]

    def norm_tile(self, xt, rx, n, G, SH, out, rout, sq, rsq, rstd, rrstd, tmp, rtmp, psb):
        for (a, b) in colsplit(n):
            w = b - a
            self.act(sq[:, :, 0:w], xt[:, :, a:b], AF.Square, rx, rsq)
            for c in range(8):
                self.mm(self.ps[:, psb, 0:w], self.ones_bf[:, :], sq[:, c, 0:w], c == 0, c == 7,
                        rsq + self.r_ones, [self.psr[psb]])
            self.act(rstd[:, 0:w], self.ps[:, psb, 0:w], AF.Sqrt, [self.psr[psb]] + self.r_eps, rrstd,
                     bias=self.eps_t[:, 0:1], scale=1.0 / 1024.0)
            self.recip(rstd[:, 0:w], rstd[:, 0:w], rrstd, rrstd)
            for c in range(8):
                t, rt = tmp[c % 2], rtmp[c % 2]
                self.stt("dve", t[:, 0:w], xt[:, c, a:b], G[:, c:c + 1], rstd[:, 0:w], ALU.mult, ALU.mult,
                         rx + rrstd + self.r_G, rt)
                self.act(out[:, c, a:b], t[:, 0:w], AF.Identity, rt + self.r_mod, rout, bias=SH[:, c:c + 1], scale=1.0)

    def pool_mixer(self, i, jp, col, X_in, kin, X_mid, kmid, Stok, invtab):
        NT = min(512, Stok)
        W = NT + 16
        ntile = Stok // NT
        with contextlib.ExitStack() as es:
            xts = [self.sb(es, f"pm_x{k}", [128, 8, W], F32) for k in range(2)]
            h, rh = self.sb(es, "pm_h", [128, 8, W], F32)
            sq, rsq = self.sb(es, "pm_sq", [128, 8, 512], BF16)
            rstd, rrstd = self.sb(es, "pm_rstd", [128, 512], F32)
            tmps = [self.sb(es, f"pm_t{k}", [128, 512], F32) for k in range(2)]
            wa = [self.sb(es, f"pm_wa{k}", [128, W], F32) for k in range(2)]
            wb = [self.sb(es, f"pm_wb{k}", [128, W], F32) for k in range(2)]
            pooled, rpooled = self.sb(es, "pm_pooled", [128, 8, NT], BF16, 8)
            inv, rinv = self.sb(es, "pm_inv", [128, 4, NT], F32)
            pw, rpw = self.sb(es, "pm_w", [128, 4, 2, 256], BF16)
            pbs, rpbs = self.sb(es, "pm_bs", [128, 2, 8], F32)
            AB, rAB = self.sb(es, "pm_AB", [128, 2, 8], F32)
            ot = [self.sb(es, f"pm_o{k}", [128, 8, NT], F32) for k in range(2)]
            yt = [self.sb(es, f"pm_y{k}", [128, NT], F32) for k in range(2)]
            self.ld("pool", pw[:, :, :, :], self.pool_w[jp].rearrange("g (k p) n -> p g k n", p=128), [], rpw)
            self.ld("sp", pbs[:, :, :], self.pool_bs[:, jp, :, :], [], rpbs)
            g1 = self.mod(i, col, 2)
            self.tt("dve", AB[:, 0, :], g1, pbs[:, 1, :], ALU.mult, self.r_mod + rpbs, rAB)
            self.tt("dve", AB[:, 1, :], AB[:, 0, :], pbs[:, 0, :], ALU.mult, rAB + rpbs, rAB)
            G = self.Gt[:, i, col, 0, :]
            SH = self.mod(i, col, 0)
            for tau in range(ntile):
                t0 = tau * NT
                lo, hi = max(t0 - 8, 0), min(t0 + NT + 8, Stok)
                xt, rx = xts[tau % 2]
                if lo > t0 - 8:
                    self.memset("pool", xt[:, :, 0:8], 0.0, rx)
                if hi < t0 + NT + 8:
                    self.memset("pool", xt[:, :, NT + 8:W], 0.0, rx)
                self.ld("sp", xt[:, :, lo - (t0 - 8):hi - (t0 - 8)], X_in[:, :, lo:hi], self.dr(kin, lo, hi), rx)
                self.ld("sp", inv[:, :, :], invtab[:, t0:t0 + NT].partition_broadcast(128), [], rinv)
                self.norm_tile(xt, rx, W, G, SH, h, rh, sq, rsq, rstd, rrstd,
                               [t[0] for t in tmps], [t[1] for t in tmps], 7)
                if lo > t0 - 8:
                    self.memset("dve", h[:, :, 0:8], 0.0, rh)
                if hi < t0 + NT + 8:
                    self.memset("dve", h[:, :, NT + 8:W], 0.0, rh)
                for c in range(8):
                    g = c // 2
                    A_, rA = wa[c % 2]
                    B_, rB = wb[c % 2]
                    hc = h[:, c, :]
                    e = "dve"
                    if g == 0:
                        self.tt(e, A_[:, 8:8 + NT], hc[:, 7:7 + NT], hc[:, 8:8 + NT], ALU.add, rh, rA)
                        Sv, rS = A_, rA
                    else:
                        self.tt(e, A_[:, 0:W - 1], hc[:, 0:W - 1], hc[:, 1:W], ALU.add, rh, rA)
                        if g == 1:
                            self.tt(e, B_[:, 8:8 + NT], A_[:, 6:6 + NT], A_[:, 8:8 + NT], ALU.add, rA, rB)
                            Sv, rS = B_, rB
                        else:
                            self.tt(e, B_[:, 0:W - 3], A_[:, 0:W - 3], A_[:, 2:W - 1], ALU.add, rA, rB)
                            if g == 2:
                                self.tt(e, A_[:, 8:8 + NT], B_[:, 4:4 + NT], B_[:, 8:8 + NT], ALU.add, rB, rA)
                                Sv, rS = A_, rA
                            else:
                                self.tt(e, A_[:, 0:W - 7], B_[:, 0:W - 7], B_[:, 4:W - 3], ALU.add, rB, rA)
                                self.tt(e, B_[:, 8:8 + NT], A_[:, 0:NT], A_[:, 8:8 + NT], ALU.add, rA, rB)
                                Sv, rS = B_, rB
                    self.tt("pool", Sv[:, 8:8 + NT], Sv[:, 8:8 + NT], inv[:, g, :], ALU.mult, rS + rinv, rS)
                    self.tt("pool", pooled[:, c, :], Sv[:, 8:8 + NT], hc[:, 8:8 + NT], ALU.subtract, rS + rh,
                            [rpooled[c]])
                o_, ro = ot[tau % 2]
                for c in range(8):
                    g, mo = c // 2, c % 2
                    pb = c % 2
                    for ki in range(2):
                        self.mm(self.ps[:, pb, 0:NT], pw[:, g, ki, mo * 128:(mo + 1) * 128], pooled[:, 2 * g + ki, :],
                                ki == 0, ki == 1, rpw + [rpooled[2 * g + ki]], [self.psr[pb]])
                    y_, ry = yt[c % 2]
                    self.act(y_[:, :], self.ps[:, pb, 0:NT], AF.Identity, [self.psr[pb]] + rAB, ry,
                             bias=AB[:, 1, c:c + 1], scale=AB[:, 0, c:c + 1])
                    self.tt("dve", o_[:, c, :], y_[:, :], xt[:, c, 8:8 + NT], ALU.add, ry + rx, ro)
                self.ld("sp", X_mid[:, :, t0:t0 + NT], o_[:, :, :], ro, self.dr(kmid, t0, t0 + NT), owner=ro[0])


    def attention(self, i, X_in, kin, C_in, kcin, X_mid, kmid, C_mid, kcmid, need_ctx_out):
        S = self.S
        NKL = S // 128
        NK = NKL + NCTX // 128
        scale = 128.0 ** -0.5
        with contextlib.ExitStack() as es0:
            KT, rKT = self.sb(es0, "a_KT", [128, 2, S + NCTX], BF16, 1)
            V, rV = self.sb(es0, "a_V", [128, NK, 256], BF16, 1)
            QTc, rQTc = self.sb(es0, "a_QTc", [128, 8, NCTX], BF16)
            gn, rgn = self.sb(es0, "a_gn", [128, 4], F32)
            self.ld("sp", gn[:, :], self.qkgain, [], rgn)
            with contextlib.ExitStack() as es:
                wq, rwq = self.sb(es, "a_wq", [128, 8, 1536], BF16)
                ws, rws = self.sb(es, "a_ws", [128, 8, 1280], BF16)
                for k in range(8):
                    self.ld("pool", wq[:, k, :], self.wqkv[k * 128:(k + 1) * 128, :], [], rwq)
                    self.ld("pool", ws[:, k, :], self.wqksw[k * 128:(k + 1) * 128, :], [], rws)
                xt, rx = self.sb(es, "a_x", [128, 8, 512], F32)
                h, rh = self.sb(es, "a_h", [128, 8, 512], BF16)
                sq, rsq = self.sb(es, "a_sq", [128, 8, 512], BF16)
                rstd, rrstd = self.sb(es, "a_rstd", [128, 512], F32)
                tmps = [self.sb(es, f"a_t{k}", [128, 512], F32) for k in range(2)]
                Ct, rCt = self.sb(es, "a_C", [128, 512], F32)
                St, rSt = self.sb(es, "a_S", [128, 512], F32)
                sqq = [self.sb(es, f"a_sqq{k}", [128, 512], BF16) for k in range(2)]
                rq = [self.sb(es, f"a_rq{k}", [128, 512], F32) for k in range(2)]
                t1 = [self.sb(es, f"a_t1{k}", [128, 512], F32) for k in range(2)]
                t2 = [self.sb(es, f"a_t2{k}", [128, 512], F32) for k in range(2)]
                qout = [self.sb(es, f"a_qo{k}", [128, 8, 512], BF16) for k in range(2)]

                def proj(col, Xsrc, ksrc, t0, n, koff, vch0, rope, qdst, rqdst):
                    self.ld("sp", xt[:, :, 0:n], Xsrc[:, :, t0:t0 + n], self.dr(ksrc, t0, t0 + n), rx)
                    self.norm_tile(xt, rx, n, self.Gt[:, i, col, 0, :], self.mod(i, col, 0), h, rh, sq, rsq,
                                   rstd, rrstd, [t[0] for t in tmps], [t[1] for t in tmps], 7)
                    if rope:
                        self.ld("sp", Ct[:, 0:n], self.ropeC[:, t0:t0 + n], [], rCt)
                        self.ld("sp", St[:, 0:n], self.ropeS[:, t0:t0 + n], [], rSt)
                    for hc in range(10):
                        isq = hc < 8
                        pa, pbk = hc % 2, 2 + hc % 2
                        c0 = hc * 128
                        for k in range(8):
                            self.mm(self.ps[:, pa, 0:n], wq[:, k, c0:c0 + 128], h[:, k, 0:n], k == 0, k == 7,
                                    rwq + rh, [self.psr[pa]])
                        if rope:
                            for k in range(8):
                                self.mm(self.ps[:, pbk, 0:n], ws[:, k, c0:c0 + 128], h[:, k, 0:n], k == 0, k == 7,
                                        rws + rh, [self.psr[pbk]])
                        sq_, rsq_ = sqq[hc % 2]
                        rq_, rrq_ = rq[hc % 2]
                        self.act(sq_[:, 0:n], self.ps[:, pa, 0:n], AF.Square, [self.psr[pa]], rsq_)
                        self.mm(self.ps[:, 4 + hc % 2, 0:n], self.ones_bf[:, :], sq_[:, 0:n], True, True,
                                rsq_ + self.r_ones, [self.psr[4 + hc % 2]])
                        self.act(rq_[:, 0:n], self.ps[:, 4 + hc % 2, 0:n], AF.Sqrt, [self.psr[4 + hc % 2]] + self.r_eps,
                                 rrq_, bias=self.eps_t[:, 0:1], scale=1.0 / 128.0)
                        self.recip(rq_[:, 0:n], rq_[:, 0:n], rrq_, rrq_)
                        g0 = 0 if isq else 2
                        if isq:
                            dst, rdst = qdst[:, hc, 0:n], rqdst
                        else:
                            dst, rdst = KT[:, hc - 8, koff:koff + n], rKT
                        if rope:
                            a_, ra_ = t1[hc % 2]
                            b_, rb_ = t2[hc % 2]
                            self.stt("dve", a_[:, 0:n], self.ps[:, pa, 0:n], gn[:, g0:g0 + 1], Ct[:, 0:n],
                                     ALU.mult, ALU.mult, [self.psr[pa]] + rgn + rCt, ra_)
                            self.stt("dve", b_[:, 0:n], self.ps[:, pbk, 0:n], gn[:, g0 + 1:g0 + 2], St[:, 0:n],
                                     ALU.mult, ALU.mult, [self.psr[pbk]] + rgn + rSt, rb_)
                            self.tt("pool", a_[:, 0:n], a_[:, 0:n], b_[:, 0:n], ALU.add, ra_ + rb_, ra_)
                            self.tt("pool", dst, a_[:, 0:n], rq_[:, 0:n], ALU.mult, ra_ + rrq_, rdst)
                        else:
                            self.stt("dve", dst, self.ps[:, pa, 0:n], gn[:, g0:g0 + 1], rq_[:, 0:n],
                                     ALU.mult, ALU.mult, [self.psr[pa]] + rgn + rrq_, rdst)
                    for s_ in range(n // 128):
                        pv = 6
                        for k in range(8):
                            self.mm(self.ps[:, pv, 0:256], h[:, k, s_ * 128:(s_ + 1) * 128], wq[:, k, 1280:1536],
                                    k == 0, k == 7, rh + rwq, [self.psr[pv]])
                        self.copy("act", V[:, vch0 + s_, :], self.ps[:, pv, 0:256], [self.psr[pv]], rV)

                proj(1, C_in, kcin, 0, NCTX, S, NKL, False, QTc, rQTc)
                for tau in range(S // 512):
                    qo_, rqo_ = qout[tau % 2]
                    proj(0, X_in, kin, tau * 512, 512, tau * 512, tau * 4, True, qo_, rqo_)
                    self.ld("sp", self.QT_d[:, :, tau * 512:(tau + 1) * 512], qo_[:, :, :], rqo_,
                            self.dr("QT_d", tau * 512, (tau + 1) * 512), owner=rqo_[0])
            self.phase_barrier()
            with contextlib.ExitStack() as es:
                wo, rwo = self.sb(es, "a_wo", [128, 8, 1024], BF16)
                for k in range(8):
                    self.ld("pool", wo[:, k, :], self.wo[k * 128:(k + 1) * 128, :], [], rwo)
                QTt = [self.sb(es, f"a_QT{k}", [128, 8, 512], BF16) for k in range(2)]
                pbuf = [self.sb(es, f"a_p{k}", [128, 512], BF16) for k in range(3)]
                acc = [self.sb(es, f"a_acc{k}", [128, 512], F32) for k in range(2)]
                accb, raccb = self.sb(es, "a_accb", [128, 512], BF16)
                rden, rrden = self.sb(es, "a_rden", [128, 512], F32)
                ao, rao = self.sb(es, "a_ao", [128, 8, 512], BF16, 8)
                xts = [self.sb(es, f"a_xx{k}", [128, 8, 512], F32) for k in range(2)]
                ots = [self.sb(es, f"a_oo{k}", [128, 8, 512], F32) for k in range(2)]
                g1s = [self.mod(i, 0, 2), self.mod(i, 1, 2)]

                def core(col, Q, rQ, n, kcs, Xsrc, ksrc, Xdst, kdst, t0, par):
                    xt_, rx_ = xts[par]
                    o_, ro_ = ots[par]
                    self.ld("sp", xt_[:, :, 0:n], Xsrc[:, :, t0:t0 + n], self.dr(ksrc, t0, t0 + n), rx_)
                    for hh in range(8):
                        kv = hh // 4
                        po = 2 + hh % 2
                        nk = len(kcs)
                        for idx, kc in enumerate(kcs):
                            pss = idx % 2
                            self.mm(self.ps[:, pss, 0:n], KT[:, kv, kc * 128:(kc + 1) * 128], Q[:, hh, 0:n], True, True,
                                    rKT + rQ, [self.psr[pss]])
                            p_, rp_ = pbuf[idx % 3]
                            self.act(p_[:, 0:n], self.ps[:, pss, 0:n], AF.Exp, [self.psr[pss]], rp_, scale=scale)
                            a_, ra_ = acc[idx % 2]
                            if idx < 2:
                                self.copy("dve", a_[:, 0:n], p_[:, 0:n], rp_, ra_)
                            else:
                                self.tt("dve", a_[:, 0:n], a_[:, 0:n], p_[:, 0:n], ALU.add, ra_ + rp_, ra_)
                            self.mm(self.ps[:, po, 0:n], V[:, kc, kv * 128:(kv + 1) * 128], p_[:, 0:n], idx == 0,
                                    idx == nk - 1, rV + rp_, [self.psr[po]])
                        self.tt("pool", accb[:, 0:n], acc[0][0][:, 0:n], acc[1][0][:, 0:n], ALU.add,
                                acc[0][1] + acc[1][1], raccb)
                        pd = 4 + hh % 2
                        self.mm(self.ps[:, pd, 0:n], self.ones_bf[:, :], accb[:, 0:n], True, True, raccb + self.r_ones,
                                [self.psr[pd]])
                        self.recip(rden[:, 0:n], self.ps[:, pd, 0:n], [self.psr[pd]], rrden)
                        self.tt("dve", ao[:, hh, 0:n], self.ps[:, po, 0:n], rden[:, 0:n], ALU.mult,
                                [self.psr[po]] + rrden, [rao[hh]])
                    for m in range(8):
                        py = 6 + m % 2
                        for k in range(8):
                            self.mm(self.ps[:, py, 0:n], wo[:, k, m * 128:(m + 1) * 128], ao[:, k, 0:n], k == 0, k == 7,
                                    rwo + [rao[k]], [self.psr[py]])
                        self.stt("dve", o_[:, m, 0:n], self.ps[:, py, 0:n], g1s[col][:, m:m + 1], xt_[:, m, 0:n],
                                 ALU.mult, ALU.add, [self.psr[py]] + self.r_mod + rx_, ro_)
                    self.ld("sp", Xdst[:, :, t0:t0 + n], o_[:, :, 0:n], ro_, self.dr(kdst, t0, t0 + n), owner=ro_[0])

                self.dump("KT", KT[:, :, :], [128, 2, S + NCTX], BF16, rKT)
                self.dump("V", V[:, :, :], [128, NK, 256], BF16, rV)
                self.dump("QTc", QTc[:, :, :], [128, 8, NCTX], BF16, rQTc)
                if need_ctx_out:
                    core(1, QTc, rQTc, NCTX, [NKL, NKL + 1], C_in, kcin, C_mid, kcmid, 0, 0)
                self.dump("ao_c", ao[:, :, :], [128, 8, 512], BF16, rao)
                self.dump("rden_c", rden[:, :], [128, 512], F32, rrden)
                self.dump("p_c", pbuf[1][0][:, :], [128, 512], BF16, pbuf[1][1])
                allk = list(range(NK))
                for tau in range(S // 512):
                    Q_, rQ_ = QTt[tau % 2]
                    self.ld("sp", Q_[:, :, :], self.QT_d[:, :, tau * 512:(tau + 1) * 512],
                            self.dr("QT_d", tau * 512, (tau + 1) * 512), rQ_)
                    core(0, Q_, rQ_, 512, allk, X_in, kin, X_mid, kmid, tau * 512, tau % 2)


    def retention(self, i, X_in, kin, C_in, kcin, X_mid, kmid):
        S = self.S
        NCH = S // 128
        with contextlib.ExitStack() as es0:
            lg, rlg = self.sb(es0, "r_lg", [128, 8], F32)
            g128, rg128 = self.sb(es0, "r_g128", [128, 8], F32)
            qo, rqo = self.sb(es0, "r_qo", [128, 8], F32)
            kts, rkts = self.sb(es0, "r_kts", [128, 8], F32)
            kTtab, rkTtab = self.sb(es0, "r_kTtab", [128, 8, 128], F32)
            mask, rmask = self.sb(es0, "r_mask", [128, 2, 128], F32)
            rposs, rrpos = self.sb(es0, "r_pos", [128, 4], F32)
            rnegb, rrneg = self.sb(es0, "r_negb", [128, 2, 128], F32)
            dlt, rdl = self.sb(es0, "r_dl", [128, 8], F32)
            self.ld("sp", dlt[:, :], self.ret_dl.partition_broadcast(128), [], rdl)
            self.ld("sp", rposs[:, :], self.rpos, [], rrpos)
            self.ld("sp", rnegb[:, :, :], self.rneg.partition_broadcast(128), [], rrneg)
            self.ld("sp", mask[:, :, :], self.rmask, [], rmask)
            self.act(lg[:, :], dlt[:, :], AF.Exp, rdl, rlg, scale=-1.0)
            self.act(lg[:, :], lg[:, :], AF.Ln, rlg + self.r_one, rlg, bias=self.one_t[:, 0:1], scale=1.0)
            self.tsm("dve", lg[:, :], lg[:, :], -1.0, rlg, rlg)
            self.act(g128[:, :], lg[:, :], AF.Exp, rlg, rg128, scale=128.0)
            for d in range(2):
                self.act(qo[:, 4 * d:4 * d + 4], lg[:, 4 * d:4 * d + 4], AF.Exp, rlg + rrpos, rqo, scale=rposs[:, d:d + 1])
                self.act(kts[:, 4 * d:4 * d + 4], lg[:, 4 * d:4 * d + 4], AF.Exp, rlg + rrpos, rkts,
                         scale=rposs[:, 2 + d:3 + d])
                for hh in range(4):
                    dh = 4 * d + hh
                    self.act(kTtab[:, dh, :], rnegb[:, d, :], AF.Exp, rrneg + rlg, rkTtab, scale=lg[:, dh:dh + 1])
            self.tsm("dve", kts[:, :], kts[:, :], 1.0 / 16.0, rkts, rkts)
            self.tsm("dve", kTtab[:, :, :], kTtab[:, :, :], 1.0 / 16.0, rkTtab, rkTtab)

            with contextlib.ExitStack() as es:
                win, rwin = self.sb(es, "r_win", [128, 8, 6144], BF16)
                for k in range(8):
                    for q4 in range(4):
                        self.ld("pool", win[:, k, q4 * 1536:(q4 + 1) * 1536],
                                self.ret_win[k * 128:(k + 1) * 128, q4 * 1536:(q4 + 1) * 1536], [], rwin)
                xt, rx = self.sb(es, "r_x", [128, 8, 512], F32)
                h, rh = self.sb(es, "r_h", [128, 8, 512], BF16)
                sq, rsq = self.sb(es, "r_sq", [128, 8, 512], BF16)
                rstd, rrstd = self.sb(es, "r_rstd", [128, 512], F32)
                tmps = [self.sb(es, f"r_t{k}", [128, 512], F32) for k in range(2)]
                QTt, rQTt = self.sb(es, "r_QTt", [128, 8, 512], BF16)
                KTt, rKTt = self.sb(es, "r_KTt", [128, 2, 8, 512], BF16)
                Ktm = [self.sb(es, f"r_Ktm{k}", [128, 2, 1024], BF16) for k in range(2)]
                Vt = [self.sb(es, f"r_Vt{k}", [128, 2048], BF16) for k in range(2)]
                Gt_ = [self.sb(es, f"r_Gt{k}", [128, 2048], BF16) for k in range(2)]

                def rproj(col, Xsrc, ksrc, t0, n, is_ctx):
                    self.ld("sp", xt[:, :, 0:n], Xsrc[:, :, t0:t0 + n], self.dr(ksrc, t0, t0 + n), rx)
                    self.norm_tile(xt, rx, n, self.Gt[:, i, col, 0, :], self.mod(i, col, 0), h, rh, sq, rsq,
                                   rstd, rrstd, [t[0] for t in tmps], [t[1] for t in tmps], 7)
                    if not is_ctx:
                        for c in range(8):
                            pb = c % 2
                            for k in range(8):
                                self.mm(self.ps[:, pb, 0:n], win[:, k, c * 128:(c + 1) * 128], h[:, k, 0:n], k == 0, k == 7,
                                        rwin + rh, [self.psr[pb]])
                            self.copy("act", QTt[:, c, 0:n], self.ps[:, pb, 0:n], [self.psr[pb]], rQTt)
                        self.ld("sp", self.RQ_d[:, :, t0:t0 + n], QTt[:, :, 0:n], rQTt, self.dr("RQ_d", t0, t0 + n),
                                owner=rQTt[0])
                        for c in range(8):
                            pb = 2 + c % 2
                            hh = c // 2
                            for k in range(8):
                                self.mm(self.ps[:, pb, 0:n], win[:, k, 1024 + c * 128:1024 + (c + 1) * 128], h[:, k, 0:n],
                                        k == 0, k == 7, rwin + rh, [self.psr[pb]])
                            for d in range(2):
                                for s_ in range(n // 128):
                                    self.tt("dve", KTt[:, d, c, s_ * 128:(s_ + 1) * 128],
                                            self.ps[:, pb, s_ * 128:(s_ + 1) * 128], kTtab[:, 4 * d + hh, :], ALU.mult,
                                            [self.psr[pb]] + rkTtab, rKTt)
                        for d in range(2):
                            self.ld("sp", self.RKT_d[:, d, :, t0:t0 + n], KTt[:, d, :, 0:n], rKTt,
                                    self.dr("RKT_d", t0, t0 + n), owner=rKTt[0])
                    for s_ in range(n // 128):
                        ch = (t0 + s_ * 128) // 128
                        par = s_ % 2
                        km, rkm = Ktm[par]
                        v_, rv_ = Vt[par]
                        g_, rg_ = Gt_[par]
                        lhs = lambda k: h[:, k, s_ * 128:(s_ + 1) * 128]
                        for grp in range(2):
                            pb = 4 + grp
                            for k in range(8):
                                self.mm(self.ps[:, pb, :], lhs(k), win[:, k, 1024 + grp * 512:1024 + (grp + 1) * 512],
                                        k == 0, k == 7, rh + rwin, [self.psr[pb]])
                            for hl in range(2):
                                hh = grp * 2 + hl
                                for d in range(2):
                                    self.act(km[:, d, hh * 256:(hh + 1) * 256], self.ps[:, pb, hl * 256:(hl + 1) * 256],
                                             AF.Identity, [self.psr[pb]] + rkts, rkm, scale=kts[:, 4 * d + hh:4 * d + hh + 1])
                        for grp in range(4):
                            pb = 6 + grp % 2
                            for k in range(8):
                                self.mm(self.ps[:, pb, :], lhs(k), win[:, k, 2048 + grp * 512:2048 + (grp + 1) * 512],
                                        k == 0, k == 7, rh + rwin, [self.psr[pb]])
                            self.copy("dve", v_[:, grp * 512:(grp + 1) * 512], self.ps[:, pb, :], [self.psr[pb]], rv_)
                        if is_ctx:
                            self.ld("sp", self.RKtmc_d[:, :, ch, :], km[:, :, :], rkm, self.dr("RKtmc_d", 0, 1), owner=rkm[0])
                            self.ld("sp", self.RVc_d[:, ch, :], v_[:, :], rv_, self.dr("RVc_d", 0, 1), owner=rv_[0])
                            continue
                        self.ld("sp", self.RKtm_d[:, :, ch, :], km[:, :, :], rkm, self.dr("RKtm_d", t0, t0 + n), owner=rkm[0])
                        self.ld("sp", self.RV_d[:, ch, :], v_[:, :], rv_, self.dr("RV_d", t0, t0 + n), owner=rv_[0])
                        for grp in range(4):
                            pb = 6 + grp % 2
                            for k in range(8):
                                self.mm(self.ps[:, pb, :], lhs(k), win[:, k, 4096 + grp * 512:4096 + (grp + 1) * 512],
                                        k == 0, k == 7, rh + rwin, [self.psr[pb]])
                            self.act(g_[:, grp * 512:(grp + 1) * 512], self.ps[:, pb, :], AF.Silu, [self.psr[pb]], rg_)
                        self.ld("sp", self.RG_d[:, ch, :], g_[:, :], rg_, self.dr("RG_d", t0, t0 + n), owner=rg_[0])

                rproj(1, C_in, kcin, 0, NCTX, True)
                for tau in range(S // 512):
                    rproj(0, X_in, kin, tau * 512, 512, False)
            self.phase_barrier()

            for d in range(2):
                with contextlib.ExitStack() as es:
                    R, rR = self.sb(es, "r_R", [128, 8, 512], F32, 8)
                    Rb, rRb = self.sb(es, "r_Rb", [128, 8, 512], BF16, 8)
                    QTc = [self.sb(es, f"r_QTc{k}", [128, 8, 128], BF16) for k in range(2)]
                    KTc = [self.sb(es, f"r_KTc{k}", [128, 8, 128], BF16) for k in range(2)]
                    Kmc = [self.sb(es, f"r_Kmc{k}", [128, 1024], BF16) for k in range(2)]
                    Vc = [self.sb(es, f"r_Vc{k}", [128, 2048], BF16) for k in range(2)]
                    AT = [self.sb(es, f"r_AT{k}", [128, 128], BF16) for k in range(2)]
                    osb = [self.sb(es, f"r_osb{k}", [128, 2048], F32) for k in range(2)]
                    if d == 1:
                        OFc = [self.sb(es, f"r_OFc{k}", [128, 2048], F32) for k in range(2)]
                        Gc = [self.sb(es, f"r_Gc{k}", [128, 2048], BF16) for k in range(2)]
                        st, rst = self.sb(es, "r_st", [128, 4, 6], F32)
                        mv, rmv = self.sb(es, "r_mv", [128, 4, 2], F32)
                        rs4, rrs4 = self.sb(es, "r_rs4", [128, 4], F32)
                        nmr, rnmr = self.sb(es, "r_nmr", [128, 4], F32)
                        yn = [self.sb(es, f"r_yn{k}", [128, 512], F32) for k in range(2)]
                        z, rz = self.sb(es, "r_z", [128, 2048], F32, 4)
                        zT, rzT = self.sb(es, "r_zT", [128, 16, 512], BF16)
                        wout, rwout = self.sb(es, "r_wout", [128, 16, 1024], BF16)
                        gnwb, rgnwb = self.sb(es, "r_gnwb", [128, 2048], F32)
                        xt2, rx2 = self.sb(es, "r_x2", [128, 8, 512], F32)
                        ot2, ro2 = self.sb(es, "r_o2", [128, 8, 512], F32)
                        for k in range(16):
                            self.ld("pool", wout[:, k, :], self.ret_wout[k * 128:(k + 1) * 128, :], [], rwout)
                        self.ld("sp", gnwb[:, :], self.ret_gnw.partition_broadcast(128), [], rgnwb)
                    g1 = self.mod(i, 0, 2)
                    for a8 in range(8):
                        self.memset("dve", R[:, a8, :], 0.0, [rR[a8]])
                        self.memset("pool", Rb[:, a8, :], 0.0, [rRb[a8]])
                    steps = []
                    cord = [0, 1] if d == 0 else [1, 0]
                    for cc in cord:
                        steps.append((True, cc))
                    lord = list(range(NCH)) if d == 0 else list(range(NCH - 1, -1, -1))
                    for c in lord:
                        steps.append((False, c))
                    for si, (is_ctx, c) in enumerate(steps):
                        par = si % 2
                        do_out = not is_ctx
                        km, rkm = Kmc[par]
                        v_, rv_ = Vc[par]
                        if is_ctx:
                            self.ld("sp", km[:, :], self.RKtmc_d[:, d, c, :], self.dr("RKtmc_d", 0, 1), rkm)
                            self.ld("sp", v_[:, :], self.RVc_d[:, c, :], self.dr("RVc_d", 0, 1), rv_)
                        else:
                            t0 = c * 128
                            self.ld("sp", km[:, :], self.RKtm_d[:, d, c, :], self.dr("RKtm_d", t0, t0 + 128), rkm)
                            self.ld("sp", v_[:, :], self.RV_d[:, c, :], self.dr("RV_d", t0, t0 + 128), rv_)
                            q_, rq_ = QTc[par]
                            kt_, rkt_ = KTc[par]
                            self.ld("sp", q_[:, :, :], self.RQ_d[:, :, t0:t0 + 128], self.dr("RQ_d", t0, t0 + 128), rq_)
                            self.ld("sp", kt_[:, :, :], self.RKT_d[:, d, :, t0:t0 + 128], self.dr("RKT_d", t0, t0 + 128), rkt_)
                            o_, ro_ = osb[par]
                            if d == 1:
                                of_, rof_ = OFc[par]
                                gc_, rgc_ = Gc[par]
                                self.ld("sp", of_[:, :], self.ROF_d[:, c, :], self.dr("ROF_d", t0, t0 + 128), rof_)
                                self.ld("sp", gc_[:, :], self.RG_d[:, c, :], self.dr("RG_d", t0, t0 + 128), rgc_)
                        for hh in range(4):
                            dh = 4 * d + hh
                            hc = slice(hh * 512, (hh + 1) * 512)
                            if do_out:
                                pa = hh % 2
                                for a in range(2):
                                    self.mm(self.ps[:, pa, 0:128], kt_[:, 2 * hh + a, :], q_[:, 2 * hh + a, :], a == 0, a == 1,
                                            rkt_ + rq_, [self.psr[pa]])
                                at_, rat_ = AT[hh % 2]
                                self.tt("dve", at_[:, :], self.ps[:, pa, 0:128], mask[:, d, :], ALU.mult,
                                        [self.psr[pa]] + rmask, rat_)
                                po = 2 + hh % 2
                                self.mm(self.ps[:, po, :], at_[:, :], v_[:, hc], True, False, rat_ + rv_, [self.psr[po]])
                                for a in range(2):
                                    self.mm(self.ps[:, po, :], q_[:, 2 * hh + a, :], Rb[:, 2 * hh + a, :], False, a == 1,
                                            rq_ + [rRb[2 * hh + a]], [self.psr[po]])
                                if d == 0:
                                    self.act(o_[:, hc], self.ps[:, po, :], AF.Identity, [self.psr[po]] + rqo, ro_,
                                             scale=qo[:, dh:dh + 1])
                                else:
                                    self.stt("dve", o_[:, hc], self.ps[:, po, :], qo[:, dh:dh + 1], of_[:, hc],
                                             ALU.mult, ALU.add, [self.psr[po]] + rqo + rof_, ro_)
                            for a in range(2):
                                a8 = 2 * hh + a
                                pr = 4 + a8 % 2
                                self.mm(self.ps[:, pr, :], km[:, hh * 256 + a * 128:hh * 256 + (a + 1) * 128], v_[:, hc],
                                        True, True, rkm + rv_, [self.psr[pr]])
                                self.stt("dve", R[:, a8, :], R[:, a8, :], g128[:, dh:dh + 1], self.ps[:, pr, :],
                                         ALU.mult, ALU.add, [rR[a8], self.psr[pr]] + rg128, [rR[a8]])
                                self.copy("act", Rb[:, a8, :], R[:, a8, :], [rR[a8]], [rRb[a8]])
                        if not do_out:
                            continue
                        if d == 0:
                            self.ld("sp", self.ROF_d[:, c, :], o_[:, :], ro_, self.dr("ROF_d", t0, t0 + 128), owner=ro_[0])
                            continue
                        for hh in range(4):
                            hc = slice(hh * 512, (hh + 1) * 512)
                            self.P.op("dve", (lambda e, o=st[:, hh, :], i_=o_[:, hc]: e.bn_stats(out=o, in_=i_)), ro_, rst)
                            self.P.op("dve", (lambda e, o=mv[:, hh, :], i_=st[:, hh, :]: e.bn_aggr(out=o, in_=i_)), rst, rmv)
                        self.act(rs4[:, :], mv[:, :, 1], AF.Sqrt, rmv + self.r_eps, rrs4, bias=self.eps_t[:, 0:1], scale=1.0)
                        self.recip(rs4[:, :], rs4[:, :], rrs4, rrs4)
                        self.stt("dve", nmr[:, :], mv[:, :, 0], -1.0, rs4[:, :], ALU.mult, ALU.mult, rmv + rrs4, rnmr)
                        for hh in range(4):
                            hc = slice(hh * 512, (hh + 1) * 512)
                            y_, ry_ = yn[hh % 2]
                            self.act(y_[:, :], o_[:, hc], AF.Identity, ro_ + rrs4 + rnmr, ry_, bias=nmr[:, hh:hh + 1],
                                     scale=rs4[:, hh:hh + 1])
                            self.tt("dve", z[:, hc], y_[:, :], gnwb[:, hc], ALU.mult, ry_ + rgnwb, [rz[hh]])
                            self.tt("pool", z[:, hc], z[:, hc], gc_[:, hc], ALU.mult, [rz[hh]] + rgc_, [rz[hh]])
                        sub = c % 4
                        for q4 in range(4):
                            pt = 6
                            for f4 in range(4):
                                fc = q4 * 4 + f4
                                self.transpose(self.ps[:, pt, f4 * 128:(f4 + 1) * 128], z[:, fc * 128:(fc + 1) * 128],
                                               self.ident[:, :], [rz[q4]] + self.r_ident, [self.psr[pt]])
                            self.copy("act", zT[:, q4 * 4:(q4 + 1) * 4, sub * 128:(sub + 1) * 128],
                                      self.ps[:, pt, :].rearrange("p (f t) -> p f t", f=4), [self.psr[pt]], rzT)
                        if sub == 0:
                            tau = c // 4
                            t0t = tau * 512
                            self.ld("sp", xt2[:, :, :], X_in[:, :, t0t:t0t + 512], self.dr(kin, t0t, t0t + 512), rx2)
                            for m in range(8):
                                py = 7
                                for k in range(16):
                                    self.mm(self.ps[:, py, :], wout[:, k, m * 128:(m + 1) * 128], zT[:, k, :], k == 0, k == 15,
                                            rwout + rzT, [self.psr[py]])
                                self.stt("dve", ot2[:, m, :], self.ps[:, py, :], g1[:, m:m + 1], xt2[:, m, :],
                                         ALU.mult, ALU.add, [self.psr[py]] + self.r_mod + rx2, ro2)
                            self.ld("sp", X_mid[:, :, t0t:t0t + 512], ot2[:, :, :], ro2, self.dr(kmid, t0t, t0t + 512),
                                    owner=ro2[0])
                self.phase_barrier()

    def ffn(self, i, col, X_mid, kmid, X_out, kout, Stok):
        NT = min(1024, Stok)
        ntile = Stok // NT
        with contextlib.ExitStack() as es:
            wdn, rwdn = self.sb(es, "f_wdn", [128, 22, 1024], BF16, 22)
            wups = [self.sb(es, f"f_wup{k}", [128, 8, 256], BF16) for k in range(3)]
            h2, rh2 = self.sb(es, "f_h2", [128, 8, NT], BF16)
            sq, rsq = self.sb(es, "f_sq", [128, 8, 512], BF16)
            rstd, rrstd = self.sb(es, "f_rstd", [128, 512], F32)
            tmps = [self.sb(es, f"f_t{k}", [128, 512], F32) for k in range(2)]
            U = [[self.sb(es, f"f_U{hf}{k}", [128, NT + 3], F32) for k in range(1)] for hf in range(2)]
            C = [[self.sb(es, f"f_C{hf}{k}", [128, NT + 1], F32) for k in range(2)] for hf in range(2)]
            actb, ract = self.sb(es, "f_act", [128, 22, NT + 1], BF16, 22)
            carry, rcarry = self.sb(es, "f_carry", [128, 44, 2], F32, 44)
            xs, rxs = self.sb(es, "f_xs", [128, 8, NT + 1], F32, 1)
            cw, rcw = self.sb(es, "f_cw", [128, 3, 44], F32)
            cb, rcb = self.sb(es, "f_cb", [128, 44], F32)
            self.ld("sp", cw[:, :, :], self.convw[:, i, :, :], [], rcw)
            self.ld("sp", cb[:, :], self.convb[:, i, :], [], rcb)
            wdv = self.wdn[i].rearrange("(k p) m -> p k m", p=128)
            for k in range(22):
                self.ld("pool", wdn[:, k, :], wdv[:, k, :], [], [rwdn[k]])
            G = self.Gt[:, i, col, 1, :]
            SH = self.mod(i, col, 3)
            g2 = self.mod(i, col, 5)
            nw = 0
            for tau in range(ntile):
                t0 = tau * NT
                last = tau == ntile - 1
                nout = NT + (1 if last else 0)
                xt, rx = xs, rxs
                self.ld("sp", xt[:, :, 0:NT], X_mid[:, :, t0:t0 + NT], self.dr(kmid, t0, t0 + NT), rx)
                self.norm_tile(xt, rx, NT, G, SH, h2, rh2, sq, rsq, rstd, rrstd,
                               [t[0] for t in tmps], [t[1] for t in tmps], 7)
                for j in range(22):
                    w, rw = wups[nw % 3]
                    nw += 1
                    self.ld("pool", w[:, :, :], self.wup[i, j], [], rw)
                    for hf in range(2):
                        ch = j + 22 * hf
                        pbase = 2 * hf
                        for (a, b) in colsplit(NT):
                            pb = pbase + a // 512
                            for k in range(8):
                                self.mm(self.ps[:, pb, 0:b - a], w[:, k, hf * 128:(hf + 1) * 128], h2[:, k, a:b],
                                        k == 0, k == 7, rw + rh2, [self.psr[pb]])
                        Ut, rU = U[hf][0]
                        Ct, rC = C[hf][j % 2]
                        for (a, b) in colsplit(NT):
                            pb = pbase + a // 512
                            self.copy("act", Ut[:, 2 + a:2 + b], self.ps[:, pb, 0:b - a], [self.psr[pb]], rU)
                        if tau == 0:
                            self.memset("pool", Ut[:, 0:2], 0.0, rU)
                        else:
                            self.copy("pool", Ut[:, 0:2], carry[:, ch, :], [rcarry[ch]], rU)
                        if last:
                            self.memset("pool", Ut[:, NT + 2:NT + 3], 0.0, rU)
                        else:
                            self.copy("pool", carry[:, ch, :], Ut[:, NT:NT + 2], rU, [rcarry[ch]])
                        self.act(Ct[:, 0:nout], Ut[:, 0:nout], AF.Identity, rU + rcw + rcb, rC,
                                 bias=cb[:, ch:ch + 1], scale=cw[:, 0, ch:ch + 1])
                        self.stt("dve", Ct[:, 0:nout], Ut[:, 1:nout + 1], cw[:, 1, ch:ch + 1], Ct[:, 0:nout],
                                 ALU.mult, ALU.add, rU + rcw + rC, rC)
                        self.stt("dve", Ct[:, 0:nout], Ut[:, 2:nout + 2], cw[:, 2, ch:ch + 1], Ct[:, 0:nout],
                                 ALU.mult, ALU.add, rU + rcw + rC, rC)
                    Ca, rCa = C[0][j % 2]
                    Cv, rCv = C[1][j % 2]
                    self.act(Ca[:, 0:nout], Ca[:, 0:nout], AF.Silu, rCa, rCa)
                    self.tt("pool", actb[:, j, 0:nout], Ca[:, 0:nout], Cv[:, 0:nout], ALU.mult, rCa + rCv, [ract[j]])
                lo = 1 if tau == 0 else 0
                self.ld("sp", xs[:, :, lo:nout], X_mid[:, :, t0 - 1 + lo:t0 - 1 + nout],
                        self.dr(kmid, t0 - 1 + lo, t0 - 1 + nout), rxs)
                for m in range(8):
                    pbase = 4 if m % 2 == 0 else 0
                    segs = colsplit(nout)
                    for si, (a, b) in enumerate(segs):
                        pb = pbase + si
                        for k in range(22):
                            self.mm(self.ps[:, pb, 0:b - a], wdn[:, k, m * 128:(m + 1) * 128], actb[:, k, a:b],
                                    k == 0, k == 21, [rwdn[k], ract[k]], [self.psr[pb]])
                    for si, (a, b) in enumerate(segs):
                        pb = pbase + si
                        self.stt("dve", xs[:, m, a:b], self.ps[:, pb, 0:b - a], g2[:, m:m + 1], xs[:, m, a:b],
                                 ALU.mult, ALU.add, [self.psr[pb]] + self.r_mod + rxs, rxs)
                self.ld("sp", X_out[:, :, t0 - 1 + lo:t0 - 1 + nout], xs[:, :, lo:nout], rxs,
                        self.dr(kout, t0 - 1 + lo, t0 - 1 + nout), owner=rxs[0])


def _fm(a):
    n = a.shape[-1] // 128
    b = a.reshape(a.shape[:-1] + (n, 128))
    return np.ascontiguousarray(np.moveaxis(b, -1, 0))


def pool_inv_table(S):
    t = np.arange(S)
    out = np.zeros((4, S), np.float32)
    for g, win in enumerate((2, 4, 8, 16)):
        lo = np.clip(t - win // 2, 0, S)
        hi = np.clip(t + win // 2, 0, S)
        out[g] = 1.0 / (hi - lo).astype(np.float32)
    return out


def rope_tables(S):
    rows = S // 64
    row = np.repeat(np.arange(rows), 64).astype(np.float32)
    colp = np.tile(np.arange(64), rows).astype(np.float32)
    inv = (np.float32(10000.0) ** (-np.arange(0, 64, 2, dtype=np.float32) / np.float32(64))).astype(np.float32)
    ar = (row[:, None] * inv[None, :]).astype(np.float32)
    ac = (colp[:, None] * inv[None, :]).astype(np.float32)
    C = np.concatenate([np.cos(ar), np.cos(ar), np.cos(ac), np.cos(ac)], axis=1)
    Sg = np.concatenate([-np.sin(ar), np.sin(ar), -np.sin(ac), np.sin(ac)], axis=1)
    return np.ascontiguousarray(C.T.astype(np.float32)), np.ascontiguousarray(Sg.T.astype(np.float32))


def make_in_maps(inputs, S, ncores):
    f = lambda a: np.ascontiguousarray(np.asarray(a, dtype=np.float32))
    x, c, ctx, c_ctx = f(inputs["x"]), f(inputs["c"]), f(inputs["ctx"]), f(inputs["c_ctx"])
    B = x.shape[0]
    ada_b = _fm(f(inputs["ada_b"]))
    norm_w = _fm(f(inputs["norm_w"]))
    pool_bs = np.ascontiguousarray(np.stack([_fm(f(inputs["pool_b"])), _fm(f(inputs["pool_scale"]))], axis=2))
    wu = f(inputs["ffn_w_up"])
    a = wu[:, :, :D_FF].reshape(4, 8, 128, 22, 128)
    v = wu[:, :, D_FF:].reshape(4, 8, 128, 22, 128)
    wup = np.ascontiguousarray(np.concatenate([a, v], axis=-1).transpose(0, 3, 2, 1, 4))
    convw = np.ascontiguousarray(_fm(f(inputs["ffn_conv_w"])))
    convb = np.ascontiguousarray(_fm(f(inputs["ffn_conv_b"])))
    common = {
        "ada_w": f(inputs["ada_w"]), "ada_b": ada_b, "norm_w": norm_w,
        "pool_w": f(inputs["pool_w"]), "pool_bs": pool_bs,
        "pool_inv": pool_inv_table(S), "pool_invc": pool_inv_table(NCTX),
        "wup": wup, "convw": convw, "convb": convb, "wdn": f(inputs["ffn_w_down"]),
        "ident": np.eye(128, dtype=np.float32),
    }
    perm = np.arange(128)
    perm = np.where(perm % 64 < 32, perm + 32, perm - 32)
    wqkv = f(inputs["attn_w_qkv"])[0]
    cols = np.concatenate([hh * 128 + perm for hh in range(10)])
    qg, kg = f(inputs["attn_q_gain"])[0], f(inputs["attn_k_gain"])[0]
    common.update({
        "wqkv": wqkv, "wqksw": np.ascontiguousarray(wqkv[:, cols]), "wo": f(inputs["attn_w_o"])[0],
        "qkgain": np.ascontiguousarray(np.stack([qg, qg[perm], kg, kg[perm]], axis=1)),
    })
    p = np.arange(128, dtype=np.float32)
    jj = np.arange(128)
    rmask = np.stack([(jj[:, None] <= jj[None, :]), (jj[:, None] >= jj[None, :])], axis=1).astype(np.float32)
    common.update({
        "ret_win": f(inputs["ret_w_in"])[0], "ret_dl": f(inputs["ret_decay_logit"])[0].reshape(1, 8),
        "ret_gnw": f(inputs["ret_gn_w"])[0].reshape(1, 2048), "ret_wout": f(inputs["ret_w_out"])[0],
        "rpos": np.ascontiguousarray(np.stack([p + 1, 128 - p, 127 - p, p], axis=1)),
        "rneg": np.ascontiguousarray(np.stack([-(p + 1), p - 128], axis=0)),
        "rmask": np.ascontiguousarray(rmask),
    })
    rC, rS = rope_tables(S)
    common.update({"ropeC": rC, "ropeS": rS})
    maps = []
    for core in range(ncores):
        b = core % B
        m = dict(common)
        m["xT"] = np.ascontiguousarray(x[b].T.reshape(8, 128, S).transpose(1, 0, 2))
        m["ctxT"] = np.ascontiguousarray(ctx[b].T.reshape(8, 128, NCTX).transpose(1, 0, 2))
        cc = np.stack([c[b], c_ctx], axis=-1)
        m["cc"] = np.ascontiguousarray(cc.reshape(8, 128, 2).transpose(1, 0, 2))
        maps.append(m)
    return maps


_CACHE = {}


def run(inputs, S, ncores, NL=4, dbg=None):
    key = (S, NL, dbg)
    if key not in _CACHE:
        _CACHE[key] = KB(S, NL, dbg).build()
    nc = _CACHE[key]
    maps = make_in_maps(inputs, S, ncores)
    res = run_bass_kernel_spmd(nc, maps, core_ids=list(range(ncores)))
    global LAST_RES
    LAST_RES = res.results
    outs = []
    for r in res.results:
        o = r["outT"]
        outs.append(np.ascontiguousarray(o.transpose(2, 1, 0).reshape(S, 1024)))
    return outs


def kernel(**inputs):
    x = np.asarray(inputs["x"])
    B, S, _ = x.shape
    outs = run(inputs, S, 8)
    return np.stack(outs[:B], axis=0).astype(np.float32)
```

```python
import contextlib
import numpy as np
import concourse.bass as bass
import concourse.mybir as mybir
from concourse.bass_utils import run_bass_kernel_spmd

F32 = mybir.dt.float32
BF16 = mybir.dt.bfloat16
AF = mybir.ActivationFunctionType
ALU = mybir.AluOpType

ENGS = ("pe", "act", "dve", "pool", "sp")
D_FF = 2816
NCTX = 256


class Res:
    __slots__ = ("name", "last_w", "readers", "dsem", "dcount")

    def __init__(self, name):
        self.name = name
        self.last_w = None
        self.readers = []
        self.dsem = None
        self.dcount = 0


class Op:
    __slots__ = ("eng", "fn", "deps", "signal", "count", "dma", "sem", "idx")

    def __init__(self, eng, fn, dma):
        self.eng = eng
        self.fn = fn
        self.deps = []
        self.signal = False
        self.count = 0
        self.dma = dma
        self.sem = None
        self.idx = 0


class Prog:
    def __init__(self, nc):
        self.nc = nc
        self.es = contextlib.ExitStack()
        self.streams = {e: [] for e in ENGS}
        self.eng_sem = {}
        self.dma_res = []
        self.nres = 0
        self.sem_pool = []
        self.last = {e: None for e in ENGS}

    def sem(self, name):
        return self.es.enter_context(self.nc.semaphore(name))

    def res(self, name=None):
        self.nres += 1
        return Res(name or f"r{self.nres}")

    def _deps(self, op, reads, writes):
        deps = []
        for r in reads:
            if r.last_w is not None:
                deps.append(r.last_w)
        for r in writes:
            if r.last_w is not None:
                deps.append(r.last_w)
            deps.extend(r.readers)
        best = {}
        for d in deps:
            if d is op:
                continue
            if d.dma:
                k = ("d", d.sem.num)
                if k not in best or best[k].count < d.count:
                    best[k] = d
            else:
                if d.eng == op.eng and d.eng == "pe" and not op.dma:
                    continue
                k = ("e", d.eng)
                if k not in best or best[k].idx < d.idx:
                    best[k] = d
        for d in best.values():
            d.signal = True
            op.deps.append(d)
        for r in reads:
            r.readers.append(op)
        for r in writes:
            r.last_w = op
            r.readers = []

    def op(self, eng, fn, reads=(), writes=()):
        o = Op(eng, fn, False)
        o.idx = len(self.streams[eng])
        self._deps(o, reads, writes)
        self.streams[eng].append(o)
        self.last[eng] = o
        return o

    def dma(self, eng, out, in_, reads=(), writes=(), owner=None):
        o = Op(eng, None, True)
        if owner is None:
            owner = (list(writes) + list(reads))[0]
        if owner.dsem is None:
            if self.sem_pool:
                owner.dsem, owner.dcount = self.sem_pool.pop()
            else:
                owner.dsem = self.sem("d_" + owner.name)
                owner.dcount = 0
            self.dma_res.append(owner)
        owner.dcount += 16
        o.sem = owner.dsem
        o.count = owner.dcount
        o.fn = lambda e, out=out, in_=in_: e.dma_start(out=out, in_=in_)
        o.idx = len(self.streams[eng])
        self._deps(o, reads, writes)
        self.streams[eng].append(o)
        return o

    def barrier(self, dummies):
        lasts = [self.last[e] for e in ENGS if self.last[e] is not None]
        dmas = []
        for r in self.dma_res:
            o = Op("sp", None, True)
            o.sem, o.count = r.dsem, r.dcount
            dmas.append(o)
        for e in ENGS:
            fn = dummies.get(e)
            o = Op(e, fn, False)
            o.idx = len(self.streams[e])
            for d in lasts:
                if d.eng == e:
                    continue
                d.signal = True
                o.deps.append(d)
            o.deps.extend(dmas)
            self.streams[e].append(o)
            if fn is not None:
                self.last[e] = o
        for r in self.dma_res:
            self.sem_pool.append((r.dsem, r.dcount))
            r.dsem = None
            r.dcount = 0
        self.dma_res = []

    def emit(self):
        nc = self.nc
        for e in ENGS:
            self.eng_sem[e] = self.sem("e_" + e)
        for e in ENGS:
            c = 0
            for o in self.streams[e]:
                if o.dma or o.fn is None:
                    continue
                if o.signal:
                    c += 1
                    o.count = c
                    o.sem = self.eng_sem[e]
        with nc.Block() as block:
            def run(e, eng):
                waited = {}
                for o in self.streams[e]:
                    for d in o.deps:
                        s, c = d.sem, d.count
                        if waited.get(s.num, 0) < c:
                            eng.wait_ge(s, c)
                            waited[s.num] = c
                    if o.fn is None:
                        continue
                    ins = o.fn(eng)
                    if o.dma:
                        ins.then_inc(o.sem, 16)
                    elif o.signal:
                        ins.then_inc(o.sem, 1)
                if e == "sp":
                    for r in self.dma_res:
                        if waited.get(r.dsem.num, 0) < r.dcount:
                            eng.wait_ge(r.dsem, r.dcount)
                    for e2 in ENGS:
                        if e2 == e:
                            continue
                        n = sum(1 for o in self.streams[e2] if (not o.dma) and o.fn is not None and o.signal)
                        if n and waited.get(self.eng_sem[e2].num, 0) < n:
                            eng.wait_ge(self.eng_sem[e2], n)

            @block.tensor
            def _(eng):
                run("pe", eng)

            @block.scalar
            def _(eng):
                run("act", eng)

            @block.vector
            def _(eng):
                run("dve", eng)

            @block.gpsimd
            def _(eng):
                run("pool", eng)

            @block.sync
            def _(eng):
                run("sp", eng)


def colsplit(n, m=512):
    return [(a, min(a + m, n)) for a in range(0, n, m)]


class KB:
    def __init__(self, S, NL=4, dbg=None):
        self.dbg = dbg
        self.S = S
        self.NL = NL
        self.nc = bass.Bass("TRN2", target_bir_lowering=False)
        self.P = Prog(self.nc)
        self.dres = {}

    def mm(self, out, lhsT, rhs, start, stop, reads, writes):
        return self.P.op("pe", lambda e: e.matmul(out, lhsT=lhsT, rhs=rhs, start=start, stop=stop), reads, writes)

    def transpose(self, out, in_, ident, reads, writes):
        return self.P.op("pe", lambda e: e.transpose(out, in_, ident), reads, writes)

    def act(self, out, in_, func, reads, writes, bias=None, scale=1.0):
        if bias is None:
            return self.P.op("act", lambda e: e.activation(out=out, in_=in_, func=func, scale=scale), reads, writes)
        return self.P.op("act", lambda e: e.activation(out=out, in_=in_, func=func, bias=bias, scale=scale), reads, writes)

    def tt(self, eng, out, in0, in1, op, reads, writes):
        return self.P.op(eng, lambda e: e.tensor_tensor(out=out, in0=in0, in1=in1, op=op), reads, writes)

    def ts(self, eng, out, in0, s1, s2, op0, op1, reads, writes):
        return self.P.op(eng, lambda e: e.tensor_scalar(out=out, in0=in0, scalar1=s1, scalar2=s2, op0=op0, op1=op1), reads, writes)

    def tsm(self, eng, out, in0, s1, reads, writes):
        return self.P.op(eng, lambda e: e.tensor_scalar_mul(out, in0, s1), reads, writes)

    def stt(self, eng, out, in0, scalar, in1, op0, op1, reads, writes):
        return self.P.op(eng, lambda e: e.scalar_tensor_tensor(out=out, in0=in0, scalar=scalar, in1=in1, op0=op0, op1=op1), reads, writes)

    def copy(self, eng, out, in_, reads, writes):
        if eng == "act":
            return self.P.op("act", lambda e: e.copy(out=out, in_=in_), reads, writes)
        return self.P.op(eng, lambda e: e.tensor_copy(out=out, in_=in_), reads, writes)

    def memset(self, eng, ap, val, writes):
        return self.P.op(eng, lambda e: e.memset(ap, val), (), writes)

    def recip(self, out, in_, reads, writes):
        return self.P.op("dve", lambda e: e.reciprocal(out=out, in_=in_), reads, writes)

    def ld(self, q, out, in_, reads, writes, owner=None):
        return self.P.dma(q, out, in_, reads, writes, owner)

    def inp(self, name, shape, dt=F32):
        return self.nc.dram_tensor(name, list(shape), dt, kind="ExternalInput").ap()

    def dram(self, name, shape, dt=F32):
        return self.nc.dram_tensor(name, list(shape), dt, kind="Internal").ap()

    def sb(self, es, name, shape, dt, nres=1):
        self.nsb = getattr(self, "nsb", 0) + 1
        name = f"s{self.nsb}_{name}"
        t = es.enter_context(self.nc.sbuf_tensor(name, list(shape), dt))
        rs = [self.P.res(f"{name}{i}") for i in range(nres)]
        return t, rs

    def dr(self, key, lo, hi, g=512):
        out = []
        for b in range(lo // g, (hi - 1) // g + 1):
            k = (key, b)
            if k not in self.dres:
                self.dres[k] = self.P.res(f"{key}_{b}")
            out.append(self.dres[k])
        return out

    def dump(self, name, ap, shape, dt, reads):
        if not self.dbg:
            return
        self.ndump = getattr(self, "ndump", 0) + 1
        t = self.nc.dram_tensor(f"dbg_{name}", list(shape), dt, kind="ExternalOutput").ap()
        self.P.dma("sp", t, ap, reads, [], owner=reads[0])

    def phase_barrier(self):
        bs = self.bar_s
        self.P.barrier({
            "act": lambda e: e.copy(out=bs[:, 0:1], in_=bs[:, 4:5]),
            "dve": lambda e: e.memset(bs[:, 1:2], 0.0),
            "pool": lambda e: e.memset(bs[:, 2:3], 0.0),
            "pe": None, "sp": None,
        })

    def build(self):
        nc, P, S = self.nc, self.P, self.S
        es0 = P.es
        I = self.inp
        self.xT = I("xT", [128, 8, S])
        self.ctxT = I("ctxT", [128, 8, NCTX])
        self.cc = I("cc", [128, 8, 2])
        self.ada_w = I("ada_w", [4, 1024, 6144])
        self.ada_b = I("ada_b", [128, 4, 48])
        self.norm_w = I("norm_w", [128, 4, 2, 8])
        self.pool_w = I("pool_w", [2, 4, 256, 256])
        self.pool_bs = I("pool_bs", [128, 2, 2, 8])
        self.pool_inv = I("pool_inv", [4, S])
        self.pool_invc = I("pool_invc", [4, NCTX])
        self.wup = I("wup", [4, 22, 128, 8, 256])
        self.convw = I("convw", [128, 4, 3, 44])
        self.convb = I("convb", [128, 4, 44])
        self.wdn = I("wdn", [4, D_FF, 1024])
        self.ident_in = I("ident", [128, 128])
        self.wqkv = I("wqkv", [1024, 1536])
        self.wqksw = I("wqksw", [1024, 1280])
        self.wo = I("wo", [1024, 1024])
        self.qkgain = I("qkgain", [128, 4])
        self.ropeC = I("ropeC", [128, S])
        self.ropeS = I("ropeS", [128, S])
        self.QT_d = self.dram("QT_d", [128, 8, S], BF16)
        NCH = S // 128
        self.ret_win = I("ret_win", [1024, 6144])
        self.ret_dl = I("ret_dl", [1, 8])
        self.ret_gnw = I("ret_gnw", [1, 2048])
        self.ret_wout = I("ret_wout", [2048, 1024])
        self.rpos = I("rpos", [128, 4])
        self.rneg = I("rneg", [2, 128])
        self.rmask = I("rmask", [128, 2, 128])
        self.RQ_d = self.dram("RQ_d", [128, 8, S], BF16)
        self.RKT_d = self.dram("RKT_d", [128, 2, 8, S], BF16)
        self.RKtm_d = self.dram("RKtm_d", [128, 2, NCH, 1024], BF16)
        self.RV_d = self.dram("RV_d", [128, NCH, 2048], BF16)
        self.RG_d = self.dram("RG_d", [128, NCH, 2048], BF16)
        self.ROF_d = self.dram("ROF_d", [128, NCH, 2048], F32)
        self.RKtmc_d = self.dram("RKtmc_d", [128, 2, 2, 1024], BF16)
        self.RVc_d = self.dram("RVc_d", [128, 2, 2048], BF16)
        self.outT = nc.dram_tensor("outT", [128, 8, S], F32, kind="ExternalOutput").ap()
        self.xA = self.dram("xA", [128, 8, S])
        self.xB = self.dram("xB", [128, 8, S])
        self.cA = self.dram("cA", [128, 8, NCTX])
        self.cB = self.dram("cB", [128, 8, NCTX])

        self.ps = es0.enter_context(nc.psum_tensor("ps", [128, 8, 512], F32))
        self.psr = [P.res(f"psb{i}") for i in range(8)]
        self.bar_s, _ = self.sb(es0, "bar_s", [128, 8], F32)
        self.ones_bf, self.r_ones = self.sb(es0, "ones_bf", [128, 128], BF16)
        self.eps_t, self.r_eps = self.sb(es0, "eps_t", [128, 1], F32)
        self.ident, self.r_ident = self.sb(es0, "ident", [128, 128], F32)
        self.modT, self.r_mod = self.sb(es0, "modT", [128, 4, 2, 48], F32)
        self.Gt, self.r_G = self.sb(es0, "Gt", [128, 4, 2, 2, 8], F32)
        self.nw_s, self.r_nw = self.sb(es0, "nw_s", [128, 4, 2, 8], F32)
        self.memset("dve", self.ones_bf[:, :], 1.0, self.r_ones)
        self.memset("dve", self.eps_t[:, :], 1e-6, self.r_eps)
        self.one_t, self.r_one = self.sb(es0, "one_t", [128, 1], F32)
        self.memset("dve", self.one_t[:, :], 1.0, self.r_one)
        self.memset("dve", self.bar_s[:, :], 0.0, [])
        self.ld("sp", self.ident[:, :], self.ident_in, [], self.r_ident)
        self.ld("sp", self.nw_s[:, :, :, :], self.norm_w, [], self.r_nw)

        self.modulation()
        self.phase_barrier()
        xin, cin = self.xT, self.ctxT
        kin, kcin = "xT", "ctxT"
        for i in range(self.NL):
            kind = i % 3
            need_ctx_out = any(k % 3 != 0 for k in range(i + 1, 4))
            lastl = (i == self.NL - 1)
            xout = self.outT if lastl else self.xB
            kout = "outT" if lastl else "xB"
            xmid, kmid_ = self.xA, "xA"
            if lastl and self.dbg == "mix":
                xmid, kmid_ = self.outT, "outT"
            if kind == 0:
                self.pool_mixer(i, i // 3, 0, xin, kin, xmid, kmid_, S, self.pool_inv)
                if need_ctx_out:
                    self.phase_barrier()
                    self.pool_mixer(i, i // 3, 1, cin, kcin, self.cA, "cA", NCTX, self.pool_invc)
            elif kind == 1:
                self.attention(i, xin, kin, cin, kcin, xmid, kmid_, self.cA, "cA", need_ctx_out)
            else:
                self.retention(i, xin, kin, cin, kcin, xmid, kmid_)
            self.phase_barrier()
            if lastl and self.dbg == "mix":
                break
            self.ffn(i, 0, self.xA, "xA", xout, kout, S)
            if need_ctx_out:
                self.phase_barrier()
                self.ffn(i, 1, self.cA, "cA", self.cB, "cB", NCTX)
            self.phase_barrier()
            xin, kin = self.xB, "xB"
            cin, kcin = self.cB, "cB"
        P.emit()
        P.es.close()
        return nc

    def modulation(self):
        nc, P = self.nc, self.P
        with contextlib.ExitStack() as es:
            ccs, r_cc = self.sb(es, "ccs", [128, 8, 2], F32)
            scc, r_scc = self.sb(es, "scc", [128, 8, 2], F32)
            adab, r_adab = self.sb(es, "adab", [128, 4, 48], F32)
            wbuf = [self.sb(es, f"adaw{k}", [128, 6144], F32) for k in range(2)]
            self.ld("sp", ccs[:, :, :], self.cc, [], r_cc)
            self.ld("sp", adab[:, :, :], self.ada_b, [], r_adab)
            self.act(scc[:, :, :], ccs[:, :, :], AF.Silu, r_cc, r_scc)
            n = 0
            macc, r_macc = self.sb(es, "macc", [128, 96], F32)
            for i in range(4):
                for k in range(8):
                    wt, rw = wbuf[n % 2]
                    pb = n % 2
                    n += 1
                    self.ld("sp", wt[:, :], self.ada_w[i, k * 128:(k + 1) * 128, :], [], rw)
                    for m in range(48):
                        self.mm(self.ps[:, pb, 2 * m:2 * m + 2], wt[:, m * 128:(m + 1) * 128], scc[:, k, :],
                                True, True, rw + r_scc, [self.psr[pb]])
                    if k == 0:
                        self.copy("dve", macc[:, :], self.ps[:, pb, 0:96], [self.psr[pb]], r_macc)
                    else:
                        self.tt("dve", macc[:, :], macc[:, :], self.ps[:, pb, 0:96], ALU.add, [self.psr[pb]] + r_macc, r_macc)
                rps = r_macc
                for col in range(2):
                    pv = macc[:, :].rearrange("p (m c) -> p m c", c=2)[:, :, col]
                    self.tt("dve", self.modT[:, i, col, :], pv, adab[:, i, :], ALU.add, rps + r_adab, self.r_mod)
                    for wn in range(2):
                        sc = self.modT[:, i, col, 8 + 24 * wn:16 + 24 * wn]
                        self.stt("dve", self.Gt[:, i, col, wn, :], sc, 1.0, self.nw_s[:, i, wn, :], ALU.add, ALU.mult,
                                 self.r_mod + self.r_nw, self.r_G)

    def mod(self, i, col, which):
        return self.modT[:, i, col, which * 8:(which + 1) * 8]

    def norm_tile(self, xt, rx, n, G, SH, out, rout, sq, rsq, rstd, rrstd, tmp, rtmp, psb):
        for (a, b) in colsplit(n):
            w = b - a
            self.act(sq[:, :, 0:w], xt[:, :, a:b], AF.Square, rx, rsq)
            for c in range(8):
                self.mm(self.ps[:, psb, 0:w], self.ones_bf[:, :], sq[:, c, 0:w], c == 0, c == 7,
                        rsq + self.r_ones, [self.psr[psb]])
            self.act(rstd[:, 0:w], self.ps[:, psb, 0:w], AF.Sqrt, [self.psr[psb]] + self.r_eps, rrstd,
                     bias=self.eps_t[:, 0:1], scale=1.0 / 1024.0)
            self.recip(rstd[:, 0:w], rstd[:, 0:w], rrstd, rrstd)
            for c in range(8):
                t, rt = tmp[c % 2], rtmp[c % 2]
                self.stt("dve", t[:, 0:w], xt[:, c, a:b], G[:, c:c + 1], rstd[:, 0:w], ALU.mult, ALU.mult,
                         rx + rrstd + self.r_G, rt)
                self.act(out[:, c, a:b], t[:, 0:w], AF.Identity, rt + self.r_mod, rout, bias=SH[:, c:c + 1], scale=1.0)

    def pool_mixer(self, i, jp, col, X_in, kin, X_mid, kmid, Stok, invtab):
        NT = min(512, Stok)
        W = NT + 16
        ntile = Stok // NT
        with contextlib.ExitStack() as es:
            xts = [self.sb(es, f"pm_x{k}", [128, 8, W], F32) for k in range(2)]
            h, rh = self.sb(es, "pm_h", [128, 8, W], F32)
            sq, rsq = self.sb(es, "pm_sq", [128, 8, 512], BF16)
            rstd, rrstd = self.sb(es, "pm_rstd", [128, 512], F32)
            tmps = [self.sb(es, f"pm_t{k}", [128, 512], F32) for k in range(2)]
            wa = [self.sb(es, f"pm_wa{k}", [128, W], F32) for k in range(2)]
            wb = [self.sb(es, f"pm_wb{k}", [128, W], F32) for k in range(2)]
            pooled, rpooled = self.sb(es, "pm_pooled", [128, 8, NT], BF16, 8)
            inv, rinv = self.sb(es, "pm_inv", [128, 4, NT], F32)
            pw, rpw = self.sb(es, "pm_w", [128, 4, 2, 256], BF16)
            pbs, rpbs = self.sb(es, "pm_bs", [128, 2, 8], F32)
            AB, rAB = self.sb(es, "pm_AB", [128, 2, 8], F32)
            ot = [self.sb(es, f"pm_o{k}", [128, 8, NT], F32) for k in range(2)]
            yt = [self.sb(es, f"pm_y{k}", [128, NT], F32) for k in range(2)]
            self.ld("pool", pw[:, :, :, :], self.pool_w[jp].rearrange("g (k p) n -> p g k n", p=128), [], rpw)
            self.ld("sp", pbs[:, :, :], self.pool_bs[:, jp, :, :], [], rpbs)
            g1 = self.mod(i, col, 2)
            self.tt("dve", AB[:, 0, :], g1, pbs[:, 1, :], ALU.mult, self.r_mod + rpbs, rAB)
            self.tt("dve", AB[:, 1, :], AB[:, 0, :], pbs[:, 0, :], ALU.mult, rAB + rpbs, rAB)
            G = self.Gt[:, i, col, 0, :]
            SH = self.mod(i, col, 0)
            for tau in range(ntile):
                t0 = tau * NT
                lo, hi = max(t0 - 8, 0), min(t0 + NT + 8, Stok)
                xt, rx = xts[tau % 2]
                if lo > t0 - 8:
                    self.memset("pool", xt[:, :, 0:8], 0.0, rx)
                if hi < t0 + NT + 8:
                    self.memset("pool", xt[:, :, NT + 8:W], 0.0, rx)
                self.ld("sp", xt[:, :, lo - (t0 - 8):hi - (t0 - 8)], X_in[:, :, lo:hi], self.dr(kin, lo, hi), rx)
                self.ld("sp", inv[:, :, :], invtab[:, t0:t0 + NT].partition_broadcast(128), [], rinv)
                self.norm_tile(xt, rx, W, G, SH, h, rh, sq, rsq, rstd, rrstd,
                               [t[0] for t in tmps], [t[1] for t in tmps], 7)
                if lo > t0 - 8:
                    self.memset("dve", h[:, :, 0:8], 0.0, rh)
                if hi < t0 + NT + 8:
                    self.memset("dve", h[:, :, NT + 8:W], 0.0, rh)
                for c in range(8):
                    g = c // 2
                    A_, rA = wa[c % 2]
                    B_, rB = wb[c % 2]
                    hc = h[:, c, :]
                    e = "dve"
                    if g == 0:
                        self.tt(e, A_[:, 8:8 + NT], hc[:, 7:7 + NT], hc[:, 8:8 + NT], ALU.add, rh, rA)
                        Sv, rS = A_, rA
                    else:
                        self.tt(e, A_[:, 0:W - 1], hc[:, 0:W - 1], hc[:, 1:W], ALU.add, rh, rA)
                        if g == 1:
                            self.tt(e, B_[:, 8:8 + NT], A_[:, 6:6 + NT], A_[:, 8:8 + NT], ALU.add, rA, rB)
                            Sv, rS = B_, rB
                        else:
                            self.tt(e, B_[:, 0:W - 3], A_[:, 0:W - 3], A_[:, 2:W - 1], ALU.add, rA, rB)
                            if g == 2:
                                self.tt(e, A_[:, 8:8 + NT], B_[:, 4:4 + NT], B_[:, 8:8 + NT], ALU.add, rB, rA)
                                Sv, rS = A_, rA
                            else:
                                self.tt(e, A_[:, 0:W - 7], B_[:, 0:W - 7], B_[:, 4:W - 3], ALU.add, rB, rA)
                                self.tt(e, B_[:, 8:8 + NT], A_[:, 0:NT], A_[:, 8:8 + NT], ALU.add, rA, rB)
                                Sv, rS = B_, rB
                    self.tt("pool", Sv[:, 8:8 + NT], Sv[:, 8:8 + NT], inv[:, g, :], ALU.mult, rS + rinv, rS)
                    self.tt("pool", pooled[:, c, :], Sv[:, 8:8 + NT], hc[:, 8:8 + NT], ALU.subtract, rS + rh,
                            [rpooled[c]])
                o_, ro = ot[tau % 2]
                for c in range(8):
                    g, mo = c // 2, c % 2
                    pb = c % 2
                    for ki in range(2):
                        self.mm(self.ps[:, pb, 0:NT], pw[:, g, ki, mo * 128:(mo + 1) * 128], pooled[:, 2 * g + ki, :],
                                ki == 0, ki == 1, rpw + [rpooled[2 * g + ki]], [self.psr[pb]])
                    y_, ry = yt[c % 2]
                    self.act(y_[:, :], self.ps[:, pb, 0:NT], AF.Identity, [self.psr[pb]] + rAB, ry,
                             bias=AB[:, 1, c:c + 1], scale=AB[:, 0, c:c + 1])
                    self.tt("dve", o_[:, c, :], y_[:, :], xt[:, c, 8:8 + NT], ALU.add, ry + rx, ro)
                self.ld("sp", X_mid[:, :, t0:t0 + NT], o_[:, :, :], ro, self.dr(kmid, t0, t0 + NT), owner=ro[0])


    def attention(self, i, X_in, kin, C_in, kcin, X_mid, kmid, C_mid, kcmid, need_ctx_out):
        S = self.S
        NKL = S // 128
        NK = NKL + NCTX // 128
        scale = 128.0 ** -0.5
        with contextlib.ExitStack() as es0:
            KT, rKT = self.sb(es0, "a_KT", [128, 2, S + NCTX], BF16, 1)
            V, rV = self.sb(es0, "a_V", [128, NK, 256], BF16, 1)
            QTc, rQTc = self.sb(es0, "a_QTc", [128, 8, NCTX], BF16)
            gn, rgn = self.sb(es0, "a_gn", [128, 4], F32)
            self.ld("sp", gn[:, :], self.qkgain, [], rgn)
            with contextlib.ExitStack() as es:
                wq, rwq = self.sb(es, "a_wq", [128, 8, 1536], BF16)
                ws, rws = self.sb(es, "a_ws", [128, 8, 1280], BF16)
                for k in range(8):
                    self.ld("pool", wq[:, k, :], self.wqkv[k * 128:(k + 1) * 128, :], [], rwq)
                    self.ld("pool", ws[:, k, :], self.wqksw[k * 128:(k + 1) * 128, :], [], rws)
                xt, rx = self.sb(es, "a_x", [128, 8, 512], F32)
                h, rh = self.sb(es, "a_h", [128, 8, 512], BF16)
                sq, rsq = self.sb(es, "a_sq", [128, 8, 512], BF16)
                rstd, rrstd = self.sb(es, "a_rstd", [128, 512], F32)
                tmps = [self.sb(es, f"a_t{k}", [128, 512], F32) for k in range(2)]
                Ct, rCt = self.sb(es, "a_C", [128, 512], F32)
                St, rSt = self.sb(es, "a_S", [128, 512], F32)
                sqq = [self.sb(es, f"a_sqq{k}", [128, 512], BF16) for k in range(2)]
                rq = [self.sb(es, f"a_rq{k}", [128, 512], F32) for k in range(2)]
                t1 = [self.sb(es, f"a_t1{k}", [128, 512], F32) for k in range(2)]
                t2 = [self.sb(es, f"a_t2{k}", [128, 512], F32) for k in range(2)]
                qout = [self.sb(es, f"a_qo{k}", [128, 8, 512], BF16) for k in range(2)]

                def proj(col, Xsrc, ksrc, t0, n, koff, vch0, rope, qdst, rqdst):
                    self.ld("sp", xt[:, :, 0:n], Xsrc[:, :, t0:t0 + n], self.dr(ksrc, t0, t0 + n), rx)
                    self.norm_tile(xt, rx, n, self.Gt[:, i, col, 0, :], self.mod(i, col, 0), h, rh, sq, rsq,
                                   rstd, rrstd, [t[0] for t in tmps], [t[1] for t in tmps], 7)
                    if rope:
                        self.ld("sp", Ct[:, 0:n], self.ropeC[:, t0:t0 + n], [], rCt)
                        self.ld("sp", St[:, 0:n], self.ropeS[:, t0:t0 + n], [], rSt)
                    for hc in range(10):
                        isq = hc < 8
                        pa, pbk = hc % 2, 2 + hc % 2
                        c0 = hc * 128
                        for k in range(8):
                            self.mm(self.ps[:, pa, 0:n], wq[:, k, c0:c0 + 128], h[:, k, 0:n], k == 0, k == 7,
                                    rwq + rh, [self.psr[pa]])
                        if rope:
                            for k in range(8):
                                self.mm(self.ps[:, pbk, 0:n], ws[:, k, c0:c0 + 128], h[:, k, 0:n], k == 0, k == 7,
                                        rws + rh, [self.psr[pbk]])
                        sq_, rsq_ = sqq[hc % 2]
                        rq_, rrq_ = rq[hc % 2]
                        self.act(sq_[:, 0:n], self.ps[:, pa, 0:n], AF.Square, [self.psr[pa]], rsq_)
                        self.mm(self.ps[:, 4 + hc % 2, 0:n], self.ones_bf[:, :], sq_[:, 0:n], True, True,
                                rsq_ + self.r_ones, [self.psr[4 + hc % 2]])
                        self.act(rq_[:, 0:n], self.ps[:, 4 + hc % 2, 0:n], AF.Sqrt, [self.psr[4 + hc % 2]] + self.r_eps,
                                 rrq_, bias=self.eps_t[:, 0:1], scale=1.0 / 128.0)
                        self.recip(rq_[:, 0:n], rq_[:, 0:n], rrq_, rrq_)
                        g0 = 0 if isq else 2
                        if isq:
                            dst, rdst = qdst[:, hc, 0:n], rqdst
                        else:
                            dst, rdst = KT[:, hc - 8, koff:koff + n], rKT
                        if rope:
                            a_, ra_ = t1[hc % 2]
                            b_, rb_ = t2[hc % 2]
                            self.stt("dve", a_[:, 0:n], self.ps[:, pa, 0:n], gn[:, g0:g0 + 1], Ct[:, 0:n],
                                     ALU.mult, ALU.mult, [self.psr[pa]] + rgn + rCt, ra_)
                            self.stt("dve", b_[:, 0:n], self.ps[:, pbk, 0:n], gn[:, g0 + 1:g0 + 2], St[:, 0:n],
                                     ALU.mult, ALU.mult, [self.psr[pbk]] + rgn + rSt, rb_)
                            self.tt("pool", a_[:, 0:n], a_[:, 0:n], b_[:, 0:n], ALU.add, ra_ + rb_, ra_)
                            self.tt("pool", dst, a_[:, 0:n], rq_[:, 0:n], ALU.mult, ra_ + rrq_, rdst)
                        else:
                            self.stt("dve", dst, self.ps[:, pa, 0:n], gn[:, g0:g0 + 1], rq_[:, 0:n],
                                     ALU.mult, ALU.mult, [self.psr[pa]] + rgn + rrq_, rdst)
                    for s_ in range(n // 128):
                        pv = 6
                        for k in range(8):
                            self.mm(self.ps[:, pv, 0:256], h[:, k, s_ * 128:(s_ + 1) * 128], wq[:, k, 1280:1536],
                                    k == 0, k == 7, rh + rwq, [self.psr[pv]])
                        self.copy("act", V[:, vch0 + s_, :], self.ps[:, pv, 0:256], [self.psr[pv]], rV)

                proj(1, C_in, kcin, 0, NCTX, S, NKL, False, QTc, rQTc)
                for tau in range(S // 512):
                    qo_, rqo_ = qout[tau % 2]
                    proj(0, X_in, kin, tau * 512, 512, tau * 512, tau * 4, True, qo_, rqo_)
                    self.ld("sp", self.QT_d[:, :, tau * 512:(tau + 1) * 512], qo_[:, :, :], rqo_,
                            self.dr("QT_d", tau * 512, (tau + 1) * 512), owner=rqo_[0])
            self.phase_barrier()
            with contextlib.ExitStack() as es:
                wo, rwo = self.sb(es, "a_wo", [128, 8, 1024], BF16)
                for k in range(8):
                    self.ld("pool", wo[:, k, :], self.wo[k * 128:(k + 1) * 128, :], [], rwo)
                QTt = [self.sb(es, f"a_QT{k}", [128, 8, 512], BF16) for k in range(2)]
                pbuf = [self.sb(es, f"a_p{k}", [128, 512], BF16) for k in range(6)]
                acc = [self.sb(es, f"a_acc{k}", [128, 512], F32) for k in range(3)]
                accb, raccb = self.sb(es, "a_accb", [128, 512], BF16)
                rden, rrden = self.sb(es, "a_rden", [128, 512], F32)
                ao, rao = self.sb(es, "a_ao", [128, 8, 512], BF16, 8)
                xts = [self.sb(es, f"a_xx{k}", [128, 8, 512], F32) for k in range(2)]
                ots = [self.sb(es, f"a_oo{k}", [128, 8, 512], F32) for k in range(2)]
                g1s = [self.mod(i, 0, 2), self.mod(i, 1, 2)]

                def core(col, Q, rQ, n, kcs, Xsrc, ksrc, Xdst, kdst, t0, par):
                    xt_, rx_ = xts[par]
                    o_, ro_ = ots[par]
                    self.ld("sp", xt_[:, :, 0:n], Xsrc[:, :, t0:t0 + n], self.dr(ksrc, t0, t0 + n), rx_)
                    for hh in range(8):
                        kv = hh // 4
                        po = 4 + hh % 2
                        nk = len(kcs)
                        LA = 3

                        def qk(idx):
                            kc = kcs[idx]
                            self.mm(self.ps[:, idx % 4, 0:n], KT[:, kv, kc * 128:(kc + 1) * 128], Q[:, hh, 0:n], True, True,
                                    rKT + rQ, [self.psr[idx % 4]])
                        for idx in range(min(LA, nk)):
                            qk(idx)
                        for idx, kc in enumerate(kcs):
                            pss = idx % 4
                            p_, rp_ = pbuf[idx % 6]
                            self.act(p_[:, 0:n], self.ps[:, pss, 0:n], AF.Exp, [self.psr[pss]], rp_, scale=scale)
                            a_, ra_ = acc[idx % 3]
                            ae = "pool" if idx % 3 == 2 else "dve"
                            if idx < 3:
                                self.copy(ae, a_[:, 0:n], p_[:, 0:n], rp_, ra_)
                            else:
                                self.tt(ae, a_[:, 0:n], a_[:, 0:n], p_[:, 0:n], ALU.add, ra_ + rp_, ra_)
                            if idx + LA < nk:
                                qk(idx + LA)
                            self.mm(self.ps[:, po, 0:n], V[:, kc, kv * 128:(kv + 1) * 128], p_[:, 0:n], idx == 0,
                                    idx == nk - 1, rV + rp_, [self.psr[po]])
                        if nk >= 3:
                            self.tt("dve", acc[0][0][:, 0:n], acc[0][0][:, 0:n], acc[2][0][:, 0:n], ALU.add,
                                    acc[0][1] + acc[2][1], acc[0][1])
                        self.tt("dve", accb[:, 0:n], acc[0][0][:, 0:n], acc[1][0][:, 0:n], ALU.add,
                                acc[0][1] + acc[1][1], raccb)
                        pd = 6
                        self.mm(self.ps[:, pd, 0:n], self.ones_bf[:, :], accb[:, 0:n], True, True, raccb + self.r_ones,
                                [self.psr[pd]])
                        self.recip(rden[:, 0:n], self.ps[:, pd, 0:n], [self.psr[pd]], rrden)
                        self.tt("dve", ao[:, hh, 0:n], self.ps[:, po, 0:n], rden[:, 0:n], ALU.mult,
                                [self.psr[po]] + rrden, [rao[hh]])
                    for m in range(8):
                        py = 6 + m % 2
                        for k in range(8):
                            self.mm(self.ps[:, py, 0:n], wo[:, k, m * 128:(m + 1) * 128], ao[:, k, 0:n], k == 0, k == 7,
                                    rwo + [rao[k]], [self.psr[py]])
                        self.stt("dve", o_[:, m, 0:n], self.ps[:, py, 0:n], g1s[col][:, m:m + 1], xt_[:, m, 0:n],
                                 ALU.mult, ALU.add, [self.psr[py]] + self.r_mod + rx_, ro_)
                    self.ld("sp", Xdst[:, :, t0:t0 + n], o_[:, :, 0:n], ro_, self.dr(kdst, t0, t0 + n), owner=ro_[0])

                self.dump("KT", KT[:, :, :], [128, 2, S + NCTX], BF16, rKT)
                self.dump("V", V[:, :, :], [128, NK, 256], BF16, rV)
                self.dump("QTc", QTc[:, :, :], [128, 8, NCTX], BF16, rQTc)
                if need_ctx_out:
                    core(1, QTc, rQTc, NCTX, [NKL, NKL + 1], C_in, kcin, C_mid, kcmid, 0, 0)
                self.dump("ao_c", ao[:, :, :], [128, 8, 512], BF16, rao)
                self.dump("rden_c", rden[:, :], [128, 512], F32, rrden)
                self.dump("p_c", pbuf[1][0][:, :], [128, 512], BF16, pbuf[1][1])
                allk = list(range(NK))
                for tau in range(S // 512):
                    Q_, rQ_ = QTt[tau % 2]
                    self.ld("sp", Q_[:, :, :], self.QT_d[:, :, tau * 512:(tau + 1) * 512],
                            self.dr("QT_d", tau * 512, (tau + 1) * 512), rQ_)
                    core(0, Q_, rQ_, 512, allk, X_in, kin, X_mid, kmid, tau * 512, tau % 2)


    def retention(self, i, X_in, kin, C_in, kcin, X_mid, kmid):
        S = self.S
        NCH = S // 128
        with contextlib.ExitStack() as es0:
            lg, rlg = self.sb(es0, "r_lg", [128, 8], F32)
            g128, rg128 = self.sb(es0, "r_g128", [128, 8], F32)
            qo, rqo = self.sb(es0, "r_qo", [128, 8], F32)
            kts, rkts = self.sb(es0, "r_kts", [128, 8], F32)
            kTtab, rkTtab = self.sb(es0, "r_kTtab", [128, 8, 128], F32)
            mask, rmask = self.sb(es0, "r_mask", [128, 2, 128], F32)
            rposs, rrpos = self.sb(es0, "r_pos", [128, 4], F32)
            rnegb, rrneg = self.sb(es0, "r_negb", [128, 2, 128], F32)
            dlt, rdl = self.sb(es0, "r_dl", [128, 8], F32)
            self.ld("sp", dlt[:, :], self.ret_dl.partition_broadcast(128), [], rdl)
            self.ld("sp", rposs[:, :], self.rpos, [], rrpos)
            self.ld("sp", rnegb[:, :, :], self.rneg.partition_broadcast(128), [], rrneg)
            self.ld("sp", mask[:, :, :], self.rmask, [], rmask)
            self.act(lg[:, :], dlt[:, :], AF.Exp, rdl, rlg, scale=-1.0)
            self.act(lg[:, :], lg[:, :], AF.Ln, rlg + self.r_one, rlg, bias=self.one_t[:, 0:1], scale=1.0)
            self.tsm("dve", lg[:, :], lg[:, :], -1.0, rlg, rlg)
            self.act(g128[:, :], lg[:, :], AF.Exp, rlg, rg128, scale=128.0)
            for d in range(2):
                self.act(qo[:, 4 * d:4 * d + 4], lg[:, 4 * d:4 * d + 4], AF.Exp, rlg + rrpos, rqo, scale=rposs[:, d:d + 1])
                self.act(kts[:, 4 * d:4 * d + 4], lg[:, 4 * d:4 * d + 4], AF.Exp, rlg + rrpos, rkts,
                         scale=rposs[:, 2 + d:3 + d])
                for hh in range(4):
                    dh = 4 * d + hh
                    self.act(kTtab[:, dh, :], rnegb[:, d, :], AF.Exp, rrneg + rlg, rkTtab, scale=lg[:, dh:dh + 1])
            self.tsm("dve", kts[:, :], kts[:, :], 1.0 / 16.0, rkts, rkts)
            self.tsm("dve", kTtab[:, :, :], kTtab[:, :, :], 1.0 / 16.0, rkTtab, rkTtab)

            with contextlib.ExitStack() as es:
                win, rwin = self.sb(es, "r_win", [128, 8, 6144], BF16)
                for k in range(8):
                    for q4 in range(4):
                        self.ld("pool", win[:, k, q4 * 1536:(q4 + 1) * 1536],
                                self.ret_win[k * 128:(k + 1) * 128, q4 * 1536:(q4 + 1) * 1536], [], rwin)
                xt, rx = self.sb(es, "r_x", [128, 8, 512], F32)
                h, rh = self.sb(es, "r_h", [128, 8, 512], BF16)
                sq, rsq = self.sb(es, "r_sq", [128, 8, 512], BF16)
                rstd, rrstd = self.sb(es, "r_rstd", [128, 512], F32)
                tmps = [self.sb(es, f"r_t{k}", [128, 512], F32) for k in range(2)]
                QTt, rQTt = self.sb(es, "r_QTt", [128, 8, 512], BF16)
                KTt, rKTt = self.sb(es, "r_KTt", [128, 2, 8, 512], BF16)
                Ktm = [self.sb(es, f"r_Ktm{k}", [128, 2, 1024], BF16) for k in range(2)]
                Vt = [self.sb(es, f"r_Vt{k}", [128, 2048], BF16) for k in range(2)]
                Gt_ = [self.sb(es, f"r_Gt{k}", [128, 2048], BF16) for k in range(2)]

                def rproj(col, Xsrc, ksrc, t0, n, is_ctx):
                    self.ld("sp", xt[:, :, 0:n], Xsrc[:, :, t0:t0 + n], self.dr(ksrc, t0, t0 + n), rx)
                    self.norm_tile(xt, rx, n, self.Gt[:, i, col, 0, :], self.mod(i, col, 0), h, rh, sq, rsq,
                                   rstd, rrstd, [t[0] for t in tmps], [t[1] for t in tmps], 7)
                    if not is_ctx:
                        for c in range(8):
                            pb = c % 2
                            for k in range(8):
                                self.mm(self.ps[:, pb, 0:n], win[:, k, c * 128:(c + 1) * 128], h[:, k, 0:n], k == 0, k == 7,
                                        rwin + rh, [self.psr[pb]])
                            self.copy("act", QTt[:, c, 0:n], self.ps[:, pb, 0:n], [self.psr[pb]], rQTt)
                        self.ld("sp", self.RQ_d[:, :, t0:t0 + n], QTt[:, :, 0:n], rQTt, self.dr("RQ_d", t0, t0 + n),
                                owner=rQTt[0])
                        for c in range(8):
                            pb = 2 + c % 2
                            hh = c // 2
                            for k in range(8):
                                self.mm(self.ps[:, pb, 0:n], win[:, k, 1024 + c * 128:1024 + (c + 1) * 128], h[:, k, 0:n],
                                        k == 0, k == 7, rwin + rh, [self.psr[pb]])
                            for d in range(2):
                                for s_ in range(n // 128):
                                    self.tt("dve", KTt[:, d, c, s_ * 128:(s_ + 1) * 128],
                                            self.ps[:, pb, s_ * 128:(s_ + 1) * 128], kTtab[:, 4 * d + hh, :], ALU.mult,
                                            [self.psr[pb]] + rkTtab, rKTt)
                        for d in range(2):
                            self.ld("sp", self.RKT_d[:, d, :, t0:t0 + n], KTt[:, d, :, 0:n], rKTt,
                                    self.dr("RKT_d", t0, t0 + n), owner=rKTt[0])
                    for s_ in range(n // 128):
                        ch = (t0 + s_ * 128) // 128
                        par = s_ % 2
                        km, rkm = Ktm[par]
                        v_, rv_ = Vt[par]
                        g_, rg_ = Gt_[par]
                        lhs = lambda k: h[:, k, s_ * 128:(s_ + 1) * 128]
                        for grp in range(2):
                            pb = 4 + grp
                            for k in range(8):
                                self.mm(self.ps[:, pb, :], lhs(k), win[:, k, 1024 + grp * 512:1024 + (grp + 1) * 512],
                                        k == 0, k == 7, rh + rwin, [self.psr[pb]])
                            for hl in range(2):
                                hh = grp * 2 + hl
                                for d in range(2):
                                    self.act(km[:, d, hh * 256:(hh + 1) * 256], self.ps[:, pb, hl * 256:(hl + 1) * 256],
                                             AF.Identity, [self.psr[pb]] + rkts, rkm, scale=kts[:, 4 * d + hh:4 * d + hh + 1])
                        for grp in range(4):
                            pb = 6 + grp % 2
                            for k in range(8):
                                self.mm(self.ps[:, pb, :], lhs(k), win[:, k, 2048 + grp * 512:2048 + (grp + 1) * 512],
                                        k == 0, k == 7, rh + rwin, [self.psr[pb]])
                            self.copy("dve", v_[:, grp * 512:(grp + 1) * 512], self.ps[:, pb, :], [self.psr[pb]], rv_)
                        if is_ctx:
                            self.ld("sp", self.RKtmc_d[:, :, ch, :], km[:, :, :], rkm, self.dr("RKtmc_d", 0, 1), owner=rkm[0])
                            self.ld("sp", self.RVc_d[:, ch, :], v_[:, :], rv_, self.dr("RVc_d", 0, 1), owner=rv_[0])
                            continue
                        self.ld("sp", self.RKtm_d[:, :, ch, :], km[:, :, :], rkm, self.dr("RKtm_d", t0, t0 + n), owner=rkm[0])
                        self.ld("sp", self.RV_d[:, ch, :], v_[:, :], rv_, self.dr("RV_d", t0, t0 + n), owner=rv_[0])
                        for grp in range(4):
                            pb = 6 + grp % 2
                            for k in range(8):
                                self.mm(self.ps[:, pb, :], lhs(k), win[:, k, 4096 + grp * 512:4096 + (grp + 1) * 512],
                                        k == 0, k == 7, rh + rwin, [self.psr[pb]])
                            self.act(g_[:, grp * 512:(grp + 1) * 512], self.ps[:, pb, :], AF.Silu, [self.psr[pb]], rg_)
                        self.ld("sp", self.RG_d[:, ch, :], g_[:, :], rg_, self.dr("RG_d", t0, t0 + n), owner=rg_[0])

                rproj(1, C_in, kcin, 0, NCTX, True)
                for tau in range(S // 512):
                    rproj(0, X_in, kin, tau * 512, 512, False)
            self.phase_barrier()

            for d in range(2):
                with contextlib.ExitStack() as es:
                    R, rR = self.sb(es, "r_R", [128, 8, 512], F32, 8)
                    Rb, rRb = self.sb(es, "r_Rb", [128, 8, 512], BF16, 8)
                    QTc = [self.sb(es, f"r_QTc{k}", [128, 8, 128], BF16) for k in range(2)]
                    KTc = [self.sb(es, f"r_KTc{k}", [128, 8, 128], BF16) for k in range(2)]
                    Kmc = [self.sb(es, f"r_Kmc{k}", [128, 1024], BF16) for k in range(2)]
                    Vc = [self.sb(es, f"r_Vc{k}", [128, 2048], BF16) for k in range(2)]
                    AT = [self.sb(es, f"r_AT{k}", [128, 128], BF16) for k in range(2)]
                    osb = [self.sb(es, f"r_osb{k}", [128, 2048], F32) for k in range(2)]
                    if d == 1:
                        OFc = [self.sb(es, f"r_OFc{k}", [128, 2048], F32) for k in range(2)]
                        Gc = [self.sb(es, f"r_Gc{k}", [128, 2048], BF16) for k in range(2)]
                        st, rst = self.sb(es, "r_st", [128, 4, 6], F32)
                        mv, rmv = self.sb(es, "r_mv", [128, 4, 2], F32)
                        rs4, rrs4 = self.sb(es, "r_rs4", [128, 4], F32)
                        nmr, rnmr = self.sb(es, "r_nmr", [128, 4], F32)
                        yn = [self.sb(es, f"r_yn{k}", [128, 512], F32) for k in range(2)]
                        z, rz = self.sb(es, "r_z", [128, 2048], F32, 4)
                        zT, rzT = self.sb(es, "r_zT", [128, 16, 512], BF16)
                        wout, rwout = self.sb(es, "r_wout", [128, 16, 1024], BF16)
                        gnwb, rgnwb = self.sb(es, "r_gnwb", [128, 2048], F32)
                        xt2, rx2 = self.sb(es, "r_x2", [128, 8, 512], F32)
                        ot2, ro2 = self.sb(es, "r_o2", [128, 8, 512], F32)
                        for k in range(16):
                            self.ld("pool", wout[:, k, :], self.ret_wout[k * 128:(k + 1) * 128, :], [], rwout)
                        self.ld("sp", gnwb[:, :], self.ret_gnw.partition_broadcast(128), [], rgnwb)
                    g1 = self.mod(i, 0, 2)
                    for a8 in range(8):
                        self.memset("dve", R[:, a8, :], 0.0, [rR[a8]])
                        self.memset("pool", Rb[:, a8, :], 0.0, [rRb[a8]])
                    steps = []
                    cord = [0, 1] if d == 0 else [1, 0]
                    for cc in cord:
                        steps.append((True, cc))
                    lord = list(range(NCH)) if d == 0 else list(range(NCH - 1, -1, -1))
                    for c in lord:
                        steps.append((False, c))
                    for si, (is_ctx, c) in enumerate(steps):
                        par = si % 2
                        do_out = not is_ctx
                        km, rkm = Kmc[par]
                        v_, rv_ = Vc[par]
                        if is_ctx:
                            self.ld("sp", km[:, :], self.RKtmc_d[:, d, c, :], self.dr("RKtmc_d", 0, 1), rkm)
                            self.ld("sp", v_[:, :], self.RVc_d[:, c, :], self.dr("RVc_d", 0, 1), rv_)
                        else:
                            t0 = c * 128
                            self.ld("sp", km[:, :], self.RKtm_d[:, d, c, :], self.dr("RKtm_d", t0, t0 + 128), rkm)
                            self.ld("sp", v_[:, :], self.RV_d[:, c, :], self.dr("RV_d", t0, t0 + 128), rv_)
                            q_, rq_ = QTc[par]
                            kt_, rkt_ = KTc[par]
                            self.ld("sp", q_[:, :, :], self.RQ_d[:, :, t0:t0 + 128], self.dr("RQ_d", t0, t0 + 128), rq_)
                            self.ld("sp", kt_[:, :, :], self.RKT_d[:, d, :, t0:t0 + 128], self.dr("RKT_d", t0, t0 + 128), rkt_)
                            o_, ro_ = osb[par]
                            if d == 1:
                                of_, rof_ = OFc[par]
                                gc_, rgc_ = Gc[par]
                                self.ld("sp", of_[:, :], self.ROF_d[:, c, :], self.dr("ROF_d", t0, t0 + 128), rof_)
                                self.ld("sp", gc_[:, :], self.RG_d[:, c, :], self.dr("RG_d", t0, t0 + 128), rgc_)
                        for hh in range(4):
                            dh = 4 * d + hh
                            hc = slice(hh * 512, (hh + 1) * 512)
                            if do_out:
                                pa = hh % 2
                                for a in range(2):
                                    self.mm(self.ps[:, pa, 0:128], kt_[:, 2 * hh + a, :], q_[:, 2 * hh + a, :], a == 0, a == 1,
                                            rkt_ + rq_, [self.psr[pa]])
                            for a in range(2):
                                a8 = 2 * hh + a
                                pr = 4 + a8 % 2
                                self.mm(self.ps[:, pr, :], km[:, hh * 256 + a * 128:hh * 256 + (a + 1) * 128], v_[:, hc],
                                        True, True, rkm + rv_, [self.psr[pr]])
                            if do_out:
                                at_, rat_ = AT[hh % 2]
                                self.tt("dve", at_[:, :], self.ps[:, pa, 0:128], mask[:, d, :], ALU.mult,
                                        [self.psr[pa]] + rmask, rat_)
                                po = 2 + hh % 2
                                self.mm(self.ps[:, po, :], at_[:, :], v_[:, hc], True, False, rat_ + rv_, [self.psr[po]])
                                for a in range(2):
                                    self.mm(self.ps[:, po, :], q_[:, 2 * hh + a, :], Rb[:, 2 * hh + a, :], False, a == 1,
                                            rq_ + [rRb[2 * hh + a]], [self.psr[po]])
                                if d == 0:
                                    self.act(o_[:, hc], self.ps[:, po, :], AF.Identity, [self.psr[po]] + rqo, ro_,
                                             scale=qo[:, dh:dh + 1])
                                else:
                                    self.stt("dve", o_[:, hc], self.ps[:, po, :], qo[:, dh:dh + 1], of_[:, hc],
                                             ALU.mult, ALU.add, [self.psr[po]] + rqo + rof_, ro_)
                            for a in range(2):
                                a8 = 2 * hh + a
                                pr = 4 + a8 % 2
                                self.stt("dve", R[:, a8, :], R[:, a8, :], g128[:, dh:dh + 1], self.ps[:, pr, :],
                                         ALU.mult, ALU.add, [rR[a8], self.psr[pr]] + rg128, [rR[a8]])
                                self.copy("act", Rb[:, a8, :], R[:, a8, :], [rR[a8]], [rRb[a8]])
                        if not do_out:
                            continue
                        if d == 0:
                            self.ld("sp", self.ROF_d[:, c, :], o_[:, :], ro_, self.dr("ROF_d", t0, t0 + 128), owner=ro_[0])
                            continue
                        for hh in range(4):
                            hc = slice(hh * 512, (hh + 1) * 512)
                            self.P.op("dve", (lambda e, o=st[:, hh, :], i_=o_[:, hc]: e.bn_stats(out=o, in_=i_)), ro_, rst)
                            self.P.op("dve", (lambda e, o=mv[:, hh, :], i_=st[:, hh, :]: e.bn_aggr(out=o, in_=i_)), rst, rmv)
                        self.act(rs4[:, :], mv[:, :, 1], AF.Sqrt, rmv + self.r_eps, rrs4, bias=self.eps_t[:, 0:1], scale=1.0)
                        self.recip(rs4[:, :], rs4[:, :], rrs4, rrs4)
                        self.stt("dve", nmr[:, :], mv[:, :, 0], -1.0, rs4[:, :], ALU.mult, ALU.mult, rmv + rrs4, rnmr)
                        for hh in range(4):
                            hc = slice(hh * 512, (hh + 1) * 512)
                            y_, ry_ = yn[hh % 2]
                            self.act(y_[:, :], o_[:, hc], AF.Identity, ro_ + rrs4 + rnmr, ry_, bias=nmr[:, hh:hh + 1],
                                     scale=rs4[:, hh:hh + 1])
                            self.tt("dve", z[:, hc], y_[:, :], gnwb[:, hc], ALU.mult, ry_ + rgnwb, [rz[hh]])
                            self.tt("pool", z[:, hc], z[:, hc], gc_[:, hc], ALU.mult, [rz[hh]] + rgc_, [rz[hh]])
                        sub = c % 4
                        for q4 in range(4):
                            pt = 6
                            for f4 in range(4):
                                fc = q4 * 4 + f4
                                self.transpose(self.ps[:, pt, f4 * 128:(f4 + 1) * 128], z[:, fc * 128:(fc + 1) * 128],
                                               self.ident[:, :], [rz[q4]] + self.r_ident, [self.psr[pt]])
                            self.copy("act", zT[:, q4 * 4:(q4 + 1) * 4, sub * 128:(sub + 1) * 128],
                                      self.ps[:, pt, :].rearrange("p (f t) -> p f t", f=4), [self.psr[pt]], rzT)
                        if sub == 0:
                            tau = c // 4
                            t0t = tau * 512
                            self.ld("sp", xt2[:, :, :], X_in[:, :, t0t:t0t + 512], self.dr(kin, t0t, t0t + 512), rx2)
                            for m in range(8):
                                py = 7
                                for k in range(16):
                                    self.mm(self.ps[:, py, :], wout[:, k, m * 128:(m + 1) * 128], zT[:, k, :], k == 0, k == 15,
                                            rwout + rzT, [self.psr[py]])
                                self.stt("dve", ot2[:, m, :], self.ps[:, py, :], g1[:, m:m + 1], xt2[:, m, :],
                                         ALU.mult, ALU.add, [self.psr[py]] + self.r_mod + rx2, ro2)
                            self.ld("sp", X_mid[:, :, t0t:t0t + 512], ot2[:, :, :], ro2, self.dr(kmid, t0t, t0t + 512),
                                    owner=ro2[0])
                self.phase_barrier()

    def ffn(self, i, col, X_mid, kmid, X_out, kout, Stok):
        NT = min(1024, Stok)
        ntile = Stok // NT
        with contextlib.ExitStack() as es:
            wdn, rwdn = self.sb(es, "f_wdn", [128, 22, 1024], BF16, 22)
            wups = [self.sb(es, f"f_wup{k}", [128, 8, 256], BF16) for k in range(3)]
            h2, rh2 = self.sb(es, "f_h2", [128, 8, NT], BF16)
            sq, rsq = self.sb(es, "f_sq", [128, 8, 512], BF16)
            rstd, rrstd = self.sb(es, "f_rstd", [128, 512], F32)
            tmps = [self.sb(es, f"f_t{k}", [128, 512], F32) for k in range(2)]
            U = [[self.sb(es, f"f_U{hf}{k}", [128, NT + 3], F32) for k in range(1)] for hf in range(2)]
            C = [[self.sb(es, f"f_C{hf}{k}", [128, NT + 1], F32) for k in range(2)] for hf in range(2)]
            actb, ract = self.sb(es, "f_act", [128, 22, NT + 1], BF16, 22)
            carry, rcarry = self.sb(es, "f_carry", [128, 44, 2], F32, 44)
            xs, rxs = self.sb(es, "f_xs", [128, 8, NT + 1], F32, 1)
            cw, rcw = self.sb(es, "f_cw", [128, 3, 44], F32)
            cb, rcb = self.sb(es, "f_cb", [128, 44], F32)
            self.ld("sp", cw[:, :, :], self.convw[:, i, :, :], [], rcw)
            self.ld("sp", cb[:, :], self.convb[:, i, :], [], rcb)
            wdv = self.wdn[i].rearrange("(k p) m -> p k m", p=128)
            for k in range(22):
                self.ld("pool", wdn[:, k, :], wdv[:, k, :], [], [rwdn[k]])
            G = self.Gt[:, i, col, 1, :]
            SH = self.mod(i, col, 3)
            g2 = self.mod(i, col, 5)
            nw = 0
            npairs = ntile * 22

            def wload(n):
                if n < npairs:
                    w_, rw_ = wups[n % 3]
                    self.ld("pool", w_[:, :, :], self.wup[i, n % 22], [], rw_)
            wload(0)
            wload(1)
            for tau in range(ntile):
                t0 = tau * NT
                last = tau == ntile - 1
                nout = NT + (1 if last else 0)
                xt, rx = xs, rxs
                self.ld("sp", xt[:, :, 0:NT], X_mid[:, :, t0:t0 + NT], self.dr(kmid, t0, t0 + NT), rx)
                self.norm_tile(xt, rx, NT, G, SH, h2, rh2, sq, rsq, rstd, rrstd,
                               [t[0] for t in tmps], [t[1] for t in tmps], 7)
                for j in range(22):
                    w, rw = wups[nw % 3]
                    wload(nw + 2)
                    nw += 1
                    for hf in range(2):
                        ch = j + 22 * hf
                        pbase = 2 * hf + 4 * (j % 2)
                        for (a, b) in colsplit(NT):
                            pb = pbase + a // 512
                            for k in range(8):
                                self.mm(self.ps[:, pb, 0:b - a], w[:, k, hf * 128:(hf + 1) * 128], h2[:, k, a:b],
                                        k == 0, k == 7, rw + rh2, [self.psr[pb]])
                        Ut, rU = U[hf][0]
                        Ct, rC = C[hf][j % 2]
                        for (a, b) in colsplit(NT):
                            pb = pbase + a // 512
                            self.copy("act", Ut[:, 2 + a:2 + b], self.ps[:, pb, 0:b - a], [self.psr[pb]], rU)
                        if tau == 0:
                            self.memset("dve", Ut[:, 0:2], 0.0, rU)
                        else:
                            self.copy("dve", Ut[:, 0:2], carry[:, ch, :], [rcarry[ch]], rU)
                        if last:
                            self.memset("dve", Ut[:, NT + 2:NT + 3], 0.0, rU)
                        else:
                            self.copy("dve", carry[:, ch, :], Ut[:, NT:NT + 2], rU, [rcarry[ch]])
                        self.act(Ct[:, 0:nout], Ut[:, 0:nout], AF.Identity, rU + rcw + rcb, rC,
                                 bias=cb[:, ch:ch + 1], scale=cw[:, 0, ch:ch + 1])
                        self.stt("dve", Ct[:, 0:nout], Ut[:, 1:nout + 1], cw[:, 1, ch:ch + 1], Ct[:, 0:nout],
                                 ALU.mult, ALU.add, rU + rcw + rC, rC)
                        self.stt("dve", Ct[:, 0:nout], Ut[:, 2:nout + 2], cw[:, 2, ch:ch + 1], Ct[:, 0:nout],
                                 ALU.mult, ALU.add, rU + rcw + rC, rC)
                    Ca, rCa = C[0][j % 2]
                    Cv, rCv = C[1][j % 2]
                    self.act(Ca[:, 0:nout], Ca[:, 0:nout], AF.Silu, rCa, rCa)
                    self.tt("pool", actb[:, j, 0:nout], Ca[:, 0:nout], Cv[:, 0:nout], ALU.mult, rCa + rCv, [ract[j]])
                lo = 1 if tau == 0 else 0
                self.ld("sp", xs[:, :, lo:nout], X_mid[:, :, t0 - 1 + lo:t0 - 1 + nout],
                        self.dr(kmid, t0 - 1 + lo, t0 - 1 + nout), rxs)
                for m in range(8):
                    pbase = 4 if m % 2 == 0 else 0
                    segs = colsplit(nout)
                    for si, (a, b) in enumerate(segs):
                        pb = pbase + si
                        for k in range(22):
                            self.mm(self.ps[:, pb, 0:b - a], wdn[:, k, m * 128:(m + 1) * 128], actb[:, k, a:b],
                                    k == 0, k == 21, [rwdn[k], ract[k]], [self.psr[pb]])
                    for si, (a, b) in enumerate(segs):
                        pb = pbase + si
                        self.stt("dve", xs[:, m, a:b], self.ps[:, pb, 0:b - a], g2[:, m:m + 1], xs[:, m, a:b],
                                 ALU.mult, ALU.add, [self.psr[pb]] + self.r_mod + rxs, rxs)
                self.ld("sp", X_out[:, :, t0 - 1 + lo:t0 - 1 + nout], xs[:, :, lo:nout], rxs,
                        self.dr(kout, t0 - 1 + lo, t0 - 1 + nout), owner=rxs[0])


def _fm(a):
    n = a.shape[-1] // 128
    b = a.reshape(a.shape[:-1] + (n, 128))
    return np.ascontiguousarray(np.moveaxis(b, -1, 0))


def pool_inv_table(S):
    t = np.arange(S)
    out = np.zeros((4, S), np.float32)
    for g, win in enumerate((2, 4, 8, 16)):
        lo = np.clip(t - win // 2, 0, S)
        hi = np.clip(t + win // 2, 0, S)
        out[g] = 1.0 / (hi - lo).astype(np.float32)
    return out


def rope_tables(S):
    rows = S // 64
    row = np.repeat(np.arange(rows), 64).astype(np.float32)
    colp = np.tile(np.arange(64), rows).astype(np.float32)
    inv = (np.float32(10000.0) ** (-np.arange(0, 64, 2, dtype=np.float32) / np.float32(64))).astype(np.float32)
    ar = (row[:, None] * inv[None, :]).astype(np.float32)
    ac = (colp[:, None] * inv[None, :]).astype(np.float32)
    C = np.concatenate([np.cos(ar), np.cos(ar), np.cos(ac), np.cos(ac)], axis=1)
    Sg = np.concatenate([-np.sin(ar), np.sin(ar), -np.sin(ac), np.sin(ac)], axis=1)
    return np.ascontiguousarray(C.T.astype(np.float32)), np.ascontiguousarray(Sg.T.astype(np.float32))


def make_in_maps(inputs, S, ncores):
    f = lambda a: np.ascontiguousarray(np.asarray(a, dtype=np.float32))
    x, c, ctx, c_ctx = f(inputs["x"]), f(inputs["c"]), f(inputs["ctx"]), f(inputs["c_ctx"])
    B = x.shape[0]
    ada_b = _fm(f(inputs["ada_b"]))
    norm_w = _fm(f(inputs["norm_w"]))
    pool_bs = np.ascontiguousarray(np.stack([_fm(f(inputs["pool_b"])), _fm(f(inputs["pool_scale"]))], axis=2))
    wu = f(inputs["ffn_w_up"])
    a = wu[:, :, :D_FF].reshape(4, 8, 128, 22, 128)
    v = wu[:, :, D_FF:].reshape(4, 8, 128, 22, 128)
    wup = np.ascontiguousarray(np.concatenate([a, v], axis=-1).transpose(0, 3, 2, 1, 4))
    convw = np.ascontiguousarray(_fm(f(inputs["ffn_conv_w"])))
    convb = np.ascontiguousarray(_fm(f(inputs["ffn_conv_b"])))
    common = {
        "ada_w": f(inputs["ada_w"]), "ada_b": ada_b, "norm_w": norm_w,
        "pool_w": f(inputs["pool_w"]), "pool_bs": pool_bs,
        "pool_inv": pool_inv_table(S), "pool_invc": pool_inv_table(NCTX),
        "wup": wup, "convw": convw, "convb": convb, "wdn": f(inputs["ffn_w_down"]),
        "ident": np.eye(128, dtype=np.float32),
    }
    perm = np.arange(128)
    perm = np.where(perm % 64 < 32, perm + 32, perm - 32)
    wqkv = f(inputs["attn_w_qkv"])[0]
    cols = np.concatenate([hh * 128 + perm for hh in range(10)])
    qg, kg = f(inputs["attn_q_gain"])[0], f(inputs["attn_k_gain"])[0]
    common.update({
        "wqkv": wqkv, "wqksw": np.ascontiguousarray(wqkv[:, cols]), "wo": f(inputs["attn_w_o"])[0],
        "qkgain": np.ascontiguousarray(np.stack([qg, qg[perm], kg, kg[perm]], axis=1)),
    })
    p = np.arange(128, dtype=np.float32)
    jj = np.arange(128)
    rmask = np.stack([(jj[:, None] <= jj[None, :]), (jj[:, None] >= jj[None, :])], axis=1).astype(np.float32)
    common.update({
        "ret_win": f(inputs["ret_w_in"])[0], "ret_dl": f(inputs["ret_decay_logit"])[0].reshape(1, 8),
        "ret_gnw": f(inputs["ret_gn_w"])[0].reshape(1, 2048), "ret_wout": f(inputs["ret_w_out"])[0],
        "rpos": np.ascontiguousarray(np.stack([p + 1, 128 - p, 127 - p, p], axis=1)),
        "rneg": np.ascontiguousarray(np.stack([-(p + 1), p - 128], axis=0)),
        "rmask": np.ascontiguousarray(rmask),
    })
    rC, rS = rope_tables(S)
    common.update({"ropeC": rC, "ropeS": rS})
    maps = []
    for core in range(ncores):
        b = core % B
        m = dict(common)
        m["xT"] = np.ascontiguousarray(x[b].T.reshape(8, 128, S).transpose(1, 0, 2))
        m["ctxT"] = np.ascontiguousarray(ctx[b].T.reshape(8, 128, NCTX).transpose(1, 0, 2))
        cc = np.stack([c[b], c_ctx], axis=-1)
        m["cc"] = np.ascontiguousarray(cc.reshape(8, 128, 2).transpose(1, 0, 2))
        maps.append(m)
    return maps


_CACHE = {}


def run(inputs, S, ncores, NL=4, dbg=None):
    key = (S, NL, dbg)
    if key not in _CACHE:
        _CACHE[key] = KB(S, NL, dbg).build()
    nc = _CACHE[key]
    maps = make_in_maps(inputs, S, ncores)
    res = run_bass_kernel_spmd(nc, maps, core_ids=list(range(ncores)))
    global LAST_RES
    LAST_RES = res.results
    outs = []
    for r in res.results:
        o = r["outT"]
        outs.append(np.ascontiguousarray(o.transpose(2, 1, 0).reshape(S, 1024)))
    return outs


def kernel(**inputs):
    x = np.asarray(inputs["x"])
    B, S, _ = x.shape
    outs = run(inputs, S, 8)
    return np.stack(outs[:B], axis=0).astype(np.float32)
```

```python
import contextlib
import numpy as np
import concourse.bass as bass
import concourse.mybir as mybir
from concourse.bass_utils import run_bass_kernel_spmd

F32 = mybir.dt.float32
BF16 = mybir.dt.bfloat16
AF = mybir.ActivationFunctionType
ALU = mybir.AluOpType

ENGS = ("pe", "act", "dve", "pool", "sp")
D_FF = 2816
NCTX = 256


class Res:
    __slots__ = ("name", "last_w", "readers", "dsem", "dcount")

    def __init__(self, name):
        self.name = name
        self.last_w = None
        self.readers = []
        self.dsem = None
        self.dcount = 0


class Op:
    __slots__ = ("eng", "fn", "deps", "signal", "count", "dma", "sem", "idx")

    def __init__(self, eng, fn, dma):
        self.eng = eng
        self.fn = fn
        self.deps = []
        self.signal = False
        self.count = 0
        self.dma = dma
        self.sem = None
        self.idx = 0


class Prog:
    def __init__(self, nc):
        self.nc = nc
        self.es = contextlib.ExitStack()
        self.streams = {e: [] for e in ENGS}
        self.eng_sem = {}
        self.dma_res = []
        self.nres = 0
        self.sem_pool = []
        self.last = {e: None for e in ENGS}

    def sem(self, name):
        return self.es.enter_context(self.nc.semaphore(name))

    def res(self, name=None):
        self.nres += 1
        return Res(name or f"r{self.nres}")

    def _deps(self, op, reads, writes):
        deps = []
        for r in reads:
            if r.last_w is not None:
                deps.append(r.last_w)
        for r in writes:
            if r.last_w is not None:
                deps.append(r.last_w)
            deps.extend(r.readers)
        best = {}
        for d in deps:
            if d is op:
                continue
            if d.dma:
                k = ("d", d.sem.num)
                if k not in best or best[k].count < d.count:
                    best[k] = d
            else:
                if d.eng == op.eng and d.eng == "pe" and not op.dma:
                    continue
                k = ("e", d.eng)
                if k not in best or best[k].idx < d.idx:
                    best[k] = d
        for d in best.values():
            d.signal = True
            op.deps.append(d)
        for r in reads:
            r.readers.append(op)
        for r in writes:
            r.last_w = op
            r.readers = []

    def op(self, eng, fn, reads=(), writes=()):
        o = Op(eng, fn, False)
        o.idx = len(self.streams[eng])
        self._deps(o, reads, writes)
        self.streams[eng].append(o)
        self.last[eng] = o
        return o

    def dma(self, eng, out, in_, reads=(), writes=(), owner=None):
        o = Op(eng, None, True)
        if owner is None:
            owner = (list(writes) + list(reads))[0]
        if owner.dsem is None:
            if self.sem_pool:
                owner.dsem, owner.dcount = self.sem_pool.pop()
            else:
                owner.dsem = self.sem("d_" + owner.name)
                owner.dcount = 0
            self.dma_res.append(owner)
        owner.dcount += 16
        o.sem = owner.dsem
        o.count = owner.dcount
        o.fn = lambda e, out=out, in_=in_: e.dma_start(out=out, in_=in_)
        o.idx = len(self.streams[eng])
        self._deps(o, reads, writes)
        self.streams[eng].append(o)
        return o

    def barrier(self, dummies):
        lasts = [self.last[e] for e in ENGS if self.last[e] is not None]
        dmas = []
        for r in self.dma_res:
            o = Op("sp", None, True)
            o.sem, o.count = r.dsem, r.dcount
            dmas.append(o)
        for e in ENGS:
            fn = dummies.get(e)
            o = Op(e, fn, False)
            o.idx = len(self.streams[e])
            for d in lasts:
                if d.eng == e:
                    continue
                d.signal = True
                o.deps.append(d)
            o.deps.extend(dmas)
            self.streams[e].append(o)
            if fn is not None:
                self.last[e] = o
        for r in self.dma_res:
            self.sem_pool.append((r.dsem, r.dcount))
            r.dsem = None
            r.dcount = 0
        self.dma_res = []

    def emit(self):
        nc = self.nc
        for e in ENGS:
            self.eng_sem[e] = self.sem("e_" + e)
        for e in ENGS:
            c = 0
            for o in self.streams[e]:
                if o.dma or o.fn is None:
                    continue
                if o.signal:
                    c += 1
                    o.count = c
                    o.sem = self.eng_sem[e]
        with nc.Block() as block:
            def run(e, eng):
                waited = {}
                for o in self.streams[e]:
                    for d in o.deps:
                        s, c = d.sem, d.count
                        if waited.get(s.num, 0) < c:
                            eng.wait_ge(s, c)
                            waited[s.num] = c
                    if o.fn is None:
                        continue
                    ins = o.fn(eng)
                    if o.dma:
                        ins.then_inc(o.sem, 16)
                    elif o.signal:
                        ins.then_inc(o.sem, 1)
                if e == "sp":
                    for r in self.dma_res:
                        if waited.get(r.dsem.num, 0) < r.dcount:
                            eng.wait_ge(r.dsem, r.dcount)
                    for e2 in ENGS:
                        if e2 == e:
                            continue
                        n = sum(1 for o in self.streams[e2] if (not o.dma) and o.fn is not None and o.signal)
                        if n and waited.get(self.eng_sem[e2].num, 0) < n:
                            eng.wait_ge(self.eng_sem[e2], n)

            @block.tensor
            def _(eng):
                run("pe", eng)

            @block.scalar
            def _(eng):
                run("act", eng)

            @block.vector
            def _(eng):
                run("dve", eng)

            @block.gpsimd
            def _(eng):
                run("pool", eng)

            @block.sync
            def _(eng):
                run("sp", eng)


def colsplit(n, m=512):
    return [(a, min(a + m, n)) for a in range(0, n, m)]


class KB:
    def __init__(self, S, NL=4, dbg=None):
        self.dbg = dbg
        self.S = S
        self.NL = NL
        self.nc = bass.Bass("TRN2", target_bir_lowering=False)
        self.P = Prog(self.nc)
        self.dres = {}

    def mm(self, out, lhsT, rhs, start, stop, reads, writes):
        return self.P.op("pe", lambda e: e.matmul(out, lhsT=lhsT, rhs=rhs, start=start, stop=stop), reads, writes)

    def transpose(self, out, in_, ident, reads, writes):
        return self.P.op("pe", lambda e: e.transpose(out, in_, ident), reads, writes)

    def act(self, out, in_, func, reads, writes, bias=None, scale=1.0):
        if bias is None:
            return self.P.op("act", lambda e: e.activation(out=out, in_=in_, func=func, scale=scale), reads, writes)
        return self.P.op("act", lambda e: e.activation(out=out, in_=in_, func=func, bias=bias, scale=scale), reads, writes)

    def tt(self, eng, out, in0, in1, op, reads, writes):
        return self.P.op(eng, lambda e: e.tensor_tensor(out=out, in0=in0, in1=in1, op=op), reads, writes)

    def ts(self, eng, out, in0, s1, s2, op0, op1, reads, writes):
        return self.P.op(eng, lambda e: e.tensor_scalar(out=out, in0=in0, scalar1=s1, scalar2=s2, op0=op0, op1=op1), reads, writes)

    def tsm(self, eng, out, in0, s1, reads, writes):
        return self.P.op(eng, lambda e: e.tensor_scalar_mul(out, in0, s1), reads, writes)

    def stt(self, eng, out, in0, scalar, in1, op0, op1, reads, writes):
        return self.P.op(eng, lambda e: e.scalar_tensor_tensor(out=out, in0=in0, scalar=scalar, in1=in1, op0=op0, op1=op1), reads, writes)

    def copy(self, eng, out, in_, reads, writes):
        if eng == "act":
            return self.P.op("act", lambda e: e.copy(out=out, in_=in_), reads, writes)
        return self.P.op(eng, lambda e: e.tensor_copy(out=out, in_=in_), reads, writes)

    def memset(self, eng, ap, val, writes):
        return self.P.op(eng, lambda e: e.memset(ap, val), (), writes)

    def recip(self, out, in_, reads, writes):
        return self.P.op("dve", lambda e: e.reciprocal(out=out, in_=in_), reads, writes)

    def ld(self, q, out, in_, reads, writes, owner=None):
        return self.P.dma(q, out, in_, reads, writes, owner)

    def inp(self, name, shape, dt=F32):
        return self.nc.dram_tensor(name, list(shape), dt, kind="ExternalInput").ap()

    def dram(self, name, shape, dt=F32):
        return self.nc.dram_tensor(name, list(shape), dt, kind="Internal").ap()

    def sb(self, es, name, shape, dt, nres=1):
        self.nsb = getattr(self, "nsb", 0) + 1
        name = f"s{self.nsb}_{name}"
        t = es.enter_context(self.nc.sbuf_tensor(name, list(shape), dt))
        rs = [self.P.res(f"{name}{i}") for i in range(nres)]
        return t, rs

    def dr(self, key, lo, hi, g=512):
        out = []
        for b in range(lo // g, (hi - 1) // g + 1):
            k = (key, b)
            if k not in self.dres:
                self.dres[k] = self.P.res(f"{key}_{b}")
            out.append(self.dres[k])
        return out

    def dump(self, name, ap, shape, dt, reads):
        if not self.dbg:
            return
        self.ndump = getattr(self, "ndump", 0) + 1
        t = self.nc.dram_tensor(f"dbg_{name}", list(shape), dt, kind="ExternalOutput").ap()
        self.P.dma("sp", t, ap, reads, [], owner=reads[0])

    def phase_barrier(self):
        bs = self.bar_s
        self.P.barrier({
            "act": lambda e: e.copy(out=bs[:, 0:1], in_=bs[:, 4:5]),
            "dve": lambda e: e.memset(bs[:, 1:2], 0.0),
            "pool": lambda e: e.memset(bs[:, 2:3], 0.0),
            "pe": None, "sp": None,
        })

    def build(self):
        nc, P, S = self.nc, self.P, self.S
        es0 = P.es
        I = self.inp
        self.xT = I("xT", [128, 8, S])
        self.ctxT = I("ctxT", [128, 8, NCTX])
        self.cc = I("cc", [128, 8, 2])
        self.ada_w = I("ada_w", [4, 1024, 6144])
        self.ada_b = I("ada_b", [128, 4, 48])
        self.norm_w = I("norm_w", [128, 4, 2, 8])
        self.pool_w = I("pool_w", [2, 4, 256, 256])
        self.pool_bs = I("pool_bs", [128, 2, 2, 8])
        self.pool_inv = I("pool_inv", [4, S])
        self.pool_invc = I("pool_invc", [4, NCTX])
        self.wup = I("wup", [4, 22, 128, 8, 256])
        self.convw = I("convw", [128, 4, 3, 44])
        self.convb = I("convb", [128, 4, 44])
        self.wdn = I("wdn", [4, D_FF, 1024])
        self.ident_in = I("ident", [128, 128])
        self.wqkv = I("wqkv", [1024, 1536])
        self.wqksw = I("wqksw", [1024, 1280])
        self.wo = I("wo", [1024, 1024])
        self.qkgain = I("qkgain", [128, 4])
        self.ropeC = I("ropeC", [128, S])
        self.ropeS = I("ropeS", [128, S])
        self.QT_d = self.dram("QT_d", [128, 8, S], BF16)
        NCH = S // 128
        self.ret_win = I("ret_win", [1024, 6144])
        self.ret_dl = I("ret_dl", [1, 8])
        self.ret_gnw = I("ret_gnw", [1, 2048])
        self.ret_wout = I("ret_wout", [2048, 1024])
        self.rpos = I("rpos", [128, 4])
        self.rneg = I("rneg", [2, 128])
        self.rmask = I("rmask", [128, 2, 128])
        self.RQ_d = self.dram("RQ_d", [128, 8, S], BF16)
        self.RKT_d = self.dram("RKT_d", [128, 2, 8, S], BF16)
        self.RKtm_d = self.dram("RKtm_d", [128, 2, NCH, 1024], BF16)
        self.RV_d = self.dram("RV_d", [128, NCH, 2048], BF16)
        self.RG_d = self.dram("RG_d", [128, NCH, 2048], BF16)
        self.ROF_d = self.dram("ROF_d", [128, NCH, 2048], F32)
        self.RKtmc_d = self.dram("RKtmc_d", [128, 2, 2, 1024], BF16)
        self.RVc_d = self.dram("RVc_d", [128, 2, 2048], BF16)
        self.outT = nc.dram_tensor("outT", [128, 8, S], F32, kind="ExternalOutput").ap()
        self.xA = self.dram("xA", [128, 8, S])
        self.xB = self.dram("xB", [128, 8, S])
        self.cA = self.dram("cA", [128, 8, NCTX])
        self.cB = self.dram("cB", [128, 8, NCTX])

        self.ps = es0.enter_context(nc.psum_tensor("ps", [128, 8, 512], F32))
        self.psr = [P.res(f"psb{i}") for i in range(8)]
        self.bar_s, _ = self.sb(es0, "bar_s", [128, 8], F32)
        self.ones_bf, self.r_ones = self.sb(es0, "ones_bf", [128, 128], BF16)
        self.eps_t, self.r_eps = self.sb(es0, "eps_t", [128, 1], F32)
        self.ident, self.r_ident = self.sb(es0, "ident", [128, 128], F32)
        self.modT, self.r_mod = self.sb(es0, "modT", [128, 4, 2, 48], F32)
        self.Gt, self.r_G = self.sb(es0, "Gt", [128, 4, 2, 2, 8], F32)
        self.nw_s, self.r_nw = self.sb(es0, "nw_s", [128, 4, 2, 8], F32)
        self.memset("dve", self.ones_bf[:, :], 1.0, self.r_ones)
        self.memset("dve", self.eps_t[:, :], 1e-6, self.r_eps)
        self.one_t, self.r_one = self.sb(es0, "one_t", [128, 1], F32)
        self.memset("dve", self.one_t[:, :], 1.0, self.r_one)
        self.memset("dve", self.bar_s[:, :], 0.0, [])
        self.ld("sp", self.ident[:, :], self.ident_in, [], self.r_ident)
        self.ld("sp", self.nw_s[:, :, :, :], self.norm_w, [], self.r_nw)

        self.modulation()
        self.phase_barrier()
        xin, cin = self.xT, self.ctxT
        kin, kcin = "xT", "ctxT"
        for i in range(self.NL):
            kind = i % 3
            need_ctx_out = any(k % 3 != 0 for k in range(i + 1, 4))
            lastl = (i == self.NL - 1)
            xout = self.outT if lastl else self.xB
            kout = "outT" if lastl else "xB"
            xmid, kmid_ = self.xA, "xA"
            if lastl and self.dbg == "mix":
                xmid, kmid_ = self.outT, "outT"
            if kind == 0:
                self.pool_mixer(i, i // 3, 0, xin, kin, xmid, kmid_, S, self.pool_inv)
                if need_ctx_out:
                    self.phase_barrier()
                    self.pool_mixer(i, i // 3, 1, cin, kcin, self.cA, "cA", NCTX, self.pool_invc)
            elif kind == 1:
                self.attention(i, xin, kin, cin, kcin, xmid, kmid_, self.cA, "cA", need_ctx_out)
            else:
                self.retention(i, xin, kin, cin, kcin, xmid, kmid_)
            self.phase_barrier()
            if lastl and self.dbg == "mix":
                break
            self.ffn(i, 0, self.xA, "xA", xout, kout, S)
            if need_ctx_out:
                self.phase_barrier()
                self.ffn(i, 1, self.cA, "cA", self.cB, "cB", NCTX)
            self.phase_barrier()
            xin, kin = self.xB, "xB"
            cin, kcin = self.cB, "cB"
        P.emit()
        P.es.close()
        return nc

    def modulation(self):
        nc, P = self.nc, self.P
        with contextlib.ExitStack() as es:
            ccs, r_cc = self.sb(es, "ccs", [128, 8, 2], F32)
            scc, r_scc = self.sb(es, "scc", [128, 8, 2], F32)
            adab, r_adab = self.sb(es, "adab", [128, 4, 48], F32)
            wbuf = [self.sb(es, f"adaw{k}", [128, 6144], F32) for k in range(2)]
            self.ld("sp", ccs[:, :, :], self.cc, [], r_cc)
            self.ld("sp", adab[:, :, :], self.ada_b, [], r_adab)
            self.act(scc[:, :, :], ccs[:, :, :], AF.Silu, r_cc, r_scc)
            n = 0
            macc, r_macc = self.sb(es, "macc", [128, 96], F32)
            for i in range(4):
                for k in range(8):
                    wt, rw = wbuf[n % 2]
                    pb = n % 2
                    n += 1
                    self.ld("sp", wt[:, :], self.ada_w[i, k * 128:(k + 1) * 128, :], [], rw)
                    for m in range(48):
                        self.mm(self.ps[:, pb, 2 * m:2 * m + 2], wt[:, m * 128:(m + 1) * 128], scc[:, k, :],
                                True, True, rw + r_scc, [self.psr[pb]])
                    if k == 0:
                        self.copy("dve", macc[:, :], self.ps[:, pb, 0:96], [self.psr[pb]], r_macc)
                    else:
                        self.tt("dve", macc[:, :], macc[:, :], self.ps[:, pb, 0:96], ALU.add, [self.psr[pb]] + r_macc, r_macc)
                rps = r_macc
                for col in range(2):
                    pv = macc[:, :].rearrange("p (m c) -> p m c", c=2)[:, :, col]
                    self.tt("dve", self.modT[:, i, col, :], pv, adab[:, i, :], ALU.add, rps + r_adab, self.r_mod)
                    for wn in range(2):
                        sc = self.modT[:, i, col, 8 + 24 * wn:16 + 24 * wn]
                        self.stt("dve", self.Gt[:, i, col, wn, :], sc, 1.0, self.nw_s[:, i, wn, :], ALU.add, ALU.mult,
                                 self.r_mod + self.r_nw, self.r_G)

    def mod(self, i, col, which):
        return self.modT[:, i, col, which * 8:(which + 1) * 8]

    def norm_tile(self, xt, rx, n, G, SH, out, rout, sq, rsq, rstd, rrstd, tmp, rtmp, psb):
        for (a, b) in colsplit(n):
            w = b - a
            self.act(sq[:, :, 0:w], xt[:, :, a:b], AF.Square, rx, rsq)
            for c in range(8):
                self.mm(self.ps[:, psb, 0:w], self.ones_bf[:, :], sq[:, c, 0:w], c == 0, c == 7,
                        rsq + self.r_ones, [self.psr[psb]])
            self.act(rstd[:, 0:w], self.ps[:, psb, 0:w], AF.Sqrt, [self.psr[psb]] + self.r_eps, rrstd,
                     bias=self.eps_t[:, 0:1], scale=1.0 / 1024.0)
            self.recip(rstd[:, 0:w], rstd[:, 0:w], rrstd, rrstd)
            for c in range(8):
                t, rt = tmp[c % 2], rtmp[c % 2]
                self.stt("dve", t[:, 0:w], xt[:, c, a:b], G[:, c:c + 1], rstd[:, 0:w], ALU.mult, ALU.mult,
                         rx + rrstd + self.r_G, rt)
                self.act(out[:, c, a:b], t[:, 0:w], AF.Identity, rt + self.r_mod, rout, bias=SH[:, c:c + 1], scale=1.0)

    def pool_mixer(self, i, jp, col, X_in, kin, X_mid, kmid, Stok, invtab):
        NT = min(512, Stok)
        W = NT + 16
        ntile = Stok // NT
        with contextlib.ExitStack() as es:
            xts = [self.sb(es, f"pm_x{k}", [128, 8, W], F32) for k in range(2)]
            h, rh = self.sb(es, "pm_h", [128, 8, W], F32)
            sq, rsq = self.sb(es, "pm_sq", [128, 8, 512], BF16)
            rstd, rrstd = self.sb(es, "pm_rstd", [128, 512], F32)
            tmps = [self.sb(es, f"pm_t{k}", [128, 512], F32) for k in range(2)]
            wa = [self.sb(es, f"pm_wa{k}", [128, W], F32) for k in range(2)]
            wb = [self.sb(es, f"pm_wb{k}", [128, W], F32) for k in range(2)]
            pooled, rpooled = self.sb(es, "pm_pooled", [128, 8, NT], BF16, 8)
            inv, rinv = self.sb(es, "pm_inv", [128, 4, NT], F32)
            pw, rpw = self.sb(es, "pm_w", [128, 4, 2, 256], BF16)
            pbs, rpbs = self.sb(es, "pm_bs", [128, 2, 8], F32)
            AB, rAB = self.sb(es, "pm_AB", [128, 2, 8], F32)
            ot = [self.sb(es, f"pm_o{k}", [128, 8, NT], F32) for k in range(2)]
            yt = [self.sb(es, f"pm_y{k}", [128, NT], F32) for k in range(2)]
            self.ld("pool", pw[:, :, :, :], self.pool_w[jp].rearrange("g (k p) n -> p g k n", p=128), [], rpw)
            self.ld("sp", pbs[:, :, :], self.pool_bs[:, jp, :, :], [], rpbs)
            g1 = self.mod(i, col, 2)
            self.tt("dve", AB[:, 0, :], g1, pbs[:, 1, :], ALU.mult, self.r_mod + rpbs, rAB)
            self.tt("dve", AB[:, 1, :], AB[:, 0, :], pbs[:, 0, :], ALU.mult, rAB + rpbs, rAB)
            G = self.Gt[:, i, col, 0, :]
            SH = self.mod(i, col, 0)
            def pm_loads(tau):
                t0 = tau * NT
                lo, hi = max(t0 - 8, 0), min(t0 + NT + 8, Stok)
                xt, rx = xts[tau % 2]
                if lo > t0 - 8:
                    self.memset("pool", xt[:, :, 0:8], 0.0, rx)
                if hi < t0 + NT + 8:
                    self.memset("pool", xt[:, :, NT + 8:W], 0.0, rx)
                self.ld("sp", xt[:, :, lo - (t0 - 8):hi - (t0 - 8)], X_in[:, :, lo:hi], self.dr(kin, lo, hi), rx)

            pm_loads(0)
            for tau in range(ntile):
                t0 = tau * NT
                lo, hi = max(t0 - 8, 0), min(t0 + NT + 8, Stok)
                xt, rx = xts[tau % 2]
                if tau + 1 < ntile:
                    pm_loads(tau + 1)
                self.ld("sp", inv[:, :, :], invtab[:, t0:t0 + NT].partition_broadcast(128), [], rinv)
                self.norm_tile(xt, rx, W, G, SH, h, rh, sq, rsq, rstd, rrstd,
                               [t[0] for t in tmps], [t[1] for t in tmps], 7)
                if lo > t0 - 8:
                    self.memset("dve", h[:, :, 0:8], 0.0, rh)
                if hi < t0 + NT + 8:
                    self.memset("dve", h[:, :, NT + 8:W], 0.0, rh)
                for c in range(8):
                    g = c // 2
                    A_, rA = wa[c % 2]
                    B_, rB = wb[c % 2]
                    hc = h[:, c, :]
                    e = "dve"
                    if g == 0:
                        self.tt(e, A_[:, 8:8 + NT], hc[:, 7:7 + NT], hc[:, 8:8 + NT], ALU.add, rh, rA)
                        Sv, rS = A_, rA
                    else:
                        self.tt(e, A_[:, 0:W - 1], hc[:, 0:W - 1], hc[:, 1:W], ALU.add, rh, rA)
                        if g == 1:
                            self.tt(e, B_[:, 8:8 + NT], A_[:, 6:6 + NT], A_[:, 8:8 + NT], ALU.add, rA, rB)
                            Sv, rS = B_, rB
                        else:
                            self.tt(e, B_[:, 0:W - 3], A_[:, 0:W - 3], A_[:, 2:W - 1], ALU.add, rA, rB)
                            if g == 2:
                                self.tt(e, A_[:, 8:8 + NT], B_[:, 4:4 + NT], B_[:, 8:8 + NT], ALU.add, rB, rA)
                                Sv, rS = A_, rA
                            else:
                                self.tt(e, A_[:, 0:W - 7], B_[:, 0:W - 7], B_[:, 4:W - 3], ALU.add, rB, rA)
                                self.tt(e, B_[:, 8:8 + NT], A_[:, 0:NT], A_[:, 8:8 + NT], ALU.add, rA, rB)
                                Sv, rS = B_, rB
                    self.tt("pool", Sv[:, 8:8 + NT], Sv[:, 8:8 + NT], inv[:, g, :], ALU.mult, rS + rinv, rS)
                    self.tt("pool", pooled[:, c, :], Sv[:, 8:8 + NT], hc[:, 8:8 + NT], ALU.subtract, rS + rh,
                            [rpooled[c]])
                o_, ro = ot[tau % 2]
                for c in range(8):
                    g, mo = c // 2, c % 2
                    pb = c % 2
                    for ki in range(2):
                        self.mm(self.ps[:, pb, 0:NT], pw[:, g, ki, mo * 128:(mo + 1) * 128], pooled[:, 2 * g + ki, :],
                                ki == 0, ki == 1, rpw + [rpooled[2 * g + ki]], [self.psr[pb]])
                    y_, ry = yt[c % 2]
                    self.act(y_[:, :], self.ps[:, pb, 0:NT], AF.Identity, [self.psr[pb]] + rAB, ry,
                             bias=AB[:, 1, c:c + 1], scale=AB[:, 0, c:c + 1])
                    self.tt("dve", o_[:, c, :], y_[:, :], xt[:, c, 8:8 + NT], ALU.add, ry + rx, ro)
                self.ld("sp", X_mid[:, :, t0:t0 + NT], o_[:, :, :], ro, self.dr(kmid, t0, t0 + NT), owner=ro[0])


    def attention(self, i, X_in, kin, C_in, kcin, X_mid, kmid, C_mid, kcmid, need_ctx_out):
        S = self.S
        NKL = S // 128
        NK = NKL + NCTX // 128
        scale = 128.0 ** -0.5
        with contextlib.ExitStack() as es0:
            KT, rKT = self.sb(es0, "a_KT", [128, 2, S + NCTX], BF16, 1)
            V, rV = self.sb(es0, "a_V", [128, NK, 256], BF16, 1)
            QTc, rQTc = self.sb(es0, "a_QTc", [128, 8, NCTX], BF16)
            gn, rgn = self.sb(es0, "a_gn", [128, 4], F32)
            self.ld("sp", gn[:, :], self.qkgain, [], rgn)
            with contextlib.ExitStack() as es:
                wq, rwq = self.sb(es, "a_wq", [128, 8, 1536], BF16)
                ws, rws = self.sb(es, "a_ws", [128, 8, 1280], BF16)
                for k in range(8):
                    self.ld("pool", wq[:, k, :], self.wqkv[k * 128:(k + 1) * 128, :], [], rwq)
                    self.ld("pool", ws[:, k, :], self.wqksw[k * 128:(k + 1) * 128, :], [], rws)
                xt, rx = self.sb(es, "a_x", [128, 8, 512], F32)
                h, rh = self.sb(es, "a_h", [128, 8, 512], BF16)
                sq, rsq = self.sb(es, "a_sq", [128, 8, 512], BF16)
                rstd, rrstd = self.sb(es, "a_rstd", [128, 512], F32)
                tmps = [self.sb(es, f"a_t{k}", [128, 512], F32) for k in range(2)]
                Ct, rCt = self.sb(es, "a_C", [128, 512], F32)
                St, rSt = self.sb(es, "a_S", [128, 512], F32)
                sqq = [self.sb(es, f"a_sqq{k}", [128, 512], BF16) for k in range(2)]
                rq = [self.sb(es, f"a_rq{k}", [128, 512], F32) for k in range(2)]
                t1 = [self.sb(es, f"a_t1{k}", [128, 512], F32) for k in range(2)]
                t2 = [self.sb(es, f"a_t2{k}", [128, 512], F32) for k in range(2)]
                qout = [self.sb(es, f"a_qo{k}", [128, 8, 512], BF16) for k in range(2)]

                def proj(col, Xsrc, ksrc, t0, n, koff, vch0, rope, qdst, rqdst):
                    self.ld("sp", xt[:, :, 0:n], Xsrc[:, :, t0:t0 + n], self.dr(ksrc, t0, t0 + n), rx)
                    self.norm_tile(xt, rx, n, self.Gt[:, i, col, 0, :], self.mod(i, col, 0), h, rh, sq, rsq,
                                   rstd, rrstd, [t[0] for t in tmps], [t[1] for t in tmps], 7)
                    if rope:
                        self.ld("sp", Ct[:, 0:n], self.ropeC[:, t0:t0 + n], [], rCt)
                        self.ld("sp", St[:, 0:n], self.ropeS[:, t0:t0 + n], [], rSt)
                    for hc in range(10):
                        isq = hc < 8
                        pa, pbk = hc % 2, 2 + hc % 2
                        c0 = hc * 128
                        for k in range(8):
                            self.mm(self.ps[:, pa, 0:n], wq[:, k, c0:c0 + 128], h[:, k, 0:n], k == 0, k == 7,
                                    rwq + rh, [self.psr[pa]])
                        if rope:
                            for k in range(8):
                                self.mm(self.ps[:, pbk, 0:n], ws[:, k, c0:c0 + 128], h[:, k, 0:n], k == 0, k == 7,
                                        rws + rh, [self.psr[pbk]])
                        sq_, rsq_ = sqq[hc % 2]
                        rq_, rrq_ = rq[hc % 2]
                        self.act(sq_[:, 0:n], self.ps[:, pa, 0:n], AF.Square, [self.psr[pa]], rsq_)
                        self.mm(self.ps[:, 4 + hc % 2, 0:n], self.ones_bf[:, :], sq_[:, 0:n], True, True,
                                rsq_ + self.r_ones, [self.psr[4 + hc % 2]])
                        self.act(rq_[:, 0:n], self.ps[:, 4 + hc % 2, 0:n], AF.Sqrt, [self.psr[4 + hc % 2]] + self.r_eps,
                                 rrq_, bias=self.eps_t[:, 0:1], scale=1.0 / 128.0)
                        self.recip(rq_[:, 0:n], rq_[:, 0:n], rrq_, rrq_)
                        g0 = 0 if isq else 2
                        if isq:
                            dst, rdst = qdst[:, hc, 0:n], rqdst
                        else:
                            dst, rdst = KT[:, hc - 8, koff:koff + n], rKT
                        if rope:
                            a_, ra_ = t1[hc % 2]
                            b_, rb_ = t2[hc % 2]
                            self.stt("dve", a_[:, 0:n], self.ps[:, pa, 0:n], gn[:, g0:g0 + 1], Ct[:, 0:n],
                                     ALU.mult, ALU.mult, [self.psr[pa]] + rgn + rCt, ra_)
                            self.stt("dve", b_[:, 0:n], self.ps[:, pbk, 0:n], gn[:, g0 + 1:g0 + 2], St[:, 0:n],
                                     ALU.mult, ALU.mult, [self.psr[pbk]] + rgn + rSt, rb_)
                            self.tt("pool", a_[:, 0:n], a_[:, 0:n], b_[:, 0:n], ALU.add, ra_ + rb_, ra_)
                            self.tt("pool", dst, a_[:, 0:n], rq_[:, 0:n], ALU.mult, ra_ + rrq_, rdst)
                        else:
                            self.stt("dve", dst, self.ps[:, pa, 0:n], gn[:, g0:g0 + 1], rq_[:, 0:n],
                                     ALU.mult, ALU.mult, [self.psr[pa]] + rgn + rrq_, rdst)
                    for s_ in range(n // 128):
                        pv = 6
                        for k in range(8):
                            self.mm(self.ps[:, pv, 0:256], h[:, k, s_ * 128:(s_ + 1) * 128], wq[:, k, 1280:1536],
                                    k == 0, k == 7, rh + rwq, [self.psr[pv]])
                        self.copy("act", V[:, vch0 + s_, :], self.ps[:, pv, 0:256], [self.psr[pv]], rV)

                proj(1, C_in, kcin, 0, NCTX, S, NKL, False, QTc, rQTc)
                for tau in range(S // 512):
                    qo_, rqo_ = qout[tau % 2]
                    proj(0, X_in, kin, tau * 512, 512, tau * 512, tau * 4, True, qo_, rqo_)
                    self.ld("sp", self.QT_d[:, :, tau * 512:(tau + 1) * 512], qo_[:, :, :], rqo_,
                            self.dr("QT_d", tau * 512, (tau + 1) * 512), owner=rqo_[0])
            self.phase_barrier()
            with contextlib.ExitStack() as es:
                wo, rwo = self.sb(es, "a_wo", [128, 8, 1024], BF16)
                for k in range(8):
                    self.ld("pool", wo[:, k, :], self.wo[k * 128:(k + 1) * 128, :], [], rwo)
                QTt = [self.sb(es, f"a_QT{k}", [128, 8, 512], BF16) for k in range(2)]
                pbuf = [self.sb(es, f"a_p{k}", [128, 512], BF16) for k in range(6)]
                acc = [self.sb(es, f"a_acc{k}", [128, 512], F32) for k in range(3)]
                accb, raccb = self.sb(es, "a_accb", [128, 512], BF16)
                rden, rrden = self.sb(es, "a_rden", [128, 512], F32)
                ao, rao = self.sb(es, "a_ao", [128, 8, 512], BF16, 8)
                xts = [self.sb(es, f"a_xx{k}", [128, 8, 512], F32) for k in range(2)]
                ots = [self.sb(es, f"a_oo{k}", [128, 8, 512], F32) for k in range(2)]
                g1s = [self.mod(i, 0, 2), self.mod(i, 1, 2)]

                def core(col, Q, rQ, n, kcs, Xsrc, ksrc, Xdst, kdst, t0, par):
                    xt_, rx_ = xts[par]
                    o_, ro_ = ots[par]
                    self.ld("sp", xt_[:, :, 0:n], Xsrc[:, :, t0:t0 + n], self.dr(ksrc, t0, t0 + n), rx_)
                    for hh in range(8):
                        kv = hh // 4
                        po = 4 + hh % 2
                        nk = len(kcs)
                        LA = 3

                        def qk(idx):
                            kc = kcs[idx]
                            self.mm(self.ps[:, idx % 4, 0:n], KT[:, kv, kc * 128:(kc + 1) * 128], Q[:, hh, 0:n], True, True,
                                    rKT + rQ, [self.psr[idx % 4]])
                        for idx in range(min(LA, nk)):
                            qk(idx)
                        for idx, kc in enumerate(kcs):
                            pss = idx % 4
                            p_, rp_ = pbuf[idx % 6]
                            self.act(p_[:, 0:n], self.ps[:, pss, 0:n], AF.Exp, [self.psr[pss]], rp_, scale=scale)
                            a_, ra_ = acc[idx % 3]
                            ae = "pool" if idx % 3 == 2 else "dve"
                            if idx < 3:
                                self.copy(ae, a_[:, 0:n], p_[:, 0:n], rp_, ra_)
                            else:
                                self.tt(ae, a_[:, 0:n], a_[:, 0:n], p_[:, 0:n], ALU.add, ra_ + rp_, ra_)
                            if idx + LA < nk:
                                qk(idx + LA)
                            self.mm(self.ps[:, po, 0:n], V[:, kc, kv * 128:(kv + 1) * 128], p_[:, 0:n], idx == 0,
                                    idx == nk - 1, rV + rp_, [self.psr[po]])
                        if nk >= 3:
                            self.tt("dve", acc[0][0][:, 0:n], acc[0][0][:, 0:n], acc[2][0][:, 0:n], ALU.add,
                                    acc[0][1] + acc[2][1], acc[0][1])
                        self.tt("dve", accb[:, 0:n], acc[0][0][:, 0:n], acc[1][0][:, 0:n], ALU.add,
                                acc[0][1] + acc[1][1], raccb)
                        pd = 6
                        self.mm(self.ps[:, pd, 0:n], self.ones_bf[:, :], accb[:, 0:n], True, True, raccb + self.r_ones,
                                [self.psr[pd]])
                        self.recip(rden[:, 0:n], self.ps[:, pd, 0:n], [self.psr[pd]], rrden)
                        self.tt("dve", ao[:, hh, 0:n], self.ps[:, po, 0:n], rden[:, 0:n], ALU.mult,
                                [self.psr[po]] + rrden, [rao[hh]])
                    for m in range(8):
                        py = 6 + m % 2
                        for k in range(8):
                            self.mm(self.ps[:, py, 0:n], wo[:, k, m * 128:(m + 1) * 128], ao[:, k, 0:n], k == 0, k == 7,
                                    rwo + [rao[k]], [self.psr[py]])
                        self.stt("dve", o_[:, m, 0:n], self.ps[:, py, 0:n], g1s[col][:, m:m + 1], xt_[:, m, 0:n],
                                 ALU.mult, ALU.add, [self.psr[py]] + self.r_mod + rx_, ro_)
                    self.ld("sp", Xdst[:, :, t0:t0 + n], o_[:, :, 0:n], ro_, self.dr(kdst, t0, t0 + n), owner=ro_[0])

                self.dump("KT", KT[:, :, :], [128, 2, S + NCTX], BF16, rKT)
                self.dump("V", V[:, :, :], [128, NK, 256], BF16, rV)
                self.dump("QTc", QTc[:, :, :], [128, 8, NCTX], BF16, rQTc)
                if need_ctx_out:
                    core(1, QTc, rQTc, NCTX, [NKL, NKL + 1], C_in, kcin, C_mid, kcmid, 0, 0)
                self.dump("ao_c", ao[:, :, :], [128, 8, 512], BF16, rao)
                self.dump("rden_c", rden[:, :], [128, 512], F32, rrden)
                self.dump("p_c", pbuf[1][0][:, :], [128, 512], BF16, pbuf[1][1])
                allk = list(range(NK))
                for tau in range(S // 512):
                    Q_, rQ_ = QTt[tau % 2]
                    self.ld("sp", Q_[:, :, :], self.QT_d[:, :, tau * 512:(tau + 1) * 512],
                            self.dr("QT_d", tau * 512, (tau + 1) * 512), rQ_)
                    core(0, Q_, rQ_, 512, allk, X_in, kin, X_mid, kmid, tau * 512, tau % 2)


    def retention(self, i, X_in, kin, C_in, kcin, X_mid, kmid):
        S = self.S
        NCH = S // 128
        with contextlib.ExitStack() as es0:
            lg, rlg = self.sb(es0, "r_lg", [128, 8], F32)
            g128, rg128 = self.sb(es0, "r_g128", [128, 8], F32)
            qo, rqo = self.sb(es0, "r_qo", [128, 8], F32)
            kts, rkts = self.sb(es0, "r_kts", [128, 8], F32)
            kTtab, rkTtab = self.sb(es0, "r_kTtab", [128, 8, 128], F32)
            mask, rmask = self.sb(es0, "r_mask", [128, 2, 128], F32)
            rposs, rrpos = self.sb(es0, "r_pos", [128, 4], F32)
            rnegb, rrneg = self.sb(es0, "r_negb", [128, 2, 128], F32)
            dlt, rdl = self.sb(es0, "r_dl", [128, 8], F32)
            self.ld("sp", dlt[:, :], self.ret_dl.partition_broadcast(128), [], rdl)
            self.ld("sp", rposs[:, :], self.rpos, [], rrpos)
            self.ld("sp", rnegb[:, :, :], self.rneg.partition_broadcast(128), [], rrneg)
            self.ld("sp", mask[:, :, :], self.rmask, [], rmask)
            self.act(lg[:, :], dlt[:, :], AF.Exp, rdl, rlg, scale=-1.0)
            self.act(lg[:, :], lg[:, :], AF.Ln, rlg + self.r_one, rlg, bias=self.one_t[:, 0:1], scale=1.0)
            self.tsm("dve", lg[:, :], lg[:, :], -1.0, rlg, rlg)
            self.act(g128[:, :], lg[:, :], AF.Exp, rlg, rg128, scale=128.0)
            for d in range(2):
                self.act(qo[:, 4 * d:4 * d + 4], lg[:, 4 * d:4 * d + 4], AF.Exp, rlg + rrpos, rqo, scale=rposs[:, d:d + 1])
                self.act(kts[:, 4 * d:4 * d + 4], lg[:, 4 * d:4 * d + 4], AF.Exp, rlg + rrpos, rkts,
                         scale=rposs[:, 2 + d:3 + d])
                for hh in range(4):
                    dh = 4 * d + hh
                    self.act(kTtab[:, dh, :], rnegb[:, d, :], AF.Exp, rrneg + rlg, rkTtab, scale=lg[:, dh:dh + 1])
            self.tsm("dve", kts[:, :], kts[:, :], 1.0 / 16.0, rkts, rkts)
            self.tsm("dve", kTtab[:, :, :], kTtab[:, :, :], 1.0 / 16.0, rkTtab, rkTtab)

            with contextlib.ExitStack() as es:
                win, rwin = self.sb(es, "r_win", [128, 8, 6144], BF16)
                for k in range(8):
                    for q4 in range(4):
                        self.ld("pool", win[:, k, q4 * 1536:(q4 + 1) * 1536],
                                self.ret_win[k * 128:(k + 1) * 128, q4 * 1536:(q4 + 1) * 1536], [], rwin)
                xt, rx = self.sb(es, "r_x", [128, 8, 512], F32)
                h, rh = self.sb(es, "r_h", [128, 8, 512], BF16)
                sq, rsq = self.sb(es, "r_sq", [128, 8, 512], BF16)
                rstd, rrstd = self.sb(es, "r_rstd", [128, 512], F32)
                tmps = [self.sb(es, f"r_t{k}", [128, 512], F32) for k in range(2)]
                QTt, rQTt = self.sb(es, "r_QTt", [128, 8, 512], BF16)
                KTt, rKTt = self.sb(es, "r_KTt", [128, 2, 8, 512], BF16)
                Ktm = [self.sb(es, f"r_Ktm{k}", [128, 2, 1024], BF16) for k in range(2)]
                Vt = [self.sb(es, f"r_Vt{k}", [128, 2048], BF16) for k in range(2)]
                Gt_ = [self.sb(es, f"r_Gt{k}", [128, 2048], BF16) for k in range(2)]

                def rproj(col, Xsrc, ksrc, t0, n, is_ctx):
                    self.ld("sp", xt[:, :, 0:n], Xsrc[:, :, t0:t0 + n], self.dr(ksrc, t0, t0 + n), rx)
                    self.norm_tile(xt, rx, n, self.Gt[:, i, col, 0, :], self.mod(i, col, 0), h, rh, sq, rsq,
                                   rstd, rrstd, [t[0] for t in tmps], [t[1] for t in tmps], 7)
                    if not is_ctx:
                        for c in range(8):
                            pb = c % 2
                            for k in range(8):
                                self.mm(self.ps[:, pb, 0:n], win[:, k, c * 128:(c + 1) * 128], h[:, k, 0:n], k == 0, k == 7,
                                        rwin + rh, [self.psr[pb]])
                            self.copy("act", QTt[:, c, 0:n], self.ps[:, pb, 0:n], [self.psr[pb]], rQTt)
                        self.ld("sp", self.RQ_d[:, :, t0:t0 + n], QTt[:, :, 0:n], rQTt, self.dr("RQ_d", t0, t0 + n),
                                owner=rQTt[0])
                        for c in range(8):
                            pb = 2 + c % 2
                            hh = c // 2
                            for k in range(8):
                                self.mm(self.ps[:, pb, 0:n], win[:, k, 1024 + c * 128:1024 + (c + 1) * 128], h[:, k, 0:n],
                                        k == 0, k == 7, rwin + rh, [self.psr[pb]])
                            for d in range(2):
                                for s_ in range(n // 128):
                                    self.tt("dve", KTt[:, d, c, s_ * 128:(s_ + 1) * 128],
                                            self.ps[:, pb, s_ * 128:(s_ + 1) * 128], kTtab[:, 4 * d + hh, :], ALU.mult,
                                            [self.psr[pb]] + rkTtab, rKTt)
                        for d in range(2):
                            self.ld("sp", self.RKT_d[:, d, :, t0:t0 + n], KTt[:, d, :, 0:n], rKTt,
                                    self.dr("RKT_d", t0, t0 + n), owner=rKTt[0])
                    for s_ in range(n // 128):
                        ch = (t0 + s_ * 128) // 128
                        par = s_ % 2
                        km, rkm = Ktm[par]
                        v_, rv_ = Vt[par]
                        g_, rg_ = Gt_[par]
                        lhs = lambda k: h[:, k, s_ * 128:(s_ + 1) * 128]
                        for grp in range(2):
                            pb = 4 + grp
                            for k in range(8):
                                self.mm(self.ps[:, pb, :], lhs(k), win[:, k, 1024 + grp * 512:1024 + (grp + 1) * 512],
                                        k == 0, k == 7, rh + rwin, [self.psr[pb]])
                            for hl in range(2):
                                hh = grp * 2 + hl
                                for d in range(2):
                                    self.act(km[:, d, hh * 256:(hh + 1) * 256], self.ps[:, pb, hl * 256:(hl + 1) * 256],
                                             AF.Identity, [self.psr[pb]] + rkts, rkm, scale=kts[:, 4 * d + hh:4 * d + hh + 1])
                        for grp in range(4):
                            pb = 6 + grp % 2
                            for k in range(8):
                                self.mm(self.ps[:, pb, :], lhs(k), win[:, k, 2048 + grp * 512:2048 + (grp + 1) * 512],
                                        k == 0, k == 7, rh + rwin, [self.psr[pb]])
                            self.copy("dve", v_[:, grp * 512:(grp + 1) * 512], self.ps[:, pb, :], [self.psr[pb]], rv_)
                        if is_ctx:
                            self.ld("sp", self.RKtmc_d[:, :, ch, :], km[:, :, :], rkm, self.dr("RKtmc_d", 0, 1), owner=rkm[0])
                            self.ld("sp", self.RVc_d[:, ch, :], v_[:, :], rv_, self.dr("RVc_d", 0, 1), owner=rv_[0])
                            continue
                        self.ld("sp", self.RKtm_d[:, :, ch, :], km[:, :, :], rkm, self.dr("RKtm_d", t0, t0 + n), owner=rkm[0])
                        self.ld("sp", self.RV_d[:, ch, :], v_[:, :], rv_, self.dr("RV_d", t0, t0 + n), owner=rv_[0])
                        for grp in range(4):
                            pb = 6 + grp % 2
                            for k in range(8):
                                self.mm(self.ps[:, pb, :], lhs(k), win[:, k, 4096 + grp * 512:4096 + (grp + 1) * 512],
                                        k == 0, k == 7, rh + rwin, [self.psr[pb]])
                            self.act(g_[:, grp * 512:(grp + 1) * 512], self.ps[:, pb, :], AF.Silu, [self.psr[pb]], rg_)
                        self.ld("sp", self.RG_d[:, ch, :], g_[:, :], rg_, self.dr("RG_d", t0, t0 + n), owner=rg_[0])

                rproj(1, C_in, kcin, 0, NCTX, True)
                for tau in range(S // 512):
                    rproj(0, X_in, kin, tau * 512, 512, False)
            self.phase_barrier()

            for d in range(2):
                with contextlib.ExitStack() as es:
                    R, rR = self.sb(es, "r_R", [128, 8, 512], F32, 8)
                    Rb, rRb = self.sb(es, "r_Rb", [128, 8, 512], BF16, 8)
                    QTc = [self.sb(es, f"r_QTc{k}", [128, 8, 128], BF16) for k in range(2)]
                    KTc = [self.sb(es, f"r_KTc{k}", [128, 8, 128], BF16) for k in range(2)]
                    Kmc = [self.sb(es, f"r_Kmc{k}", [128, 1024], BF16) for k in range(2)]
                    Vc = [self.sb(es, f"r_Vc{k}", [128, 2048], BF16) for k in range(2)]
                    AT = [self.sb(es, f"r_AT{k}", [128, 128], BF16) for k in range(2)]
                    osb = [self.sb(es, f"r_osb{k}", [128, 2048], F32) for k in range(2)]
                    if d == 1:
                        OFc = [self.sb(es, f"r_OFc{k}", [128, 2048], F32) for k in range(2)]
                        Gc = [self.sb(es, f"r_Gc{k}", [128, 2048], BF16) for k in range(2)]
                        st, rst = self.sb(es, "r_st", [128, 4, 6], F32)
                        mv, rmv = self.sb(es, "r_mv", [128, 4, 2], F32)
                        rs4, rrs4 = self.sb(es, "r_rs4", [128, 4], F32)
                        nmr, rnmr = self.sb(es, "r_nmr", [128, 4], F32)
                        yn = [self.sb(es, f"r_yn{k}", [128, 512], F32) for k in range(2)]
                        z, rz = self.sb(es, "r_z", [128, 2048], F32, 4)
                        zT, rzT = self.sb(es, "r_zT", [128, 16, 512], BF16)
                        wout, rwout = self.sb(es, "r_wout", [128, 16, 1024], BF16)
                        gnwb, rgnwb = self.sb(es, "r_gnwb", [128, 2048], F32)
                        xt2, rx2 = self.sb(es, "r_x2", [128, 8, 512], F32)
                        ot2, ro2 = self.sb(es, "r_o2", [128, 8, 512], F32)
                        for k in range(16):
                            self.ld("pool", wout[:, k, :], self.ret_wout[k * 128:(k + 1) * 128, :], [], rwout)
                        self.ld("sp", gnwb[:, :], self.ret_gnw.partition_broadcast(128), [], rgnwb)
                    g1 = self.mod(i, 0, 2)
                    for a8 in range(8):
                        self.memset("dve", R[:, a8, :], 0.0, [rR[a8]])
                        self.memset("pool", Rb[:, a8, :], 0.0, [rRb[a8]])
                    steps = []
                    cord = [0, 1] if d == 0 else [1, 0]
                    for cc in cord:
                        steps.append((True, cc))
                    lord = list(range(NCH)) if d == 0 else list(range(NCH - 1, -1, -1))
                    for c in lord:
                        steps.append((False, c))
                    def issue_loads(si):
                        is_ctx, c = steps[si]
                        par = si % 2
                        km, rkm = Kmc[par]
                        v_, rv_ = Vc[par]
                        if is_ctx:
                            self.ld("sp", km[:, :], self.RKtmc_d[:, d, c, :], self.dr("RKtmc_d", 0, 1), rkm)
                            self.ld("sp", v_[:, :], self.RVc_d[:, c, :], self.dr("RVc_d", 0, 1), rv_)
                        else:
                            t0 = c * 128
                            self.ld("sp", km[:, :], self.RKtm_d[:, d, c, :], self.dr("RKtm_d", t0, t0 + 128), rkm)
                            self.ld("sp", v_[:, :], self.RV_d[:, c, :], self.dr("RV_d", t0, t0 + 128), rv_)
                            q_, rq_ = QTc[par]
                            kt_, rkt_ = KTc[par]
                            self.ld("sp", q_[:, :, :], self.RQ_d[:, :, t0:t0 + 128], self.dr("RQ_d", t0, t0 + 128), rq_)
                            self.ld("sp", kt_[:, :, :], self.RKT_d[:, d, :, t0:t0 + 128], self.dr("RKT_d", t0, t0 + 128), rkt_)
                            if d == 1:
                                of_, rof_ = OFc[par]
                                gc_, rgc_ = Gc[par]
                                self.ld("sp", of_[:, :], self.ROF_d[:, c, :], self.dr("ROF_d", t0, t0 + 128), rof_)
                                self.ld("sp", gc_[:, :], self.RG_d[:, c, :], self.dr("RG_d", t0, t0 + 128), rgc_)

                    issue_loads(0)
                    for si, (is_ctx, c) in enumerate(steps):
                        par = si % 2
                        do_out = not is_ctx
                        if si + 1 < len(steps):
                            issue_loads(si + 1)
                        km, rkm = Kmc[par]
                        v_, rv_ = Vc[par]
                        if not is_ctx:
                            t0 = c * 128
                            q_, rq_ = QTc[par]
                            kt_, rkt_ = KTc[par]
                            o_, ro_ = osb[par]
                            if d == 1:
                                of_, rof_ = OFc[par]
                                gc_, rgc_ = Gc[par]
                        for hh in range(4):
                            dh = 4 * d + hh
                            hc = slice(hh * 512, (hh + 1) * 512)
                            if do_out:
                                pa = hh % 2
                                for a in range(2):
                                    self.mm(self.ps[:, pa, 0:128], kt_[:, 2 * hh + a, :], q_[:, 2 * hh + a, :], a == 0, a == 1,
                                            rkt_ + rq_, [self.psr[pa]])
                            for a in range(2):
                                a8 = 2 * hh + a
                                pr = 4 + a8 % 2
                                self.mm(self.ps[:, pr, :], km[:, hh * 256 + a * 128:hh * 256 + (a + 1) * 128], v_[:, hc],
                                        True, True, rkm + rv_, [self.psr[pr]])
                            if do_out:
                                at_, rat_ = AT[hh % 2]
                                self.tt("dve", at_[:, :], self.ps[:, pa, 0:128], mask[:, d, :], ALU.mult,
                                        [self.psr[pa]] + rmask, rat_)
                                po = 2 + hh % 2
                                self.mm(self.ps[:, po, :], at_[:, :], v_[:, hc], True, False, rat_ + rv_, [self.psr[po]])
                                for a in range(2):
                                    self.mm(self.ps[:, po, :], q_[:, 2 * hh + a, :], Rb[:, 2 * hh + a, :], False, a == 1,
                                            rq_ + [rRb[2 * hh + a]], [self.psr[po]])
                                if d == 0:
                                    self.act(o_[:, hc], self.ps[:, po, :], AF.Identity, [self.psr[po]] + rqo, ro_,
                                             scale=qo[:, dh:dh + 1])
                                else:
                                    self.stt("dve", o_[:, hc], self.ps[:, po, :], qo[:, dh:dh + 1], of_[:, hc],
                                             ALU.mult, ALU.add, [self.psr[po]] + rqo + rof_, ro_)
                            for a in range(2):
                                a8 = 2 * hh + a
                                pr = 4 + a8 % 2
                                self.stt("dve", R[:, a8, :], R[:, a8, :], g128[:, dh:dh + 1], self.ps[:, pr, :],
                                         ALU.mult, ALU.add, [rR[a8], self.psr[pr]] + rg128, [rR[a8]])
                                self.copy("act", Rb[:, a8, :], R[:, a8, :], [rR[a8]], [rRb[a8]])
                        if not do_out:
                            continue
                        if d == 0:
                            self.ld("sp", self.ROF_d[:, c, :], o_[:, :], ro_, self.dr("ROF_d", t0, t0 + 128), owner=ro_[0])
                            continue
                        for hh in range(4):
                            hc = slice(hh * 512, (hh + 1) * 512)
                            self.P.op("dve", (lambda e, o=st[:, hh, :], i_=o_[:, hc]: e.bn_stats(out=o, in_=i_)), ro_, rst)
                            self.P.op("dve", (lambda e, o=mv[:, hh, :], i_=st[:, hh, :]: e.bn_aggr(out=o, in_=i_)), rst, rmv)
                        self.act(rs4[:, :], mv[:, :, 1], AF.Sqrt, rmv + self.r_eps, rrs4, bias=self.eps_t[:, 0:1], scale=1.0)
                        self.recip(rs4[:, :], rs4[:, :], rrs4, rrs4)
                        self.stt("dve", nmr[:, :], mv[:, :, 0], -1.0, rs4[:, :], ALU.mult, ALU.mult, rmv + rrs4, rnmr)
                        for hh in range(4):
                            hc = slice(hh * 512, (hh + 1) * 512)
                            y_, ry_ = yn[hh % 2]
                            self.act(y_[:, :], o_[:, hc], AF.Identity, ro_ + rrs4 + rnmr, ry_, bias=nmr[:, hh:hh + 1],
                                     scale=rs4[:, hh:hh + 1])
                            self.tt("dve", z[:, hc], y_[:, :], gnwb[:, hc], ALU.mult, ry_ + rgnwb, [rz[hh]])
                            self.tt("pool", z[:, hc], z[:, hc], gc_[:, hc], ALU.mult, [rz[hh]] + rgc_, [rz[hh]])
                        sub = c % 4
                        for q4 in range(4):
                            pt = 6
                            for f4 in range(4):
                                fc = q4 * 4 + f4
                                self.transpose(self.ps[:, pt, f4 * 128:(f4 + 1) * 128], z[:, fc * 128:(fc + 1) * 128],
                                               self.ident[:, :], [rz[q4]] + self.r_ident, [self.psr[pt]])
                            self.copy("act", zT[:, q4 * 4:(q4 + 1) * 4, sub * 128:(sub + 1) * 128],
                                      self.ps[:, pt, :].rearrange("p (f t) -> p f t", f=4), [self.psr[pt]], rzT)
                        if sub == 0:
                            tau = c // 4
                            t0t = tau * 512
                            self.ld("sp", xt2[:, :, :], X_in[:, :, t0t:t0t + 512], self.dr(kin, t0t, t0t + 512), rx2)
                            for m in range(8):
                                py = 7
                                for k in range(16):
                                    self.mm(self.ps[:, py, :], wout[:, k, m * 128:(m + 1) * 128], zT[:, k, :], k == 0, k == 15,
                                            rwout + rzT, [self.psr[py]])
                                self.stt("dve", ot2[:, m, :], self.ps[:, py, :], g1[:, m:m + 1], xt2[:, m, :],
                                         ALU.mult, ALU.add, [self.psr[py]] + self.r_mod + rx2, ro2)
                            self.ld("sp", X_mid[:, :, t0t:t0t + 512], ot2[:, :, :], ro2, self.dr(kmid, t0t, t0t + 512),
                                    owner=ro2[0])
                self.phase_barrier()

    def ffn(self, i, col, X_mid, kmid, X_out, kout, Stok):
        NT = min(1024, Stok)
        ntile = Stok // NT
        with contextlib.ExitStack() as es:
            wdn, rwdn = self.sb(es, "f_wdn", [128, 22, 1024], BF16, 22)
            wups = [self.sb(es, f"f_wup{k}", [128, 8, 256], BF16) for k in range(3)]
            h2, rh2 = self.sb(es, "f_h2", [128, 8, NT], BF16)
            sq, rsq = self.sb(es, "f_sq", [128, 8, 512], BF16)
            rstd, rrstd = self.sb(es, "f_rstd", [128, 512], F32)
            tmps = [self.sb(es, f"f_t{k}", [128, 512], F32) for k in range(2)]
            U = [[self.sb(es, f"f_U{hf}{k}", [128, NT + 3], F32) for k in range(1)] for hf in range(2)]
            C = [[self.sb(es, f"f_C{hf}{k}", [128, NT + 1], F32) for k in range(2)] for hf in range(2)]
            actb, ract = self.sb(es, "f_act", [128, 22, NT + 1], BF16, 22)
            carry, rcarry = self.sb(es, "f_carry", [128, 44, 2], F32, 44)
            xs, rxs = self.sb(es, "f_xs", [128, 8, NT + 1], F32, 1)
            cw, rcw = self.sb(es, "f_cw", [128, 3, 44], F32)
            cb, rcb = self.sb(es, "f_cb", [128, 44], F32)
            self.ld("sp", cw[:, :, :], self.convw[:, i, :, :], [], rcw)
            self.ld("sp", cb[:, :], self.convb[:, i, :], [], rcb)
            wdv = self.wdn[i].rearrange("(k p) m -> p k m", p=128)
            for k in range(22):
                self.ld("pool", wdn[:, k, :], wdv[:, k, :], [], [rwdn[k]])
            G = self.Gt[:, i, col, 1, :]
            SH = self.mod(i, col, 3)
            g2 = self.mod(i, col, 5)
            nw = 0
            npairs = ntile * 22

            def wload(n):
                if n < npairs:
                    w_, rw_ = wups[n % 3]
                    self.ld("pool", w_[:, :, :], self.wup[i, n % 22], [], rw_)
            wload(0)
            wload(1)
            for tau in range(ntile):
                t0 = tau * NT
                last = tau == ntile - 1
                nout = NT + (1 if last else 0)
                xt, rx = xs, rxs
                self.ld("sp", xt[:, :, 0:NT], X_mid[:, :, t0:t0 + NT], self.dr(kmid, t0, t0 + NT), rx)
                self.norm_tile(xt, rx, NT, G, SH, h2, rh2, sq, rsq, rstd, rrstd,
                               [t[0] for t in tmps], [t[1] for t in tmps], 7)
                for j in range(22):
                    w, rw = wups[nw % 3]
                    wload(nw + 2)
                    nw += 1
                    for hf in range(2):
                        ch = j + 22 * hf
                        pbase = 2 * hf + 4 * (j % 2)
                        for (a, b) in colsplit(NT):
                            pb = pbase + a // 512
                            for k in range(8):
                                self.mm(self.ps[:, pb, 0:b - a], w[:, k, hf * 128:(hf + 1) * 128], h2[:, k, a:b],
                                        k == 0, k == 7, rw + rh2, [self.psr[pb]])
                        Ut, rU = U[hf][0]
                        Ct, rC = C[hf][j % 2]
                        for (a, b) in colsplit(NT):
                            pb = pbase + a // 512
                            self.copy("act", Ut[:, 2 + a:2 + b], self.ps[:, pb, 0:b - a], [self.psr[pb]], rU)
                        if tau == 0:
                            self.memset("dve", Ut[:, 0:2], 0.0, rU)
                        else:
                            self.copy("dve", Ut[:, 0:2], carry[:, ch, :], [rcarry[ch]], rU)
                        if last:
                            self.memset("dve", Ut[:, NT + 2:NT + 3], 0.0, rU)
                        else:
                            self.copy("dve", carry[:, ch, :], Ut[:, NT:NT + 2], rU, [rcarry[ch]])
                        self.act(Ct[:, 0:nout], Ut[:, 0:nout], AF.Identity, rU + rcw + rcb, rC,
                                 bias=cb[:, ch:ch + 1], scale=cw[:, 0, ch:ch + 1])
                        self.stt("dve", Ct[:, 0:nout], Ut[:, 1:nout + 1], cw[:, 1, ch:ch + 1], Ct[:, 0:nout],
                                 ALU.mult, ALU.add, rU + rcw + rC, rC)
                        self.stt("dve", Ct[:, 0:nout], Ut[:, 2:nout + 2], cw[:, 2, ch:ch + 1], Ct[:, 0:nout],
                                 ALU.mult, ALU.add, rU + rcw + rC, rC)
                    Ca, rCa = C[0][j % 2]
                    Cv, rCv = C[1][j % 2]
                    self.act(Ca[:, 0:nout], Ca[:, 0:nout], AF.Silu, rCa, rCa)
                    self.tt("pool", actb[:, j, 0:nout], Ca[:, 0:nout], Cv[:, 0:nout], ALU.mult, rCa + rCv, [ract[j]])
                lo = 1 if tau == 0 else 0
                self.ld("sp", xs[:, :, lo:nout], X_mid[:, :, t0 - 1 + lo:t0 - 1 + nout],
                        self.dr(kmid, t0 - 1 + lo, t0 - 1 + nout), rxs)
                for m in range(8):
                    pbase = 4 if m % 2 == 0 else 0
                    segs = colsplit(nout)
                    for si, (a, b) in enumerate(segs):
                        pb = pbase + si
                        for k in range(22):
                            self.mm(self.ps[:, pb, 0:b - a], wdn[:, k, m * 128:(m + 1) * 128], actb[:, k, a:b],
                                    k == 0, k == 21, [rwdn[k], ract[k]], [self.psr[pb]])
                    for si, (a, b) in enumerate(segs):
                        pb = pbase + si
                        self.stt("dve", xs[:, m, a:b], self.ps[:, pb, 0:b - a], g2[:, m:m + 1], xs[:, m, a:b],
                                 ALU.mult, ALU.add, [self.psr[pb]] + self.r_mod + rxs, rxs)
                self.ld("sp", X_out[:, :, t0 - 1 + lo:t0 - 1 + nout], xs[:, :, lo:nout], rxs,
                        self.dr(kout, t0 - 1 + lo, t0 - 1 + nout), owner=rxs[0])


def _fm(a):
    n = a.shape[-1] // 128
    b = a.reshape(a.shape[:-1] + (n, 128))
    return np.ascontiguousarray(np.moveaxis(b, -1, 0))


def pool_inv_table(S):
    t = np.arange(S)
    out = np.zeros((4, S), np.float32)
    for g, win in enumerate((2, 4, 8, 16)):
        lo = np.clip(t - win // 2, 0, S)
        hi = np.clip(t + win // 2, 0, S)
        out[g] = 1.0 / (hi - lo).astype(np.float32)
    return out


def rope_tables(S):
    rows = S // 64
    row = np.repeat(np.arange(rows), 64).astype(np.float32)
    colp = np.tile(np.arange(64), rows).astype(np.float32)
    inv = (np.float32(10000.0) ** (-np.arange(0, 64, 2, dtype=np.float32) / np.float32(64))).astype(np.float32)
    ar = (row[:, None] * inv[None, :]).astype(np.float32)
    ac = (colp[:, None] * inv[None, :]).astype(np.float32)
    C = np.concatenate([np.cos(ar), np.cos(ar), np.cos(ac), np.cos(ac)], axis=1)
    Sg = np.concatenate([-np.sin(ar), np.sin(ar), -np.sin(ac), np.sin(ac)], axis=1)
    return np.ascontiguousarray(C.T.astype(np.float32)), np.ascontiguousarray(Sg.T.astype(np.float32))


def make_in_maps(inputs, S, ncores):
    f = lambda a: np.ascontiguousarray(np.asarray(a, dtype=np.float32))
    x, c, ctx, c_ctx = f(inputs["x"]), f(inputs["c"]), f(inputs["ctx"]), f(inputs["c_ctx"])
    B = x.shape[0]
    ada_b = _fm(f(inputs["ada_b"]))
    norm_w = _fm(f(inputs["norm_w"]))
    pool_bs = np.ascontiguousarray(np.stack([_fm(f(inputs["pool_b"])), _fm(f(inputs["pool_scale"]))], axis=2))
    wu = f(inputs["ffn_w_up"])
    a = wu[:, :, :D_FF].reshape(4, 8, 128, 22, 128)
    v = wu[:, :, D_FF:].reshape(4, 8, 128, 22, 128)
    wup = np.ascontiguousarray(np.concatenate([a, v], axis=-1).transpose(0, 3, 2, 1, 4))
    convw = np.ascontiguousarray(_fm(f(inputs["ffn_conv_w"])))
    convb = np.ascontiguousarray(_fm(f(inputs["ffn_conv_b"])))
    common = {
        "ada_w": f(inputs["ada_w"]), "ada_b": ada_b, "norm_w": norm_w,
        "pool_w": f(inputs["pool_w"]), "pool_bs": pool_bs,
        "pool_inv": pool_inv_table(S), "pool_invc": pool_inv_table(NCTX),
        "wup": wup, "convw": convw, "convb": convb, "wdn": f(inputs["ffn_w_down"]),
        "ident": np.eye(128, dtype=np.float32),
    }
    perm = np.arange(128)
    perm = np.where(perm % 64 < 32, perm + 32, perm - 32)
    wqkv = f(inputs["attn_w_qkv"])[0]
    cols = np.concatenate([hh * 128 + perm for hh in range(10)])
    qg, kg = f(inputs["attn_q_gain"])[0], f(inputs["attn_k_gain"])[0]
    common.update({
        "wqkv": wqkv, "wqksw": np.ascontiguousarray(wqkv[:, cols]), "wo": f(inputs["attn_w_o"])[0],
        "qkgain": np.ascontiguousarray(np.stack([qg, qg[perm], kg, kg[perm]], axis=1)),
    })
    p = np.arange(128, dtype=np.float32)
    jj = np.arange(128)
    rmask = np.stack([(jj[:, None] <= jj[None, :]), (jj[:, None] >= jj[None, :])], axis=1).astype(np.float32)
    common.update({
        "ret_win": f(inputs["ret_w_in"])[0], "ret_dl": f(inputs["ret_decay_logit"])[0].reshape(1, 8),
        "ret_gnw": f(inputs["ret_gn_w"])[0].reshape(1, 2048), "ret_wout": f(inputs["ret_w_out"])[0],
        "rpos": np.ascontiguousarray(np.stack([p + 1, 128 - p, 127 - p, p], axis=1)),
        "rneg": np.ascontiguousarray(np.stack([-(p + 1), p - 128], axis=0)),
        "rmask": np.ascontiguousarray(rmask),
    })
    rC, rS = rope_tables(S)
    common.update({"ropeC": rC, "ropeS": rS})
    maps = []
    for core in range(ncores):
        b = core % B
        m = dict(common)
        m["xT"] = np.ascontiguousarray(x[b].T.reshape(8, 128, S).transpose(1, 0, 2))
        m["ctxT"] = np.ascontiguousarray(ctx[b].T.reshape(8, 128, NCTX).transpose(1, 0, 2))
        cc = np.stack([c[b], c_ctx], axis=-1)
        m["cc"] = np.ascontiguousarray(cc.reshape(8, 128, 2).transpose(1, 0, 2))
        maps.append(m)
    return maps


_CACHE = {}


def run(inputs, S, ncores, NL=4, dbg=None):
    key = (S, NL, dbg)
    if key not in _CACHE:
        _CACHE[key] = KB(S, NL, dbg).build()
    nc = _CACHE[key]
    maps = make_in_maps(inputs, S, ncores)
    res = run_bass_kernel_spmd(nc, maps, core_ids=list(range(ncores)))
    global LAST_RES
    LAST_RES = res.results
    outs = []
    for r in res.results:
        o = r["outT"]
        outs.append(np.ascontiguousarray(o.transpose(2, 1, 0).reshape(S, 1024)))
    return outs


def kernel(**inputs):
    x = np.asarray(inputs["x"])
    B, S, _ = x.shape
    outs = run(inputs, S, 8)
    return np.stack(outs[:B], axis=0).astype(np.float32)
```
